# Optimizing a Trainium2 kernel written in Bass

```python
import jax, jax.numpy as jnp
from jax import lax
import numpy as np

D_MODEL = 1024
BATCH = 16
SEQ = 4096
DEPTH = 2
DEC_BATCH = 32
DEC_SEQ = 16
PAST_LEN = 4096

CHUNK = 64
EPS = 1e-6
N_BRANCH = 3
SG_CHUNK = 128
SG_GROUPS = 4
SG_GROUP_DIM = 128
SG_WIDTH = 512
FOX_HEADS = 8
FOX_HEAD_DIM = 64
FOX_WIDTH = 512
Q_BLOCK = 128
ML_HEADS = 4
ML_HEAD_DIM = 128
ML_WIDTH = 512
CONV_W = 4
D_FF = 2816
IN_NAMES = ("gate", "sg_u", "sg_v", "fox_q", "fox_k", "fox_v", "fox_f",
            "ml_qk", "ml_v", "ml_i", "ml_f", "ml_o")
IN_SPLIT = (N_BRANCH * D_MODEL, SG_WIDTH, SG_WIDTH, FOX_WIDTH, FOX_WIDTH, FOX_WIDTH, FOX_HEADS,
            2 * ML_WIDTH, ML_WIDTH, ML_HEADS, ML_HEADS, ML_WIDTH)
N_IN = N_BRANCH * D_MODEL + 2 * SG_WIDTH + 3 * FOX_WIDTH + FOX_HEADS + 4 * ML_WIDTH + 2 * ML_HEADS

kernel_name = "hybrid_gmlp_fox_mlstm_stream_step"


def rmsnorm(x, g):
    x32 = x.astype(jnp.float32)
    y = x32 * lax.rsqrt(jnp.mean(x32 * x32, axis=-1, keepdims=True) + EPS)
    return (y * g.astype(jnp.float32)).astype(x.dtype)


def layernorm(x, g, b):
    x32 = x.astype(jnp.float32)
    mu = jnp.mean(x32, axis=-1, keepdims=True)
    xc = x32 - mu
    y = xc * lax.rsqrt(jnp.mean(xc * xc, axis=-1, keepdims=True) + EPS)
    return (y * g.astype(jnp.float32) + b.astype(jnp.float32)).astype(x.dtype)


def project_tokens(xn, lp):
    seg = {}
    start = 0
    for name, width in zip(IN_NAMES, IN_SPLIT):
        seg[name] = jnp.einsum("...d,dn->...n", xn, lp["w_in"][:, start:start + width])
        start += width
    lead = xn.shape[:-1]
    out = {}
    out["gates"] = jax.nn.sigmoid(seg["gate"].reshape(lead + (N_BRANCH, D_MODEL)) + lp["b_gate"])
    out["sg_u"] = jax.nn.gelu(seg["sg_u"], approximate=False)
    out["sg_v"] = layernorm(jax.nn.gelu(seg["sg_v"], approximate=False), lp["sg_ln_g"], lp["sg_ln_b"])
    heads_f = lead + (FOX_HEADS, FOX_HEAD_DIM)
    out["fox_q"] = seg["fox_q"].reshape(heads_f)
    out["fox_k"] = seg["fox_k"].reshape(heads_f)
    out["fox_v"] = seg["fox_v"].reshape(heads_f)
    out["fox_logf"] = jax.nn.log_sigmoid((seg["fox_f"] + lp["b_fox_f"]).astype(jnp.float32))
    out["ml_qk"] = seg["ml_qk"]
    out["ml_v"] = seg["ml_v"].reshape(lead + (ML_HEADS, ML_HEAD_DIM))
    out["ml_ig"] = (seg["ml_i"] + lp["b_ml_i"]).astype(jnp.float32)
    out["ml_logf"] = jax.nn.log_sigmoid((seg["ml_f"] + lp["b_ml_f"]).astype(jnp.float32))
    out["ml_o"] = jax.nn.sigmoid(seg["ml_o"])
    return out


def spatial_gating(u, v, w_s, b_s):
    B, T, _ = v.shape
    L = min(T, SG_CHUNK)
    w = w_s[:, :L, :L] * jnp.tril(jnp.ones((L, L), dtype=w_s.dtype))
    vb = v.reshape(B, T // L, L, SG_GROUPS, SG_GROUP_DIM)
    mix = jnp.einsum("gts,bnsgc->bntgc", w, vb) + b_s[:, :L].T[None, None, :, :, None]
    return u * mix.reshape(B, T, SG_WIDTH)


def fox_prompt(q, k, v, logf):
    B, S = q.shape[:2]
    F = jnp.cumsum(logf, axis=1)
    Fk = F.transpose(0, 2, 1)
    nb = S // Q_BLOCK
    qb = q.reshape(B, nb, Q_BLOCK, FOX_HEADS, FOX_HEAD_DIM).swapaxes(0, 1)
    Fb = F.reshape(B, nb, Q_BLOCK, FOX_HEADS).swapaxes(0, 1)
    kpos = jnp.arange(S)
    scale = FOX_HEAD_DIM ** -0.5

    def block(args):
        qi, Fi, i = args
        s = jnp.einsum("bqhd,bkhd->bhqk", qi, k, preferred_element_type=jnp.float32) * scale
        s = s + Fi.transpose(0, 2, 1)[..., None] - Fk[:, :, None, :]
        qpos = i * Q_BLOCK + jnp.arange(Q_BLOCK)
        s = jnp.where(kpos[None, :] <= qpos[:, None], s, -jnp.inf)
        p = jax.nn.softmax(s, axis=-1).astype(v.dtype)
        return jnp.einsum("bhqk,bkhd->bqhd", p, v)

    out = lax.map(block, (qb, Fb, jnp.arange(nb)))
    return out.swapaxes(0, 1).reshape(B, S, FOX_WIDTH)


def fox_sample(q, k, v, logf, ck, cv, clogf):
    B, T = q.shape[:2]
    P = ck.shape[1]
    scale = FOX_HEAD_DIM ** -0.5
    Fp = jnp.cumsum(clogf.astype(jnp.float32), axis=1)
    Fn = Fp[:, -1:, :] + jnp.cumsum(logf, axis=1)
    Fp_t = Fp.transpose(0, 2, 1)
    Fn_t = Fn.transpose(0, 2, 1)
    s_past = jnp.einsum("bqhd,bkhd->bhqk", q, ck.astype(q.dtype), preferred_element_type=jnp.float32) * scale
    s_past = s_past + Fn_t[..., None] - Fp_t[:, :, None, :]
    s_new = jnp.einsum("bqhd,bkhd->bhqk", q, k, preferred_element_type=jnp.float32) * scale
    s_new = s_new + Fn_t[..., None] - Fn_t[:, :, None, :]
    s_new = jnp.where(jnp.tril(jnp.ones((T, T), dtype=bool)), s_new, -jnp.inf)
    p = jax.nn.softmax(jnp.concatenate([s_past, s_new], axis=-1), axis=-1).astype(v.dtype)
    out = (jnp.einsum("bhqk,bkhd->bqhd", p[..., :P], cv.astype(v.dtype))
           + jnp.einsum("bhqk,bkhd->bqhd", p[..., P:], v))
    return out.reshape(B, T, FOX_WIDTH)


def causal_conv(x_raw, prev, w, b):
    T = x_raw.shape[1]
    xp = jnp.concatenate([prev.astype(x_raw.dtype), x_raw], axis=1)
    y = b + w[0] * xp[:, 0:T]
    for j in range(1, CONV_W):
        y = y + w[j] * xp[:, j:j + T]
    return jax.nn.silu(y), xp[:, T:]


def mlstm_chunk(state, xs):
    C, n, m = state
    q, k, v, ig, logf = xs
    L = q.shape[1]
    qf, kf, vf = q.astype(jnp.float32), k.astype(jnp.float32), v.astype(jnp.float32)
    b = jnp.cumsum(logf, axis=1).transpose(0, 2, 1)
    igt = ig.transpose(0, 2, 1)
    causal = jnp.tril(jnp.ones((L, L), dtype=bool))
    Dm = jnp.where(causal, b[..., :, None] - b[..., None, :] + igt[..., None, :], -jnp.inf)
    m_t = jnp.maximum(b + m[..., None], jnp.max(Dm, axis=-1))
    w = jnp.exp(Dm - m_t[..., None])
    decay = jnp.exp(b + m[..., None] - m_t)
    s = jnp.einsum("bthd,bshd->bhts", qf, kf) * w
    num = decay[..., None] * jnp.einsum("bthd,bhde->bhte", qf, C) + jnp.einsum("bhts,bshe->bhte", s, vf)
    den = decay * jnp.einsum("bthd,bhd->bht", qf, n) + jnp.sum(s, axis=-1)
    h = num / jnp.maximum(jnp.abs(den), jnp.exp(-m_t))[..., None]
    w_end = w[..., -1, :]
    d_end = decay[..., -1]
    C_new = d_end[..., None, None] * C + jnp.einsum("bhs,bshd,bshe->bhde", w_end, kf, vf)
    n_new = d_end[..., None] * n + jnp.einsum("bhs,bshd->bhd", w_end, kf)
    return (C_new, n_new, m_t[..., -1]), h.transpose(0, 2, 1, 3)


def mlstm_prompt(q, k, v, ig, logf):
    B, S = q.shape[:2]
    nc = S // CHUNK

    def chunks(a):
        return a.reshape((B, nc, CHUNK) + a.shape[2:]).swapaxes(0, 1)

    init = (jnp.zeros((B, ML_HEADS, ML_HEAD_DIM, ML_HEAD_DIM), jnp.float32),
            jnp.zeros((B, ML_HEADS, ML_HEAD_DIM), jnp.float32),
            jnp.zeros((B, ML_HEADS), jnp.float32))
    state, h = lax.scan(mlstm_chunk, init, (chunks(q), chunks(k), chunks(v), chunks(ig), chunks(logf)))
    return state, h.swapaxes(0, 1).reshape(B, S, ML_HEADS, ML_HEAD_DIM)


def mlstm_qk(qk_act):
    lead = qk_act.shape[:-1]
    q = qk_act[..., :ML_WIDTH].reshape(lead + (ML_HEADS, ML_HEAD_DIM))
    k = qk_act[..., ML_WIDTH:].reshape(lead + (ML_HEADS, ML_HEAD_DIM)) * (ML_HEAD_DIM ** -0.5)
    return q, k


def mlstm_out(h, o, g, dtype):
    lead = h.shape[:-2]
    hh = h * o.reshape(lead + (ML_HEADS, ML_HEAD_DIM)).astype(jnp.float32)
    hh = hh * lax.rsqrt(jnp.mean(hh * hh, axis=-1, keepdims=True) + EPS) * g.astype(jnp.float32)
    return hh.reshape(lead + (ML_WIDTH,)).astype(dtype)


def merge_and_ffn(x, gates, y_sg, y_fox, y_ml, lp):
    mixed = (gates[..., 0, :] * (y_sg @ lp["w_br"][0])
             + gates[..., 1, :] * (y_fox @ lp["w_br"][1])
             + gates[..., 2, :] * (y_ml @ lp["w_br"][2]))
    x = x + mixed @ lp["w_out"]
    hn = rmsnorm(x, lp["norm_ffn"])
    hu = hn @ lp["w_ffn_in"]
    return x + (jax.nn.silu(hu[..., :D_FF]) * hu[..., D_FF:]) @ lp["w_ffn_out"]


def layer_prompt(x, lp):
    B = x.shape[0]
    xn = rmsnorm(x, lp["norm_mix"])
    t = project_tokens(xn, lp)
    y_sg = spatial_gating(t["sg_u"], t["sg_v"], lp["sg_w"], lp["sg_b"])
    y_fox = fox_prompt(t["fox_q"], t["fox_k"], t["fox_v"], t["fox_logf"])
    conv_prev = jnp.zeros((B, CONV_W - 1, 2 * ML_WIDTH), x.dtype)
    qk_act, conv_state = causal_conv(t["ml_qk"], conv_prev, lp["ml_conv_w"], lp["ml_conv_b"])
    q, k = mlstm_qk(qk_act)
    (C, n, m), h = mlstm_prompt(q, k, t["ml_v"], t["ml_ig"], t["ml_logf"])
    y_ml = mlstm_out(h, t["ml_o"], lp["ml_norm_g"], x.dtype)
    x = merge_and_ffn(x, t["gates"], y_sg, y_fox, y_ml, lp)
    return x, (t["fox_k"], t["fox_v"], t["fox_logf"], C, n, m, conv_state)


def layer_sample(x, ck, cv, clogf, c_C, c_n, c_m, c_conv, lp):
    xn = rmsnorm(x, lp["norm_mix"])
    t = project_tokens(xn, lp)
    y_sg = spatial_gating(t["sg_u"], t["sg_v"], lp["sg_w"], lp["sg_b"])
    y_fox = fox_sample(t["fox_q"], t["fox_k"], t["fox_v"], t["fox_logf"], ck, cv, clogf)
    qk_act, conv_state = causal_conv(t["ml_qk"], c_conv, lp["ml_conv_w"], lp["ml_conv_b"])
    q, k = mlstm_qk(qk_act)
    state0 = (c_C.astype(jnp.float32), c_n.astype(jnp.float32), c_m.astype(jnp.float32))
    (C, n, m), h = mlstm_chunk(state0, (q, k, t["ml_v"], t["ml_ig"], t["ml_logf"]))
    y_ml = mlstm_out(h, t["ml_o"], lp["ml_norm_g"], x.dtype)
    x = merge_and_ffn(x, t["gates"], y_sg, y_fox, y_ml, lp)
    return x, (t["fox_k"], t["fox_v"], t["fox_logf"], C, n, m, conv_state, t["sg_v"])


def setup_inputs(seed: int = 0) -> dict:
    key = jax.random.key(seed)
    ks = jax.random.split(key, 32)
    f32 = jnp.float32
    nrm = lambda k, shape, s: jax.random.normal(k, shape, f32) * s
    uni = lambda k, shape, lo, hi: jax.random.uniform(k, shape, f32, lo, hi)
    L = DEPTH
    return {
        "x_prompt": nrm(ks[0], (BATCH, SEQ, D_MODEL), 1.0),
        "x_sample": nrm(ks[1], (DEC_BATCH, DEC_SEQ, D_MODEL), 1.0),
        "cache_fox_k": nrm(ks[2], (L, DEC_BATCH, PAST_LEN, FOX_HEADS, FOX_HEAD_DIM), 1.0),
        "cache_fox_v": nrm(ks[3], (L, DEC_BATCH, PAST_LEN, FOX_HEADS, FOX_HEAD_DIM), 1.0),
        "cache_fox_logf": jax.nn.log_sigmoid(nrm(ks[4], (L, DEC_BATCH, PAST_LEN, FOX_HEADS), 1.0) + 2.5),
        "state_ml_c": nrm(ks[5], (L, DEC_BATCH, ML_HEADS, ML_HEAD_DIM, ML_HEAD_DIM), 1.0),
        "state_ml_n": nrm(ks[6], (L, DEC_BATCH, ML_HEADS, ML_HEAD_DIM), 1.0),
        "state_ml_m": nrm(ks[7], (L, DEC_BATCH, ML_HEADS), 1.0),
        "state_ml_conv": nrm(ks[8], (L, DEC_BATCH, CONV_W - 1, 2 * ML_WIDTH), 1.0),
        "norm_mix": 1.0 + nrm(ks[9], (L, D_MODEL), 0.02),
        "w_in": nrm(ks[10], (L, D_MODEL, N_IN), D_MODEL ** -0.5),
        "b_gate": nrm(ks[11], (L, N_BRANCH, D_MODEL), 0.1),
        "sg_ln_g": 1.0 + nrm(ks[12], (L, SG_WIDTH), 0.02),
        "sg_ln_b": nrm(ks[13], (L, SG_WIDTH), 0.02),
        "sg_w": nrm(ks[14], (L, SG_GROUPS, SG_CHUNK, SG_CHUNK), SG_CHUNK ** -0.5),
        "sg_b": 1.0 + nrm(ks[15], (L, SG_GROUPS, SG_CHUNK), 0.02),
        "b_fox_f": uni(ks[16], (L, FOX_HEADS), 1.0, 4.0),
        "ml_conv_w": nrm(ks[17], (L, CONV_W, 2 * ML_WIDTH), CONV_W ** -0.5),
        "ml_conv_b": nrm(ks[18], (L, 2 * ML_WIDTH), 0.02),
        "b_ml_i": nrm(ks[19], (L, ML_HEADS), 0.1),
        "b_ml_f": uni(ks[20], (L, ML_HEADS), 3.0, 6.0),
        "ml_norm_g": 1.0 + nrm(ks[21], (L, ML_HEADS, ML_HEAD_DIM), 0.02),
        "w_br": nrm(ks[22], (L, N_BRANCH, SG_WIDTH, D_MODEL), SG_WIDTH ** -0.5),
        "w_out": nrm(ks[23], (L, D_MODEL, D_MODEL), D_MODEL ** -0.5),
        "norm_ffn": 1.0 + nrm(ks[24], (L, D_MODEL), 0.02),
        "w_ffn_in": nrm(ks[25], (L, D_MODEL, 2 * D_FF), D_MODEL ** -0.5),
        "w_ffn_out": nrm(ks[26], (L, D_FF, D_MODEL), D_FF ** -0.5),
        "norm_final": 1.0 + nrm(ks[27], (D_MODEL,), 0.02),
    }


def reference(x_prompt, x_sample, cache_fox_k, cache_fox_v, cache_fox_logf, state_ml_c, state_ml_n,
              state_ml_m, state_ml_conv, norm_mix, w_in, b_gate, sg_ln_g, sg_ln_b, sg_w, sg_b, b_fox_f,
              ml_conv_w, ml_conv_b, b_ml_i, b_ml_f, ml_norm_g, w_br, w_out, norm_ffn, w_ffn_in, w_ffn_out,
              norm_final):
    xp, xs = x_prompt, x_sample
    st_p, st_s = [], []
    for l in range(DEPTH):
        lp = dict(norm_mix=norm_mix[l], w_in=w_in[l], b_gate=b_gate[l], sg_ln_g=sg_ln_g[l],
                  sg_ln_b=sg_ln_b[l], sg_w=sg_w[l], sg_b=sg_b[l], b_fox_f=b_fox_f[l],
                  ml_conv_w=ml_conv_w[l], ml_conv_b=ml_conv_b[l], b_ml_i=b_ml_i[l], b_ml_f=b_ml_f[l],
                  ml_norm_g=ml_norm_g[l], w_br=w_br[l], w_out=w_out[l], norm_ffn=norm_ffn[l],
                  w_ffn_in=w_ffn_in[l], w_ffn_out=w_ffn_out[l])
        xp, sp = layer_prompt(xp, lp)
        xs, ss = layer_sample(xs, cache_fox_k[l], cache_fox_v[l], cache_fox_logf[l], state_ml_c[l],
                              state_ml_n[l], state_ml_m[l], state_ml_conv[l], lp)
        st_p.append(sp)
        st_s.append(ss)
    y_prompt = rmsnorm(xp, norm_final)
    y_sample = rmsnorm(xs, norm_final)
    stk = lambda lst, i: jnp.stack([s[i] for s in lst], axis=0)
    return (y_prompt, y_sample,
            stk(st_p, 0), stk(st_p, 1), stk(st_p, 2), stk(st_p, 3), stk(st_p, 4), stk(st_p, 5), stk(st_p, 6),
            stk(st_s, 0), stk(st_s, 1), stk(st_s, 2), stk(st_s, 3), stk(st_s, 4), stk(st_s, 5), stk(st_s, 6),
            stk(st_s, 7))
```

```python
import contextlib
import numpy as np
import concourse.bass as bass
import concourse.mybir as mybir
from concourse.bass_utils import run_bass_kernel_spmd

F32 = mybir.dt.float32
BF16 = mybir.dt.bfloat16
AF = mybir.ActivationFunctionType
ALU = mybir.AluOpType
AX = mybir.AxisListType

D = 1024
NIN = 7696
DFF = 2816
EPS = 1e-6
C_GATE, C_SGU, C_SGV, C_FQ, C_FK, C_FV, C_FF = 0, 3072, 3584, 4096, 4608, 5120, 5632
C_MQK, C_MV, C_MI, C_MF, C_MO = 5640, 6664, 7176, 7180, 7184
SLAB = 4096


class Buf:
    __slots__ = ("name", "w", "r", "excl")

    def __init__(self, name, excl=False):
        self.name = name
        self.w = None
        self.r = {}
        self.excl = excl


class V:
    __slots__ = ("ap", "bufs")

    def __init__(self, ap, bufs):
        self.ap = ap
        self.bufs = bufs


class T:
    def __init__(self, t, name):
        self.t = t
        self.buf = Buf(name)

    def __getitem__(self, idx):
        return V(self.t[idx], [self.buf])

    def v(self, ap):
        return V(ap, [self.buf])


class Queue:
    def __init__(self, name, sem):
        self.name = name
        self.sem = sem
        self.cnt = 0
        self.ops = []
        self.seen = {}
        self.pending = False


class Ctx:
    def __init__(self, nc, n_dma_sems=12):
        self.nc = nc
        self.es = contextlib.ExitStack()
        self.q = {}
        for nm in ("sync", "scalar", "vector", "gpsimd", "tensor"):
            self.q[nm] = Queue(nm, self.es.enter_context(nc.semaphore("s_" + nm)))
        self.dq = {}
        for nm in ("sync", "gpsimd", "scalar"):
            self.dq[nm] = [Queue("dma_%s%d" % (nm, i), self.es.enter_context(nc.semaphore("s_d%s%d" % (nm, i))))
                           for i in range(n_dma_sems)]
        self.dq_next = {"sync": 0, "gpsimd": 0, "scalar": 0}
        self.n_inst = 0

    def sbuf(self, name, shape, dtype):
        return T(self.es.enter_context(self.nc.sbuf_tensor(name, list(shape), dtype)), name)

    def psum(self, name, shape, dtype):
        t = T(self.es.enter_context(self.nc.psum_tensor(name, list(shape), dtype)), name)
        t.buf.excl = True
        return t

    def _need(self, q, waits, dep, raw):
        if dep is None:
            return
        pq, c = dep
        if pq is q and ((not raw) or q.name == "tensor"):
            return
        if q.seen.get(pq, 0) >= c:
            return
        if waits.get(pq, 0) < c:
            waits[pq] = c

    def _deps(self, q, rb, wb):
        waits = {}
        for b in rb:
            self._need(q, waits, b.w, True)
            if b.excl:
                for rq, c in b.r.items():
                    if rq is not q:
                        self._need(q, waits, (rq, c), False)
        for b in wb:
            self._need(q, waits, b.w, False)
            for rq, c in b.r.items():
                self._need(q, waits, (rq, c), False)
        return waits

    def _emit_waits(self, q, waits):
        for pq, c in waits.items():
            q.seen[pq] = c
            q.ops.append(lambda e, sem=pq.sem, c=c: e.wait_ge(sem, c))

    def op(self, eng, fn, reads=(), writes=(), signal=True):
        q = self.q[eng]
        rb = [b for v in reads for b in v.bufs]
        wb = [b for v in writes for b in v.bufs]
        self._emit_waits(q, self._deps(q, rb, wb))
        self.n_inst += 1
        if signal:
            q.cnt += 1
            c = q.cnt
            q.ops.append(lambda e, fn=fn, sem=q.sem: fn(e).then_inc(sem, 1))
            q.pending = False
        else:
            c = q.cnt + 1
            q.ops.append(lambda e, fn=fn: fn(e))
            q.pending = True
        for b in rb:
            if b.r.get(q, 0) < c:
                b.r[q] = c
        for b in wb:
            b.w = (q, c)
            b.r = {}

    def dma(self, eng, out, in_, **kw):
        q = self.q[eng]
        pool = self.dq[eng]
        dq = pool[self.dq_next[eng]]
        self.dq_next[eng] = (self.dq_next[eng] + 1) % len(pool)
        rb = list(in_.bufs) if isinstance(in_, V) else []
        wb = list(out.bufs) if isinstance(out, V) else []
        waits = self._deps(q, rb, wb)
        if dq.cnt > 0 and q.seen.get(dq, 0) < dq.cnt and waits.get(dq, 0) < dq.cnt:
            waits[dq] = dq.cnt
        self._emit_waits(q, waits)
        dq.cnt += 16
        c = dq.cnt
        oap = out.ap if isinstance(out, V) else out
        iap = in_.ap if isinstance(in_, V) else in_
        q.ops.append(lambda e, oap=oap, iap=iap, sem=dq.sem, kw=kw: e.dma_start(out=oap, in_=iap, **kw).then_inc(sem, 16))
        self.n_inst += 1
        for b in rb:
            if b.r.get(dq, 0) < c:
                b.r[dq] = c
        for b in wb:
            b.w = (dq, c)
            b.r = {}

    def barrier(self):
        allq = list(self.q.values())
        for q in allq:
            assert not q.pending, q.name
        alld = [d for pool in self.dq.values() for d in pool]
        for q in allq:
            waits = {}
            for pq in allq + alld:
                if pq is not q and pq.cnt > 0 and q.seen.get(pq, 0) < pq.cnt:
                    waits[pq] = pq.cnt
            self._emit_waits(q, waits)

    def finish(self):
        self.barrier()
        nc = self.nc
        with nc.Block() as block:
            @block.sync
            def _(e):
                for f in self.q["sync"].ops:
                    f(e)

            @block.scalar
            def _(e):
                for f in self.q["scalar"].ops:
                    f(e)

            @block.vector
            def _(e):
                for f in self.q["vector"].ops:
                    f(e)

            @block.gpsimd
            def _(e):
                for f in self.q["gpsimd"].ops:
                    f(e)

            @block.tensor
            def _(e):
                for f in self.q["tensor"].ops:
                    f(e)
        self.es.close()


def build(cfg):
    NSEQ = cfg.get("nseq", 2)
    NBLK = cfg.get("nblk", cfg.get("SP", 4096) // 512)
    DO_SAMPLE = cfg.get("sample", True)
    NL = cfg.get("nl", 2)
    STAGE = cfg.get("stage", 99)
    FS = cfg.get("fs", 99)
    QS = cfg.get("qs", 99)
    SP = cfg.get("SP", 4096)
    PAST = cfg.get("PAST", 4096)
    NKP = PAST // 128

    nc = bass.Bass("TRN2", target_bir_lowering=False)
    cx = contextlib.ExitStack()
    cx.enter_context(nc.allow_non_contiguous_dma(reason="small strided parameter / state transfers"))
    cx.enter_context(nc.allow_low_precision(reason="bf16 matmul operands, fp32 accumulation"))

    def din(name, shape, dt=F32):
        return nc.dram_tensor(name, list(shape), dt, kind="ExternalInput").ap()

    def dout(name, shape, dt=F32):
        return nc.dram_tensor(name, list(shape), dt, kind="ExternalOutput").ap()

    xp = din("xp", [2, SP, D]); xs = din("xs", [4, 16, D])
    ck = din("ck", [2, 4, PAST, 512]); cv = din("cv", [2, 4, PAST, 512]); clf = din("clf", [2, 4, PAST, 8])
    smc = din("smc", [2, 4, 4, 128, 128]); smn = din("smn", [2, 4, 4, 128]); smm = din("smm", [2, 4, 4])
    smconv = din("smconv", [2, 4, 3, D])
    norm_mix = din("norm_mix", [2, D]); w_in = din("w_in", [2, D, NIN]); b_gate = din("b_gate", [2, 3, D])
    sg_ln_g = din("sg_ln_g", [2, 512]); sg_ln_b = din("sg_ln_b", [2, 512]); sg_w = din("sg_w", [2, 4, 128, 128])
    sg_b = din("sg_b", [2, 4, 128]); b_fox_f = din("b_fox_f", [2, 8]); ml_conv_w = din("ml_conv_w", [2, 4, D])
    ml_conv_b = din("ml_conv_b", [2, D]); b_ml_i = din("b_ml_i", [2, 4]); b_ml_f = din("b_ml_f", [2, 4])
    ml_norm_g = din("ml_norm_g", [2, 4, 128]); w_br = din("w_br", [2, 3, 512, D]); w_out = din("w_out", [2, D, D])
    norm_ffn = din("norm_ffn", [2, D]); w_ffn_in = din("w_ffn_in", [2, D, 2 * DFF]); w_ffn_out = din("w_ffn_out", [2, DFF, D])
    norm_final = din("norm_final", [1, D])
    c_ident = din("c_ident", [128, 128]); c_tri = din("c_tri", [128, 128])

    y_p = dout("y_p", [2, SP, D]); y_s = dout("y_s", [4, 16, D])
    fk_p = dout("fk_p", [2, 2, SP, 512]); fv_p = dout("fv_p", [2, 2, SP, 512]); flf_p = dout("flf_p", [2, 2, SP, 8])
    mc_p = dout("mc_p", [2, 2, 4, 128, 128]); mn_p = dout("mn_p", [2, 2, 4, 128]); mm_p = dout("mm_p", [2, 2, 4])
    mconv_p = dout("mconv_p", [2, 2, 3, D])
    fk_s = dout("fk_s", [2, 4, 16, 512]); fv_s = dout("fv_s", [2, 4, 16, 512]); flf_s = dout("flf_s", [2, 4, 16, 8])
    mc_s = dout("mc_s", [2, 4, 4, 128, 128]); mn_s = dout("mn_s", [2, 4, 4, 128]); mm_s = dout("mm_s", [2, 4, 4])
    mconv_s = dout("mconv_s", [2, 4, 3, D]); sgv_s = dout("sgv_s", [2, 4, 16, 512])

    DBG = cfg.get("dbg", False)
    if DBG:
        dbg_fox = dout("dbg_fox", [64, 8, 512], BF16); dbg_sg = dout("dbg_sg", [128, 4, 512], BF16); dbg_ml = dout("dbg_ml", [128, 4, 512], BF16)
    kts = nc.dram_tensor("kts", [2, 32, 128, 512], BF16, kind="Internal").ap()
    vsc = nc.dram_tensor("vsc", [2, 32, 128, 576], BF16, kind="Internal").ap()
    kts_buf = [[Buf("kts%d_%d" % (l, k)) for k in range(32)] for l in range(2)]
    vsc_buf = [[Buf("vsc%d_%d" % (l, k)) for k in range(32)] for l in range(2)]

    c = Ctx(nc)
    sb = c.sbuf

    def rd(*xs_):
        return [x for x in xs_ if isinstance(x, V)]

    def A(x):
        return x.ap if isinstance(x, V) else x

    def mm(out, lhsT, rhs, start, stop, signal=None):
        c.op("tensor", lambda e: e.matmul(out.ap, lhsT=lhsT.ap, rhs=rhs.ap, start=start, stop=stop, skip_group_check=True),
             reads=[lhsT, rhs], writes=[out], signal=(stop if signal is None else signal))

    def tr(out, in_, ident, signal=True):
        c.op("tensor", lambda e: e.transpose(out.ap, in_.ap, ident.ap), reads=[in_, ident], writes=[out], signal=signal)

    def act(out, in_, func, bias=None, scale=1.0, accum=None, eng="scalar"):
        kw = {}
        if bias is not None:
            kw["bias"] = A(bias)
        if accum is not None:
            kw["accum_out"] = accum.ap
        c.op("scalar", lambda e: e.activation(out=out.ap, in_=in_.ap, func=func, scale=A(scale), **kw),
             reads=rd(in_, bias, scale), writes=[out] + ([accum] if accum is not None else []))

    def tt(out, in0, in1, op, eng="vector"):
        c.op(eng, lambda e: e.tensor_tensor(out=out.ap, in0=in0.ap, in1=in1.ap, op=op), reads=[in0, in1], writes=[out])

    def ts(out, in0, s1, s2, op0, op1=None, eng="vector"):
        if op1 is None:
            c.op(eng, lambda e: e.tensor_single_scalar(out=out.ap, in_=in0.ap, scalar=A(s1), op=op0), reads=rd(in0, s1), writes=[out])
        else:
            c.op(eng, lambda e: e.tensor_scalar(out=out.ap, in0=in0.ap, scalar1=A(s1), scalar2=A(s2), op0=op0, op1=op1),
                 reads=rd(in0, s1, s2), writes=[out])

    def stt(out, in0, scalar, in1, op0, op1, eng="vector"):
        c.op(eng, lambda e: e.scalar_tensor_tensor(out=out.ap, in0=in0.ap, scalar=A(scalar), in1=in1.ap, op0=op0, op1=op1),
             reads=rd(in0, scalar, in1), writes=[out])

    def cp(out, in_, eng="vector"):
        if eng == "scalar":
            c.op("scalar", lambda e: e.copy(out=out.ap, in_=in_.ap), reads=[in_], writes=[out])
        else:
            c.op(eng, lambda e: e.tensor_copy(out=out.ap, in_=in_.ap), reads=[in_], writes=[out])

    def memset(out, val, eng="vector"):
        c.op(eng, lambda e: e.memset(out.ap, val), writes=[out])

    def bc(v, shape, axis):
        return V(v.ap.unsqueeze(axis).to_broadcast(list(shape)), v.bufs)

    ident_f = sb("ident_f", [128, 128], F32); ident_b = sb("ident_b", [128, 128], BF16)
    tri_f = sb("tri_f", [128, 128], F32); tri_b = sb("tri_b", [128, 128], BF16)
    ones_f = sb("ones_f", [128, 128], F32)
    c.dma("sync", ident_f[:], c_ident); c.dma("sync", tri_f[:], c_tri)
    cp(ident_b[:], ident_f[:]); cp(tri_b[:], tri_f[:]); memset(ones_f[:], 1.0)

    gmix = sb("gmix", [128, 2, 8], F32); gffn = sb("gffn", [128, 2, 8], F32)
    bgate = sb("bgate", [128, 2, 3, 8], F32)
    wconv = sb("wconv", [128, 2, 4, 8], F32); bconv = sb("bconv", [128, 2, 8], F32)
    gml = sb("gml", [128, 2, 4], F32)
    lng = sb("lng", [128, 2, 512], F32); lnb = sb("lnb", [128, 2, 512], F32)
    bsbc = sb("bsbc", [128, 2, 512], F32)
    wsT = sb("wsT", [128, 2, 4, 128], BF16)
    bsm = sb("bsm", [128, 2, 16], F32)
    wsm = sb("wsm", [128, 2, 8, 16], BF16)
    nfin = sb("nfin", [128, D], F32)
    for l in range(2):
        c.dma("sync", gmix[:, l, :], norm_mix[l].rearrange("(c p) -> p c", p=128))
        c.dma("sync", gffn[:, l, :], norm_ffn[l].rearrange("(c p) -> p c", p=128))
        for b in range(3):
            c.dma("sync", bgate[:, l, b, :], b_gate[l, b].rearrange("(c p) -> p c", p=128))
        for j in range(4):
            c.dma("sync", wconv[:, l, j, :], ml_conv_w[l, j].rearrange("(c p) -> p c", p=128))
        c.dma("sync", bconv[:, l, :], ml_conv_b[l].rearrange("(c p) -> p c", p=128))
        c.dma("sync", gml[:, l, :], ml_norm_g[l].rearrange("h p -> p h"))
        c.dma("sync", lng[:, l, :], sg_ln_g[l:l + 1, :].partition_broadcast(128))
        c.dma("sync", lnb[:, l, :], sg_ln_b[l:l + 1, :].partition_broadcast(128))
        c.dma("sync", bsbc[:, l, :], sg_b.rearrange("l g t -> l (g t)")[l:l + 1, :].partition_broadcast(128))
        c.dma("sync", bsm[:, l, 0:8], b_fox_f[l:l + 1, :].partition_broadcast(128))
        c.dma("sync", bsm[:, l, 8:12], b_ml_i[l:l + 1, :].partition_broadcast(128))
        c.dma("sync", bsm[:, l, 12:16], b_ml_f[l:l + 1, :].partition_broadcast(128))
        wv = w_in[l].rearrange("(kc p) n -> p kc n", p=128)
        c.dma("gpsimd", wsm[:, l, :, 0:8], wv[:, :, C_FF:C_FF + 8])
        c.dma("gpsimd", wsm[:, l, :, 8:16], wv[:, :, C_MI:C_MI + 8])
    c.dma("sync", nfin[:], norm_final.partition_broadcast(128))

    bank = [c.psum("bank%d" % i, [128, 512], F32) for i in range(8)]

    def bkb(i):
        return bank[i].t[:, :].bitcast(BF16)

    stg = [sb("stg%d" % i, [128, D], F32) for i in range(2)]
    for l in range(2):
        wtv = V(stg[0].t[:, 0:512].rearrange("p (g s) -> p g s", g=4), [stg[0].buf])
        c.dma("sync", wtv, sg_w[l].rearrange("g t s -> t g s"))
        for g in range(4):
            mm(bank[g][:, 0:128], stg[0][:, g * 128:(g + 1) * 128], ident_f[:], True, True)
            tt(wsT[:, l, g, :], bank[g][:, 0:128], tri_f[:], ALU.mult)

    X = sb("X", [128, 4, D], F32)
    xnT = sb("xnT", [128, 8, 512], BF16)
    slabs = [sb("slab%d" % i, [128, SLAB], BF16) for i in range(3)]
    junk = sb("junk", [128, D], BF16)
    xs_bs = [sb("xs_b%d" % i, [128, D], BF16) for i in range(2)]
    st_col = sb("st_col", [128, 8], F32)
    uT = sb("uT", [128, 4, 512], BF16); vb = sb("vb", [128, 4, 512], BF16); ysgT = sb("ysgT", [128, 4, 512], BF16)
    sga = sb("sga", [128, 512], F32); sgt = sb("sgt", [128, 512], F32)
    qT = sb("qT", [128, 4, 512], BF16); kTb = sb("kTb", [128, 4, 512], BF16)
    vaug = sb("vaug", [128, 4, 8, 72], BF16)
    yfoxT = sb("yfoxT", [64, 8, 512], BF16)
    zz = sb("zz", [128, 4, 16], F32); lsg = sb("lsg", [128, 4, 16], F32)
    zt = [sb("zt%d" % i, [128, 16], F32) for i in range(3)]
    negF = [sb("negF%d" % l, [128, 33, 8], F32) for l in range(2)]
    Rrun = [sb("Rrun%d" % l, [128, 8], F32) for l in range(2)]
    biasq = sb("biasq", [128, 33, 8], F32)
    arena = sb("arena", [128, 12800], BF16)
    ar = arena.t
    def aview(off, n, name):
        t_ = T(ar[:, off:off + n], name)
        return t_
    ktg = [aview(0, 2048, "ktg0"), aview(2048, 2048, "ktg1")]
    vg = [aview(4096, 2304, "vg0"), aview(6400, 2304, "vg1")]
    kst = aview(8704, 2048, "kst"); vst = aview(10752, 2048, "vst")
    hT = aview(0, 11264, "hT")
    mixTt = aview(0, 4096, "mixT")
    pT = [sb("pT%d" % i, [128, 4, 128], BF16) for i in range(4)]
    ptot = sb("ptot", [128, 33, 8], F32)
    rawT = [sb("rawT%d" % i, [128, 4, 131], F32) for i in range(2)]
    qmT = sb("qmT", [128, 4, 512], BF16); kmT = sb("kmT", [128, 4, 512], BF16)
    vml = sb("vml", [128, 4, 512], BF16); omt = sb("omt", [128, 4, 512], BF16)
    vpa = sb("vpa", [128, 4, 136], BF16); ktok = sb("ktok", [128, 4, 128], BF16)
    sTm = sb("sTm", [128, 4, 128], BF16); Cbf = sb("Cbf", [128, 4, 136], BF16)
    ymt = sb("ymt", [128, 512], BF16); ymlT = sb("ymlT", [128, 4, 512], BF16)
    Cst = [sb("Cst%d" % l, [128, 4, 132], F32) for l in range(2)]
    mrep = [sb("mrep%d" % l, [128, 4], F32) for l in range(2)]
    halo = [sb("halo%d" % l, [128, 8, 3], F32) for l in range(2)]
    sm = [sb("sm%d" % i, [128, 16], F32) for i in range(8)]
    dg = sb("dg", [4, 4], F32)
    gsb = [sb("gsb%d" % i, [128, 512], BF16) for i in range(3)]
    tmf = [sb("tmf%d" % i, [128, 512], F32) for i in range(3)]
    hh = tmf[0]; rsum = tmf[1]; bcs = tmf[2]
    saf = [sga, sgt]
    yout = stg
    mixv = mixTt.t[:, :].rearrange("p (k w) -> p k w", k=8)

    memset(vaug[:, :, :, 64:65], 1.0)
    for i in range(2):
        memset(V(vg[i].t[:, :].rearrange("p (k h e) -> p k h e", k=4, h=8)[:, :, :, 64:65], [vg[i].buf]), 1.0)

    class WS:
        def __init__(self):
            self.plan = []
            self.issued = 0
            self.taken = 0

        def add(self, parts):
            self.plan.append(parts)

        def _issue(self, i):
            slot = slabs[i % len(slabs)]
            for (off, shape, src) in self.plan[i]:
                n = 1
                for s_ in shape[1:]:
                    n *= s_
                dst = slot.t[0:shape[0], off:off + n]
                if len(shape) == 3:
                    dst = dst.rearrange("p (a b) -> p a b", a=shape[1])
                elif len(shape) == 4:
                    dst = dst.rearrange("p (a b d) -> p a b d", a=shape[1], b=shape[2])
                c.dma("gpsimd", V(dst, [slot.buf]), src)

        def take(self):
            i = self.taken
            self.taken += 1
            while self.issued < len(self.plan) and self.issued <= i + len(slabs) - 2:
                self._issue(self.issued)
                self.issued += 1
            assert self.issued > i
            return slabs[i % len(slabs)]

    ws = WS()

    def plan_layer(l):
        wv = w_in[l].rearrange("(kc p) n -> p kc n", p=128)
        for c0 in (C_SGU, C_SGV, C_FQ, C_FK, C_FV, C_MQK, C_MQK + 512, C_MV, C_MO):
            ws.add([(0, (128, 8, 512), wv[:, :, c0:c0 + 512])])
        for j in range(8):
            ws.add([(b * 1024, (128, 8, 128), wv[:, :, b * 1024 + j * 128: b * 1024 + (j + 1) * 128]) for b in range(3)])
            ws.add([(0, (128, 4, 128), w_br[l, 0].rearrange("(c p) n -> p c n", p=128)[:, :, j * 128:(j + 1) * 128]),
                    (512, (128, 4, 128), w_br[l, 2].rearrange("(c p) n -> p c n", p=128)[:, :, j * 128:(j + 1) * 128]),
                    (1024, (64, 8, 128), w_br[l, 1].rearrange("(h p) n -> p h n", p=64)[:, :, j * 128:(j + 1) * 128])])
        wo = w_out[l].rearrange("(kc p) n -> p kc n", p=128)
        for nh in range(2):
            ws.add([(0, (128, 8, 512), wo[:, :, nh * 512:(nh + 1) * 512])])
        wf = w_ffn_in[l].rearrange("(kc p) n -> p kc n", p=128)
        for jg in range(6):
            ncol = 512 if jg < 5 else 256
            ws.add([(0, (128, 8, ncol), wf[:, :, jg * 512: jg * 512 + ncol])])
            ws.add([(0, (128, 8, ncol), wf[:, :, DFF + jg * 512: DFF + jg * 512 + ncol])])
        wfo = w_ffn_out[l].rearrange("(j p) n -> p j n", p=128)
        for nh in range(2):
            for js in range(6):
                nj = 4 if js < 5 else 2
                ws.add([(0, (128, nj, 512), wfo[:, js * 4: js * 4 + nj, nh * 512:(nh + 1) * 512])])

    def rstd_from_ss(ss, n, Lp, tmp):
        ts(ss, ss, 1.0 / n, EPS, ALU.mult, ALU.add)
        act(ss, ss, AF.Ln)
        act(ss, ss, AF.Exp, scale=-0.5)

    def norm_to_featT(l, gcol, L, ntile):
        W = ntile * L
        for t in range(ntile):
            ss = st_col[0:L, 0:1]
            act(junk[0:L, :], X[0:L, t, :], AF.Square, accum=ss)
            rstd_from_ss(ss, D, L, None)
            xs_b = xs_bs[t % 2]
            ts(xs_b[0:L, :], X[0:L, t, :], ss, None, ALU.mult)
            pb = bkb(t % 2)
            for kc in range(8):
                tr(V(pb[:, kc * L:(kc + 1) * L], [bank[t % 2].buf]), xs_b[0:L, kc * 128:(kc + 1) * 128], ident_b[0:L, 0:L], signal=(kc == 7))
            tt(xnT[:, :, t * L:(t + 1) * L], V(pb[:, 0:8 * L].rearrange("p (c j) -> p c j", c=8), [bank[t % 2].buf]),
               bc(gcol, [128, 8, L], 2), ALU.mult)

    def proj_F(slab, cols, W, bk):
        sv = slab.t[:, :].rearrange("p (a b) -> p a b", a=8)
        for kc in range(8):
            mm(bank[bk][:, 0:W], V(sv[:, kc, cols[0]:cols[1]], [slab.buf]), xnT[:, kc, 0:W], kc == 0, kc == 7)

    def proj_T(slab, ncol, t, L, bk, n0=0):
        sv = slab.t[:, 0:8 * ncol].rearrange("p (a b) -> p a b", a=8)
        for kc in range(8):
            mm(bank[bk][0:L, 0:ncol], xnT[:, kc, t * L:(t + 1) * L], V(sv[:, kc, :], [slab.buf]), kc == 0, kc == 7)

    def layer_block(l, tiles):
        L = tiles[0]["L"]
        NT = len(tiles)
        W = NT * L
        norm_to_featT(l, gmix[:, l, :], L, NT)
        if STAGE < 1:
            return
        s_u = ws.take()
        for cc in range(4):
            bk = cc % 4
            proj_F(s_u, (cc * 128, (cc + 1) * 128), W, bk)
            act(uT[:, cc, 0:W], bank[bk][:, 0:W], AF.Gelu)
        s_v = ws.take()
        for t, tl in enumerate(tiles):
            bk = 4 + t % 2
            proj_T(s_v, 512, t, L, bk)
            sacc = st_col[0:L, 1:2]
            act(sga[0:L, :], bank[bk][0:L, :], AF.Gelu, accum=sacc)
            ts(sacc, sacc, -1.0 / 512, None, ALU.mult)
            ssq = st_col[0:L, 2:3]
            act(sgt[0:L, :], sga[0:L, :], AF.Square, bias=sacc, accum=ssq)
            rstd_from_ss(ssq, 512, L, None)
            ts(sga[0:L, :], sga[0:L, :], sacc, ssq, ALU.add, ALU.mult)
            tt(sga[0:L, :], sga[0:L, :], lng[0:L, l, :], ALU.mult)
            if tl["kind"] == "s":
                tt(sgt[0:L, :], sga[0:L, :], lnb[0:L, l, :], ALU.add)
                c.dma("sync", sgv_s[l, tl["b"]], sgt[0:L, :])
                cp(vb[0:L, t, :], sgt[0:L, :])
            else:
                tt(vb[0:L, t, :], sga[0:L, :], lnb[0:L, l, :], ALU.add)
            bk2 = 6 + t % 2
            for g in range(4):
                mm(bank[bk2][:, g * L:(g + 1) * L], vb[0:L, t, g * 128:(g + 1) * 128], wsT[0:L, l, g, 0:L], True, True, signal=(g == 3))
            mx = V(bank[bk2].t[:, 0:4 * L].rearrange("p (g j) -> p g j", g=4), [bank[bk2].buf])
            tmv = V(tmf[0].t[:, 0:4 * L].rearrange("p (g j) -> p g j", g=4), [tmf[0].buf])
            tt(tmv, mx, V(bsbc.t[:, l, :].rearrange("p (g j) -> p g j", g=4)[:, :, 0:L], [bsbc.buf]), ALU.add)
            tt(ysgT[:, :, t * L:(t + 1) * L], tmv, uT[:, :, t * L:(t + 1) * L], ALU.mult)
        if STAGE < 2:
            return
        s_q = ws.take()
        for cc in range(4):
            proj_F(s_q, (cc * 128, (cc + 1) * 128), W, cc)
            act(qT[:, cc, 0:W], bank[cc][:, 0:W], AF.Copy, scale=0.125)
        s_k = ws.take()
        for cc in range(4):
            proj_F(s_k, (cc * 128, (cc + 1) * 128), W, 4 + cc)
            cp(kTb[:, cc, 0:W], bank[4 + cc][:, 0:W])
        for t, tl in enumerate(tiles):
            proj_T(s_k, 512, t, L, t % 4)
            st = stg[t % 2]
            cp(st[0:L, 0:512], bank[t % 4][0:L, :], eng="scalar")
            dst = (fk_p[l, tl["b"], tl["pos"]:tl["pos"] + L, :] if tl["kind"] == "p" else fk_s[l, tl["b"]])
            c.dma("sync", dst, st[0:L, 0:512])
        s_vv = ws.take()
        for t, tl in enumerate(tiles):
            bk = 4 + t % 4
            proj_T(s_vv, 512, t, L, bk)
            st = stg[t % 2]
            cp(st[0:L, 512:1024], bank[bk][0:L, :], eng="scalar")
            dst = (fv_p[l, tl["b"], tl["pos"]:tl["pos"] + L, :] if tl["kind"] == "p" else fv_s[l, tl["b"]])
            c.dma("sync", dst, st[0:L, 512:1024])
            cp(vaug[0:L, t, :, 0:64], V(bank[bk].t[0:L, :].rearrange("p (h e) -> p h e", h=8), [bank[bk].buf]))
        for t, tl in enumerate(tiles):
            bk = t % 4
            for kc in range(8):
                mm(bank[bk][0:L, 0:16], xnT[:, kc, t * L:(t + 1) * L], wsm[:, l, kc, :], kc == 0, kc == 7)
            tt(zz[0:L, t, :], bank[bk][0:L, 0:16], bsm[0:L, l, :], ALU.add)
            stt(zt[0][0:L, :], zz[0:L, t, :], -1.0, zz[0:L, t, :], ALU.mult, ALU.max)
            act(zt[1][0:L, :], zt[0][0:L, :], AF.Exp, scale=-1.0)
            act(zt[1][0:L, :], zt[1][0:L, :], AF.Ln, bias=1.0)
            ts(zt[2][0:L, :], zz[0:L, t, :], 0.0, None, ALU.min)
            tt(lsg[0:L, t, :], zt[2][0:L, :], zt[1][0:L, :], ALU.subtract)
            dst = (flf_p[l, tl["b"], tl["pos"]:tl["pos"] + L, :] if tl["kind"] == "p" else flf_s[l, tl["b"]])
            c.dma("sync", dst, lsg[0:L, t, 0:8])
        if STAGE < 3:
            return
        for t, tl in enumerate(tiles):
            fox_tile(l, t, tl, L)
        if STAGE < 4:
            return
        s_mq = ws.take()
        s_mk = ws.take()
        for ci in range(8):
            slab = s_mq if ci < 4 else s_mk
            cc = ci % 4
            bk = ci % 4
            proj_F(slab, (cc * 128, (cc + 1) * 128), W, bk)
            rw = rawT[ci % 2]
            cp(rw[:, 0:NT, 3:3 + L], V(bank[bk].t[:, 0:W].rearrange("p (t j) -> p t j", t=NT), [bank[bk].buf]), eng="scalar")
            for t, tl in enumerate(tiles):
                if tl["kind"] == "s":
                    c.dma("sync", rw[:, t, 0:3], smconv[l, tl["b"], :, ci * 128:(ci + 1) * 128].rearrange("j p -> p j"))
                elif t == 0:
                    if tl["first"]:
                        memset(rw[:, 0, 0:3], 0.0)
                    else:
                        cp(rw[:, 0, 0:3], halo[l][:, ci, :])
                else:
                    cp(rw[:, t, 0:3], rw[:, t - 1, L:L + 3])
            for t, tl in enumerate(tiles):
                if tl["kind"] == "s" or tl["last"]:
                    dst = (mconv_p if tl["kind"] == "p" else mconv_s)[l, tl["b"], :, ci * 128:(ci + 1) * 128].rearrange("j p -> p j")
                    c.dma("sync", dst, rw[:, t, L:L + 3])
            if tiles[-1]["kind"] == "p" and not tiles[-1]["last"]:
                cp(halo[l][:, ci, :], rw[:, NT - 1, L:L + 3])
            ca_t = stg[ci % 2]
            class _CA:
                def __getitem__(self_, idx):
                    return V(ca_t.t[:, 0:512].rearrange("p (t j) -> p t j", t=4)[idx], [ca_t.buf])
            ca = _CA()
            ts(ca[:, 0:NT, 0:L], rw[:, 0:NT, 3:3 + L], wconv[:, l, 3, ci:ci + 1], None, ALU.mult)
            for j in (2, 1, 0):
                stt(ca[:, 0:NT, 0:L], rw[:, 0:NT, j:j + L], wconv[:, l, j, ci:ci + 1], ca[:, 0:NT, 0:L], ALU.mult, ALU.add)
            if ci < 4:
                act(V(qmT.t[:, cc, 0:W].rearrange("p (t j) -> p t j", t=NT), [qmT.buf]), ca[:, 0:NT, 0:L], AF.Silu, bias=bconv[:, l, ci:ci + 1])
            else:
                act(ca[:, 0:NT, 0:L], ca[:, 0:NT, 0:L], AF.Silu, bias=bconv[:, l, ci:ci + 1])
                ts(V(kmT.t[:, cc, 0:W].rearrange("p (t j) -> p t j", t=NT), [kmT.buf]), ca[:, 0:NT, 0:L], 128.0 ** -0.5, None, ALU.mult)
        s_mv = ws.take()
        for t in range(NT):
            bk = 4 + t % 4
            proj_T(s_mv, 512, t, L, bk)
            cp(vml[0:L, t, :], bank[bk][0:L, :])
        s_mo = ws.take()
        for t in range(NT):
            bk = t % 4
            proj_T(s_mo, 512, t, L, bk)
            act(omt[0:L, t, :], bank[bk][0:L, :], AF.Sigmoid)
        if STAGE < 5:
            return
        for t, tl in enumerate(tiles):
            ml_tile(l, t, tl, L)
        if DBG and l == 0:
            c.dma("sync", dbg_fox, yfoxT[:]); c.dma("sync", dbg_sg, ysgT[:]); c.dma("sync", dbg_ml, ymlT[:])
        if STAGE < 6:
            return
        for j in range(8):
            s_g = ws.take()
            s_b = ws.take()
            gv = s_g.t[:, 0:3072].rearrange("p (b kc n) -> p b kc n", b=3, kc=8)
            for b in range(3):
                for kc in range(8):
                    mm(bank[b][:, 0:W], V(gv[:, b, kc, :], [s_g.buf]), xnT[:, kc, 0:W], kc == 0, kc == 7)
                act(gsb[b][:, 0:W], bank[b][:, 0:W], AF.Sigmoid, bias=bgate[:, l, b, j:j + 1])
            w0 = s_b.t[:, 0:512].rearrange("p (c n) -> p c n", c=4)
            w2 = s_b.t[:, 512:1024].rearrange("p (c n) -> p c n", c=4)
            w1 = s_b.t[0:64, 1024:2048].rearrange("p (h n) -> p h n", h=8)
            for cc in range(4):
                mm(bank[4][:, 0:W], V(w0[:, cc, :], [s_b.buf]), ysgT[:, cc, 0:W], cc == 0, cc == 3)
            for h in range(8):
                mm(bank[5][:, 0:W], V(w1[:, h, :], [s_b.buf]), yfoxT[0:64, h, 0:W], h == 0, h == 7)
            for cc in range(4):
                mm(bank[6][:, 0:W], V(w2[:, cc, :], [s_b.buf]), ymlT[:, cc, 0:W], cc == 0, cc == 3)
            for b in range(3):
                tt(tmf[b][:, 0:W], bank[4 + b][:, 0:W], gsb[b][:, 0:W], ALU.mult)
            tt(tmf[0][:, 0:W], tmf[0][:, 0:W], tmf[1][:, 0:W], ALU.add)
            tt(V(mixv[:, j, 0:W], [mixTt.buf]), tmf[0][:, 0:W], tmf[2][:, 0:W], ALU.add)
        for nh in range(2):
            s_o = ws.take()
            sv = s_o.t[:, :].rearrange("p (a b) -> p a b", a=8)
            for t in range(NT):
                bk = (nh * NT + t) % 8
                for kc in range(8):
                    mm(bank[bk][0:L, :], V(mixv[:, kc, t * L:(t + 1) * L], [mixTt.buf]), V(sv[:, kc, :], [s_o.buf]), kc == 0, kc == 7)
                tt(X[0:L, t, nh * 512:(nh + 1) * 512], X[0:L, t, nh * 512:(nh + 1) * 512], bank[bk][0:L, :], ALU.add)
        if STAGE < 7:
            return
        norm_to_featT(l, gffn[:, l, :], L, NT)
        c.barrier()
        hv = hT.t[:, :].rearrange("p (j w) -> p j w", j=22)
        for jg in range(6):
            s_a = ws.take()
            s_bb = ws.take()
            nj = 4 if jg < 5 else 2
            ncol = nj * 128
            for jj in range(nj):
                j = jg * 4 + jj
                bka, bkb_ = (2 * jj) % 8, (2 * jj + 1) % 8
                sva = s_a.t[:, 0:8 * ncol].rearrange("p (a b) -> p a b", a=8)
                svb = s_bb.t[:, 0:8 * ncol].rearrange("p (a b) -> p a b", a=8)
                for kc in range(8):
                    mm(bank[bka][:, 0:W], V(sva[:, kc, jj * 128:(jj + 1) * 128], [s_a.buf]), xnT[:, kc, 0:W], kc == 0, kc == 7)
                for kc in range(8):
                    mm(bank[bkb_][:, 0:W], V(svb[:, kc, jj * 128:(jj + 1) * 128], [s_bb.buf]), xnT[:, kc, 0:W], kc == 0, kc == 7)
                sa = saf[j % 2]
                act(sa[:, 0:W], bank[bka][:, 0:W], AF.Silu)
                tt(V(hv[:, j, 0:W], [hT.buf]), sa[:, 0:W], bank[bkb_][:, 0:W], ALU.mult)
        for nh in range(2):
            for js in range(6):
                s_f = ws.take()
                nj = 4 if js < 5 else 2
                sv = s_f.t[:, 0:nj * 512].rearrange("p (a b) -> p a b", a=nj)
                for t in range(NT):
                    bk = nh * 4 + t
                    for jj in range(nj):
                        j = js * 4 + jj
                        mm(bank[bk][0:L, :], V(hv[:, j, t * L:(t + 1) * L], [hT.buf]), V(sv[:, jj, :], [s_f.buf]), j == 0, j == 21, signal=(jj == nj - 1))
            for t in range(NT):
                bk = nh * 4 + t
                tt(X[0:L, t, nh * 512:(nh + 1) * 512], X[0:L, t, nh * 512:(nh + 1) * 512], bank[bk][0:L, :], ALU.add)
        c.barrier()

    def fox_tile(l, t, tl, L):
        kb = tl["kb"]
        b = tl["b"]
        R = Rrun[l]
        nF = negF[l]
        if tl["kind"] == "s":
            clt = stg[1]
            cl3 = clt.t[:, 0:NKP * 8].rearrange("p (k h) -> p k h", k=NKP)
            for q4 in range((NKP + 7) // 8):
                k1 = min(NKP, q4 * 8 + 8)
                c.dma("sync", V(cl3[:, q4 * 8:k1, :], [clt.buf]),
                      clf[l, b, q4 * 1024:k1 * 128, :].rearrange("(k p) h -> p k h", p=128))
            mm(bank[6][:, 0:NKP * 8], tri_f[:], clt[:, 0:NKP * 8], True, True)
            mm(bank[7][:, 0:NKP * 8], ones_f[:], clt[:, 0:NKP * 8], True, True)
            memset(ptot[:, 0, :], 0.0)
            tot3 = bank[7].t[:, 0:NKP * 8].rearrange("p (k h) -> p k h", k=NKP)
            for k in range(NKP):
                tt(ptot[:, k + 1, :], ptot[:, k, :], V(tot3[:, k, :], [bank[7].buf]), ALU.add)
            cin3 = V(bank[6].t[:, 0:NKP * 8].rearrange("p (k h) -> p k h", k=NKP), [bank[6].buf])
            stt(nF[:, 0:NKP, :], cin3, -1.0, ptot[:, 0:NKP, :], ALU.mult, ALU.subtract)
            cp(R[:, :], ptot[:, NKP, :])
        elif tl["first"]:
            memset(R[:, :], 0.0)
        mm(bank[6][0:L, 256:272], tri_f[0:L, 0:L], lsg[0:L, t, :], True, True)
        mm(bank[7][:, 256:272], ones_f[0:L, :], lsg[0:L, t, :], True, True)
        stt(nF[0:L, kb, :], bank[6][0:L, 256:264], -1.0, R[0:L, :], ALU.mult, ALU.subtract)
        tt(R[:, :], R[:, :], bank[7][:, 256:264], ALU.add)
        nkb = kb + 1
        tt(biasq[:, 0:nkb, :], nF[:, 0:nkb, :], bc(R[:, :], [128, nkb, 8], 1), ALU.add)
        if FS < 1:
            return
        if tl["kind"] == "p" and not tl["last"]:
            c.dma("sync", V(kts[l, kb].rearrange("p (c n) -> p c n", c=4), [kts_buf[l][kb]]), kTb[:, :, t * L:(t + 1) * L])
            c.dma("sync", V(vsc[l, kb], [vsc_buf[l][kb]]), V(vaug.t[:, t, :, :].rearrange("p h e -> p (h e)"), [vaug.buf]))
        if FS < 2:
            return
        oacc = [bank[0], bank[1]]

        items = []

        def load_group(g):
            k0 = g * 4
            nk = min(4, kb - k0)
            kt = ktg[g % 2]
            vgt = vg[g % 2]
            ktv = kt.t[:, :].rearrange("p (k c n) -> p k c n", k=4, c=4)
            vgv = vgt.t[:, :].rearrange("p (k h e) -> p k h e", k=4, h=8)
            if tl["kind"] == "p":
                c.dma("sync", V(ktv[:, 0:nk], [kt.buf]),
                      V(kts[l, k0:k0 + nk].rearrange("k p (c n) -> p k c n", c=4), [kts_buf[l][k0 + i] for i in range(nk)]))
                c.dma("sync", V(vgt.t[:, 0:nk * 576].rearrange("p (k f) -> p k f", k=nk), [vgt.buf]),
                      V(vsc[l, k0:k0 + nk].rearrange("k p f -> p k f"), [vsc_buf[l][k0 + i] for i in range(nk)]))
            else:
                ksv = kst.t[:, :].rearrange("p (k n) -> p k n", k=4)
                c.dma("gpsimd", V(ksv, [kst.buf]), ck[l, b, k0 * 128:(k0 + 4) * 128, :].rearrange("(k p) n -> p k n", p=128))
                c.dma("gpsimd", V(vst.t[:, :].rearrange("p (k n) -> p k n", k=4), [vst.buf]),
                      cv[l, b, k0 * 128:(k0 + 4) * 128, :].rearrange("(k p) n -> p k n", p=128))
                for k in range(4):
                    pb_i = 6 + (k % 2)
                    pb = bkb(pb_i)
                    for pr in range(4):
                        tr(V(pb[:, pr * 128:(pr + 1) * 128], [bank[pb_i].buf]), V(ksv[:, k, pr * 128:(pr + 1) * 128], [kst.buf]),
                           ident_b[:, :], signal=(pr == 3))
                    cp(V(ktv[:, k, :, :], [kt.buf]), V(pb[:, 0:512].rearrange("p (c n) -> p c n", c=4), [bank[pb_i].buf]), eng="scalar")
                cp(V(vgv[:, :, :, 0:64], [vgt.buf]), V(vst.t[:, :].rearrange("p (k h e) -> p k h e", k=4, h=8), [vst.buf]))
                memset(V(vgv[:, :, :, 64:65], [vgt.buf]), 1.0)
            for k in range(nk):
                items.append((ktv[:, k], [kt.buf], vgv[:, k], [vgt.buf], 128, k0 + k, False))

        ngrp = (kb + 3) // 4
        state = {"first": True}

        def qk_exp(it, slot):
            kview, kbuf, vview, vbuf, Lk, kbi, diag = it
            for h in range(8):
                p, half = h // 2, h % 2
                sb_ = bank[2 + 2 * slot + half]
                mm(sb_[0:Lk, p * L:(p + 1) * L], V(kview[64 * half:64 * half + 64, p, 0:Lk], kbuf),
                   qT[64 * half:64 * half + 64, p, t * L:(t + 1) * L], True, True)
            if QS < 1:
                return
            for h in range(8):
                p, half = h // 2, h % 2
                sb_ = bank[2 + 2 * slot + half]
                pv = pT[2 * slot + half][0:Lk, p, 0:L]
                act(pv, sb_[0:Lk, p * L:(p + 1) * L], AF.Exp, bias=biasq[0:Lk, kbi, h:h + 1])
            if QS < 2:
                return
            if diag:
                for hb in range(2):
                    pvv = pT[2 * slot + hb][0:Lk, :, 0:L]
                    tt(pvv, pvv, bc(tri_b[0:Lk, 0:L], [Lk, 4, L], 1), ALU.mult)

        def pv_mm(it, slot, last):
            kview, kbuf, vview, vbuf, Lk, kbi, diag = it
            if QS < 3:
                return
            for h in range(8):
                ob = oacc[h // 4]
                pv = pT[2 * slot + h % 2][0:Lk, h // 2, 0:L]
                mm(ob[0:65, (h % 4) * L:(h % 4 + 1) * L], V(vview[0:Lk, h, 0:65], vbuf), pv,
                   start=(state["first"] and h % 4 == 0), stop=last, signal=(last and h % 4 == 3))
            state["first"] = False

        pipe = {"prev": None, "idx": 0}

        def push(it):
            slot = pipe["idx"] % 2
            qk_exp(it, slot)
            if pipe["prev"] is not None:
                pv_mm(pipe["prev"][0], pipe["prev"][1], False)
            pipe["prev"] = (it, slot)
            pipe["idx"] += 1

        for g in range(ngrp if FS >= 4 else 0):
            del items[:]
            load_group(g)
            for it in list(items):
                push(it)
        push((kTb.t[:, :, t * L:(t + 1) * L], [kTb.buf], vaug.t[:, t], [vaug.buf], L, kb, True))
        pv_mm(pipe["prev"][0], pipe["prev"][1], True)
        if FS < 3:
            c.op("vector", lambda e: e.memset(junk.t[:, 0:4], 0.0), reads=[oacc[0][:], oacc[1][:]], writes=[junk[:]])
            return
        for hb in range(2):
            ob = oacc[hb]
            c.op("vector", lambda e, ob=ob: e.reciprocal(out=rsum.t[64:65, 0:4 * L], in_=ob.t[64:65, 0:4 * L]), reads=[ob[:]], writes=[rsum[:]])
            mm(bank[6 + hb][0:64, 0:4 * L], ones_f[64:65, 0:64], rsum[64:65, 0:4 * L], True, True)
            cp(bcs[0:64, 0:4 * L], bank[6 + hb][0:64, 0:4 * L], eng="scalar")
            tt(yfoxT[0:64, hb * 4:(hb + 1) * 4, t * L:(t + 1) * L],
               V(ob.t[0:64, 0:4 * L].rearrange("p (h j) -> p h j", h=4), [ob.buf]),
               V(bcs.t[0:64, 0:4 * L].rearrange("p (h j) -> p h j", h=4), [bcs.buf]), ALU.mult)

    def ml_tile(l, t, tl, L):
        b = tl["b"]
        Cs = Cst[l]
        mr = mrep[l]
        if tl["kind"] == "s":
            c.dma("sync", Cs[:, :, 0:128], smc[l, b].rearrange("h d e -> d h e"))
            c.dma("sync", Cs[:, :, 128], smn[l, b].rearrange("h d -> d h"))
            c.dma("sync", mr[:, :], smm[l, b:b + 1, :].partition_broadcast(128))
        elif tl["first"]:
            memset(Cs[:, :, :], 0.0)
            memset(mr[:, :], 0.0)
        mm(bank[6][0:L, 256:272], tri_f[0:L, 0:L], lsg[0:L, t, :], True, True)
        mm(bank[7][:, 256:272], ones_f[0:L, :], lsg[0:L, t, :], True, True)
        bcol = sm[0]
        cp(bcol[0:L, 0:4], bank[6][0:L, 268:272])
        bend = sm[1]
        cp(bend[:, 0:4], bank[7][:, 268:272])
        acol = sm[2]
        tt(acol[0:L, 0:4], zz[0:L, t, 8:12], bcol[0:L, 0:4], ALU.subtract)
        mm(bank[6][0:4, 0:L], acol[0:L, 0:4], ident_f[0:L, 0:L], True, True)
        amx = sm[3]
        c.op("vector", lambda e: e.reduce_max(out=amx.t[0:4, 0:1], in_=bank[6].t[0:4, 0:L], axis=AX.X), reads=[bank[6][:]], writes=[amx[:]])
        ts(dg[:, :], ident_f[0:4, 0:4], amx[0:4, 0:1], None, ALU.mult)
        mm(bank[7][:, 0:4], ones_f[0:4, :], dg[:, :], True, True)
        Rr = sm[4]
        tt(Rr[:, 0:4], bank[7][:, 0:4], mr[:, :], ALU.max)
        ecol = sm[5]
        tt(ecol[0:L, 0:4], acol[0:L, 0:4], Rr[0:L, 0:4], ALU.subtract)
        act(ecol[0:L, 0:4], ecol[0:L, 0:4], AF.Exp)
        sc = sm[3]
        tt(sc[:, 4:8], mr[:, :], Rr[:, 0:4], ALU.subtract)
        act(sc[:, 4:8], sc[:, 4:8], AF.Exp)
        thr = sm[2]
        tt(thr[0:L, 4:8], bcol[0:L, 0:4], Rr[0:L, 0:4], ALU.add)
        act(thr[0:L, 4:8], thr[0:L, 4:8], AF.Exp, scale=-1.0)
        tt(mr[:, :], bend[:, 0:4], Rr[:, 0:4], ALU.add)
        tt(vpa[0:L, :, 0:128], V(vml.t[0:L, t, :].rearrange("p (h e) -> p h e", h=4), [vml.buf]), bc(ecol[0:L, 0:4], [L, 4, 128], 2), ALU.mult)
        cp(vpa[0:L, :, 128], ecol[0:L, 0:4])
        pb = bkb(5)
        for h in range(4):
            tr(V(pb[0:L, h * 128:(h + 1) * 128], [bank[5].buf]), kmT[:, h, t * L:(t + 1) * L], ident_b[:, :], signal=(h == 3))
        cp(ktok[0:L, :, :], V(pb[0:L, 0:512].rearrange("p (h d) -> p h d", h=4), [bank[5].buf]), eng="scalar")
        for h in range(4):
            mm(bank[4][0:L, h * L:(h + 1) * L], kmT[:, h, t * L:(t + 1) * L], qmT[:, h, t * L:(t + 1) * L], True, True, signal=(h == 3))
        tt(sTm[0:L, :, 0:L], V(bank[4].t[0:L, 0:4 * L].rearrange("p (h j) -> p h j", h=4), [bank[4].buf]), bc(tri_f[0:L, 0:L], [L, 4, L], 1), ALU.mult)
        tt(Cbf[:, :, 0:129], Cs[:, :, 0:129], bc(sc[:, 4:8], [128, 4, 129], 2), ALU.mult)
        for h in range(4):
            nb = bank[h // 2]
            o0 = (h % 2) * 192
            mm(nb[0:L, o0:o0 + 129], sTm[0:L, h, 0:L], vpa[0:L, h, 0:129], True, False, signal=False)
            mm(nb[0:L, o0:o0 + 129], qmT[:, h, t * L:(t + 1) * L], Cbf[:, h, 0:129], False, True, signal=True)
            db = bank[2 + h // 2]
            mm(db[:, o0:o0 + 129], ktok[0:L, h, :], vpa[0:L, h, 0:129], True, True)
        for h in range(4):
            db = bank[2 + h // 2]
            o0 = (h % 2) * 192
            stt(Cs[:, h, 0:129], Cs[:, h, 0:129], sc[:, 4 + h:5 + h], db[:, o0:o0 + 129], ALU.mult, ALU.add)
        dn = sm[5]
        for h in range(4):
            nb = bank[h // 2]
            o0 = (h % 2) * 192
            cp(dn[0:L, 8 + h:9 + h], nb[0:L, o0 + 128:o0 + 129])
        stt(dn[0:L, 12:16], dn[0:L, 8:12], -1.0, dn[0:L, 8:12], ALU.mult, ALU.max)
        tt(dn[0:L, 12:16], dn[0:L, 12:16], thr[0:L, 4:8], ALU.max)
        c.op("vector", lambda e: e.reciprocal(out=dn.t[0:L, 12:16], in_=dn.t[0:L, 12:16]), reads=[dn[:]], writes=[dn[:]])
        ssq = sm[1]
        for h in range(4):
            nb = bank[h // 2]
            o0 = (h % 2) * 192
            stt(hh[0:L, h * 128:(h + 1) * 128], nb[0:L, o0:o0 + 128], dn[0:L, 12 + h:13 + h], omt[0:L, t, h * 128:(h + 1) * 128], ALU.mult, ALU.mult)
            act(junk[0:L, 0:128], hh[0:L, h * 128:(h + 1) * 128], AF.Square, accum=ssq[0:L, 8 + h:9 + h])
        ts(ssq[0:L, 8:12], ssq[0:L, 8:12], 1.0 / 128, EPS, ALU.mult, ALU.add)
        act(ssq[0:L, 8:12], ssq[0:L, 8:12], AF.Ln)
        act(ssq[0:L, 8:12], ssq[0:L, 8:12], AF.Exp, scale=-0.5)
        tt(V(ymt.t[0:L, :].rearrange("p (h e) -> p h e", h=4), [ymt.buf]), V(hh.t[0:L, :].rearrange("p (h e) -> p h e", h=4), [hh.buf]),
           bc(ssq[0:L, 8:12], [L, 4, 128], 2), ALU.mult)
        pb = bkb(5)
        for h in range(4):
            tr(V(pb[:, h * L:(h + 1) * L], [bank[5].buf]), ymt[0:L, h * 128:(h + 1) * 128], ident_b[0:L, 0:L], signal=(h == 3))
        tt(ymlT[:, :, t * L:(t + 1) * L], V(pb[:, 0:4 * L].rearrange("p (h j) -> p h j", h=4), [bank[5].buf]), bc(gml[:, l, :], [128, 4, L], 2), ALU.mult)
        if tl["kind"] == "s" or tl["last"]:
            mc_o, mn_o, mm_o = (mc_p, mn_p, mm_p) if tl["kind"] == "p" else (mc_s, mn_s, mm_s)
            c.dma("sync", mc_o[l, b].rearrange("h d e -> d h e"), Cs[:, :, 0:128])
            c.dma("sync", mn_o[l, b].rearrange("h d -> d h"), Cs[:, :, 128])
            c.dma("sync", mm_o[l, b:b + 1, :], mr[0:1, :])

    blocks = []
    for s in range(NSEQ):
        for bi in range(NBLK):
            tiles = []
            for t in range(4):
                pos = bi * 512 + t * 128
                tiles.append(dict(kind="p", L=128, b=s, pos=pos, kb=pos // 128, first=(pos == 0), last=(pos == SP - 128)))
            blocks.append(tiles)
    if DO_SAMPLE:
        blocks.append([dict(kind="s", L=16, b=bb, pos=PAST, kb=NKP, first=True, last=True) for bb in range(4)])

    for bi_, tiles in enumerate(blocks):
        for l in range(NL):
            plan_layer(l)
    for tiles in blocks:
        L = tiles[0]["L"]
        for t, tl in enumerate(tiles):
            src = xp[tl["b"], tl["pos"]:tl["pos"] + L, :] if tl["kind"] == "p" else xs[tl["b"]]
            c.dma("sync", X[0:L, t, :], src)
        for l in range(NL):
            layer_block(l, tiles)
        for t, tl in enumerate(tiles):
            ss = st_col[0:L, 0:1]
            act(junk[0:L, :], X[0:L, t, :], AF.Square, accum=ss)
            rstd_from_ss(ss, D, L, None)
            yo = yout[t % 2]
            stt(yo[0:L, :], X[0:L, t, :], ss, nfin[0:L, :], ALU.mult, ALU.mult)
            dst = y_p[tl["b"], tl["pos"]:tl["pos"] + L, :] if tl["kind"] == "p" else y_s[tl["b"]]
            c.dma("sync", dst, yo[0:L, :])
    c.finish()
    cx.close()
    return nc, c


_CONST = None


def _consts():
    ident = np.eye(128, dtype=np.float32)
    tri = np.triu(np.ones((128, 128), dtype=np.float32))
    return ident, tri


def make_in_maps(inputs):
    ident, tri = _consts()
    f = lambda a: np.ascontiguousarray(np.asarray(a, dtype=np.float32))
    shared = dict(
        norm_mix=f(inputs["norm_mix"]), w_in=f(inputs["w_in"]), b_gate=f(inputs["b_gate"]),
        sg_ln_g=f(inputs["sg_ln_g"]), sg_ln_b=f(inputs["sg_ln_b"]), sg_w=f(inputs["sg_w"]), sg_b=f(inputs["sg_b"]),
        b_fox_f=f(inputs["b_fox_f"]), ml_conv_w=f(inputs["ml_conv_w"]), ml_conv_b=f(inputs["ml_conv_b"]),
        b_ml_i=f(inputs["b_ml_i"]), b_ml_f=f(inputs["b_ml_f"]), ml_norm_g=f(inputs["ml_norm_g"]),
        w_br=f(inputs["w_br"]), w_out=f(inputs["w_out"]), norm_ffn=f(inputs["norm_ffn"]),
        w_ffn_in=f(inputs["w_ffn_in"]), w_ffn_out=f(inputs["w_ffn_out"]),
        norm_final=f(inputs["norm_final"]).reshape(1, D), c_ident=ident, c_tri=tri)
    maps = []
    for i in range(8):
        m = dict(shared)
        m["xp"] = f(inputs["x_prompt"][2 * i:2 * i + 2])
        m["xs"] = f(inputs["x_sample"][4 * i:4 * i + 4])
        m["ck"] = f(np.asarray(inputs["cache_fox_k"])[:, 4 * i:4 * i + 4].reshape(2, 4, 4096, 512))
        m["cv"] = f(np.asarray(inputs["cache_fox_v"])[:, 4 * i:4 * i + 4].reshape(2, 4, 4096, 512))
        m["clf"] = f(np.asarray(inputs["cache_fox_logf"])[:, 4 * i:4 * i + 4])
        m["smc"] = f(np.asarray(inputs["state_ml_c"])[:, 4 * i:4 * i + 4])
        m["smn"] = f(np.asarray(inputs["state_ml_n"])[:, 4 * i:4 * i + 4])
        m["smm"] = f(np.asarray(inputs["state_ml_m"])[:, 4 * i:4 * i + 4])
        m["smconv"] = f(np.asarray(inputs["state_ml_conv"])[:, 4 * i:4 * i + 4])
        maps.append(m)
    return maps


def assemble(results):
    cat = lambda k, ax: np.concatenate([np.asarray(r[k]) for r in results], axis=ax)
    y_p = cat("y_p", 0); y_s = cat("y_s", 0)
    fk_p = cat("fk_p", 1).reshape(2, 16, 4096, 8, 64); fv_p = cat("fv_p", 1).reshape(2, 16, 4096, 8, 64)
    flf_p = cat("flf_p", 1); mc_p = cat("mc_p", 1); mn_p = cat("mn_p", 1); mm_p = cat("mm_p", 1); mconv_p = cat("mconv_p", 1)
    fk_s = cat("fk_s", 1).reshape(2, 32, 16, 8, 64); fv_s = cat("fv_s", 1).reshape(2, 32, 16, 8, 64)
    flf_s = cat("flf_s", 1); mc_s = cat("mc_s", 1); mn_s = cat("mn_s", 1); mm_s = cat("mm_s", 1)
    mconv_s = cat("mconv_s", 1); sgv_s = cat("sgv_s", 1)
    return (y_p, y_s, fk_p, fv_p, flf_p, mc_p, mn_p, mm_p, mconv_p,
            fk_s, fv_s, flf_s, mc_s, mn_s, mm_s, mconv_s, sgv_s)


_NC = {}


def kernel(**inputs):
    cfg = inputs.pop("_cfg", {})
    key = tuple(sorted(cfg.items()))
    if key not in _NC:
        _NC[key] = build(cfg)[0]
    nc = _NC[key]
    maps = make_in_maps(inputs)
    res = run_bass_kernel_spmd(nc, maps, core_ids=list(range(8)))
    return assemble(res.results)


def kernel_debug(cfg, maps):
    nc = build(cfg)[0]
    res = run_bass_kernel_spmd(nc, maps, core_ids=list(range(len(maps))))
    return res.results
```

```python
import contextlib
import numpy as np
import concourse.bass as bass
import concourse.mybir as mybir
from concourse.bass_utils import run_bass_kernel_spmd

F32 = mybir.dt.float32
BF16 = mybir.dt.bfloat16
AF = mybir.ActivationFunctionType
ALU = mybir.AluOpType
AX = mybir.AxisListType

D = 1024
NIN = 7696
DFF = 2816
EPS = 1e-6
C_GATE, C_SGU, C_SGV, C_FQ, C_FK, C_FV, C_FF = 0, 3072, 3584, 4096, 4608, 5120, 5632
C_MQK, C_MV, C_MI, C_MF, C_MO = 5640, 6664, 7176, 7180, 7184
SLAB = 4096


class Buf:
    __slots__ = ("name", "w", "r", "excl")

    def __init__(self, name, excl=False):
        self.name = name
        self.w = None
        self.r = {}
        self.excl = excl


class V:
    __slots__ = ("ap", "bufs")

    def __init__(self, ap, bufs):
        self.ap = ap
        self.bufs = bufs


class T:
    def __init__(self, t, name):
        self.t = t
        self.buf = Buf(name)

    def __getitem__(self, idx):
        return V(self.t[idx], [self.buf])

    def v(self, ap):
        return V(ap, [self.buf])


class Queue:
    def __init__(self, name, sem):
        self.name = name
        self.sem = sem
        self.cnt = 0
        self.ops = []
        self.seen = {}
        self.pending = False


class Ctx:
    def __init__(self, nc, n_dma_sems=12):
        self.nc = nc
        self.es = contextlib.ExitStack()
        self.q = {}
        for nm in ("sync", "scalar", "vector", "gpsimd", "tensor"):
            self.q[nm] = Queue(nm, self.es.enter_context(nc.semaphore("s_" + nm)))
        self.dq = {}
        for nm in ("sync", "gpsimd", "scalar"):
            self.dq[nm] = [Queue("dma_%s%d" % (nm, i), self.es.enter_context(nc.semaphore("s_d%s%d" % (nm, i))))
                           for i in range(n_dma_sems)]
        self.dq_next = {"sync": 0, "gpsimd": 0, "scalar": 0}
        self.n_inst = 0

    def sbuf(self, name, shape, dtype):
        return T(self.es.enter_context(self.nc.sbuf_tensor(name, list(shape), dtype)), name)

    def psum(self, name, shape, dtype):
        t = T(self.es.enter_context(self.nc.psum_tensor(name, list(shape), dtype)), name)
        t.buf.excl = True
        return t

    def _need(self, q, waits, dep, raw):
        if dep is None:
            return
        pq, c = dep
        if pq is q and ((not raw) or q.name == "tensor"):
            return
        if q.seen.get(pq, 0) >= c:
            return
        if waits.get(pq, 0) < c:
            waits[pq] = c

    def _deps(self, q, rb, wb):
        waits = {}
        for b in rb:
            self._need(q, waits, b.w, True)
            if b.excl:
                for rq, c in b.r.items():
                    if rq is not q:
                        self._need(q, waits, (rq, c), False)
        for b in wb:
            self._need(q, waits, b.w, False)
            for rq, c in b.r.items():
                self._need(q, waits, (rq, c), False)
        return waits

    def _emit_waits(self, q, waits):
        for pq, c in waits.items():
            q.seen[pq] = c
            q.ops.append(lambda e, sem=pq.sem, c=c: e.wait_ge(sem, c))

    def op(self, eng, fn, reads=(), writes=(), signal=True):
        q = self.q[eng]
        rb = [b for v in reads for b in v.bufs]
        wb = [b for v in writes for b in v.bufs]
        self._emit_waits(q, self._deps(q, rb, wb))
        self.n_inst += 1
        if signal:
            q.cnt += 1
            c = q.cnt
            q.ops.append(lambda e, fn=fn, sem=q.sem: fn(e).then_inc(sem, 1))
            q.pending = False
        else:
            c = q.cnt + 1
            q.ops.append(lambda e, fn=fn: fn(e))
            q.pending = True
        for b in rb:
            if b.r.get(q, 0) < c:
                b.r[q] = c
        for b in wb:
            b.w = (q, c)
            b.r = {}

    def dma(self, eng, out, in_, **kw):
        q = self.q[eng]
        pool = self.dq[eng]
        dq = pool[self.dq_next[eng]]
        self.dq_next[eng] = (self.dq_next[eng] + 1) % len(pool)
        rb = list(in_.bufs) if isinstance(in_, V) else []
        wb = list(out.bufs) if isinstance(out, V) else []
        waits = self._deps(q, rb, wb)
        if dq.cnt > 0 and q.seen.get(dq, 0) < dq.cnt and waits.get(dq, 0) < dq.cnt:
            waits[dq] = dq.cnt
        self._emit_waits(q, waits)
        dq.cnt += 16
        c = dq.cnt
        oap = out.ap if isinstance(out, V) else out
        iap = in_.ap if isinstance(in_, V) else in_
        q.ops.append(lambda e, oap=oap, iap=iap, sem=dq.sem, kw=kw: e.dma_start(out=oap, in_=iap, **kw).then_inc(sem, 16))
        self.n_inst += 1
        for b in rb:
            if b.r.get(dq, 0) < c:
                b.r[dq] = c
        for b in wb:
            b.w = (dq, c)
            b.r = {}

    def barrier(self):
        allq = list(self.q.values())
        for q in allq:
            assert not q.pending, q.name
        alld = [d for pool in self.dq.values() for d in pool]
        for q in allq:
            waits = {}
            for pq in allq + alld:
                if pq is not q and pq.cnt > 0 and q.seen.get(pq, 0) < pq.cnt:
                    waits[pq] = pq.cnt
            self._emit_waits(q, waits)

    def finish(self):
        self.barrier()
        nc = self.nc
        with nc.Block() as block:
            @block.sync
            def _(e):
                for f in self.q["sync"].ops:
                    f(e)

            @block.scalar
            def _(e):
                for f in self.q["scalar"].ops:
                    f(e)

            @block.vector
            def _(e):
                for f in self.q["vector"].ops:
                    f(e)

            @block.gpsimd
            def _(e):
                for f in self.q["gpsimd"].ops:
                    f(e)

            @block.tensor
            def _(e):
                for f in self.q["tensor"].ops:
                    f(e)
        self.es.close()


def build(cfg):
    NSEQ = cfg.get("nseq", 2)
    NBLK = cfg.get("nblk", cfg.get("SP", 4096) // 512)
    DO_SAMPLE = cfg.get("sample", True)
    NL = cfg.get("nl", 2)
    STAGE = cfg.get("stage", 99)
    FS = cfg.get("fs", 99)
    QS = cfg.get("qs", 99)
    ABL = cfg.get("abl", "")
    SP = cfg.get("SP", 4096)
    PAST = cfg.get("PAST", 4096)
    NKP = PAST // 128

    nc = bass.Bass("TRN2", target_bir_lowering=False)
    cx = contextlib.ExitStack()
    cx.enter_context(nc.allow_non_contiguous_dma(reason="small strided parameter / state transfers"))
    cx.enter_context(nc.allow_low_precision(reason="bf16 matmul operands, fp32 accumulation"))

    def din(name, shape, dt=F32):
        return nc.dram_tensor(name, list(shape), dt, kind="ExternalInput").ap()

    def dout(name, shape, dt=F32):
        return nc.dram_tensor(name, list(shape), dt, kind="ExternalOutput").ap()

    xp = din("xp", [2, SP, D]); xs = din("xs", [4, 16, D])
    ck = din("ck", [2, 4, PAST, 512]); cv = din("cv", [2, 4, PAST, 512]); clf = din("clf", [2, 4, PAST, 8])
    smc = din("smc", [2, 4, 4, 128, 128]); smn = din("smn", [2, 4, 4, 128]); smm = din("smm", [2, 4, 4])
    smconv = din("smconv", [2, 4, 3, D])
    norm_mix = din("norm_mix", [2, D]); w_in = din("w_in", [2, D, NIN]); b_gate = din("b_gate", [2, 3, D])
    sg_ln_g = din("sg_ln_g", [2, 512]); sg_ln_b = din("sg_ln_b", [2, 512]); sg_w = din("sg_w", [2, 4, 128, 128])
    sg_b = din("sg_b", [2, 4, 128]); b_fox_f = din("b_fox_f", [2, 8]); ml_conv_w = din("ml_conv_w", [2, 4, D])
    ml_conv_b = din("ml_conv_b", [2, D]); b_ml_i = din("b_ml_i", [2, 4]); b_ml_f = din("b_ml_f", [2, 4])
    ml_norm_g = din("ml_norm_g", [2, 4, 128]); w_br = din("w_br", [2, 3, 512, D]); w_out = din("w_out", [2, D, D])
    norm_ffn = din("norm_ffn", [2, D]); w_ffn_in = din("w_ffn_in", [2, D, 2 * DFF]); w_ffn_out = din("w_ffn_out", [2, DFF, D])
    norm_final = din("norm_final", [1, D])
    c_ident = din("c_ident", [128, 128]); c_tri = din("c_tri", [128, 128])

    y_p = dout("y_p", [2, SP, D]); y_s = dout("y_s", [4, 16, D])
    fk_p = dout("fk_p", [2, 2, SP, 512]); fv_p = dout("fv_p", [2, 2, SP, 512]); flf_p = dout("flf_p", [2, 2, SP, 8])
    mc_p = dout("mc_p", [2, 2, 4, 128, 128]); mn_p = dout("mn_p", [2, 2, 4, 128]); mm_p = dout("mm_p", [2, 2, 4])
    mconv_p = dout("mconv_p", [2, 2, 3, D])
    fk_s = dout("fk_s", [2, 4, 16, 512]); fv_s = dout("fv_s", [2, 4, 16, 512]); flf_s = dout("flf_s", [2, 4, 16, 8])
    mc_s = dout("mc_s", [2, 4, 4, 128, 128]); mn_s = dout("mn_s", [2, 4, 4, 128]); mm_s = dout("mm_s", [2, 4, 4])
    mconv_s = dout("mconv_s", [2, 4, 3, D]); sgv_s = dout("sgv_s", [2, 4, 16, 512])

    DBG = cfg.get("dbg", False)
    if DBG:
        dbg_fox = dout("dbg_fox", [64, 8, 512], BF16); dbg_sg = dout("dbg_sg", [128, 4, 512], BF16); dbg_ml = dout("dbg_ml", [128, 4, 512], BF16)
    kts = nc.dram_tensor("kts", [2, 32, 128, 512], BF16, kind="Internal").ap()
    vsc = nc.dram_tensor("vsc", [2, 32, 128, 576], BF16, kind="Internal").ap()
    kts_buf = [[Buf("kts%d_%d" % (l, k)) for k in range(32)] for l in range(2)]
    vsc_buf = [[Buf("vsc%d_%d" % (l, k)) for k in range(32)] for l in range(2)]

    c = Ctx(nc)
    sb = c.sbuf

    def rd(*xs_):
        return [x for x in xs_ if isinstance(x, V)]

    def A(x):
        return x.ap if isinstance(x, V) else x

    def mm(out, lhsT, rhs, start, stop, signal=None):
        c.op("tensor", lambda e: e.matmul(out.ap, lhsT=lhsT.ap, rhs=rhs.ap, start=start, stop=stop, skip_group_check=True),
             reads=[lhsT, rhs], writes=[out], signal=(stop if signal is None else signal))

    def tr(out, in_, ident, signal=True):
        c.op("tensor", lambda e: e.transpose(out.ap, in_.ap, ident.ap), reads=[in_, ident], writes=[out], signal=signal)

    def act(out, in_, func, bias=None, scale=1.0, accum=None, eng="scalar"):
        kw = {}
        if bias is not None:
            kw["bias"] = A(bias)
        if accum is not None:
            kw["accum_out"] = accum.ap
        c.op("scalar", lambda e: e.activation(out=out.ap, in_=in_.ap, func=func, scale=A(scale), **kw),
             reads=rd(in_, bias, scale), writes=[out] + ([accum] if accum is not None else []))

    def tt(out, in0, in1, op, eng="vector"):
        c.op(eng, lambda e: e.tensor_tensor(out=out.ap, in0=in0.ap, in1=in1.ap, op=op), reads=[in0, in1], writes=[out])

    def ts(out, in0, s1, s2, op0, op1=None, eng="vector"):
        if op1 is None:
            c.op(eng, lambda e: e.tensor_single_scalar(out=out.ap, in_=in0.ap, scalar=A(s1), op=op0), reads=rd(in0, s1), writes=[out])
        else:
            c.op(eng, lambda e: e.tensor_scalar(out=out.ap, in0=in0.ap, scalar1=A(s1), scalar2=A(s2), op0=op0, op1=op1),
                 reads=rd(in0, s1, s2), writes=[out])

    def stt(out, in0, scalar, in1, op0, op1, eng="vector"):
        c.op(eng, lambda e: e.scalar_tensor_tensor(out=out.ap, in0=in0.ap, scalar=A(scalar), in1=in1.ap, op0=op0, op1=op1),
             reads=rd(in0, scalar, in1), writes=[out])

    def cp(out, in_, eng="vector"):
        if eng == "scalar":
            c.op("scalar", lambda e: e.copy(out=out.ap, in_=in_.ap), reads=[in_], writes=[out])
        else:
            c.op(eng, lambda e: e.tensor_copy(out=out.ap, in_=in_.ap), reads=[in_], writes=[out])

    def memset(out, val, eng="vector"):
        c.op(eng, lambda e: e.memset(out.ap, val), writes=[out])

    def bc(v, shape, axis):
        return V(v.ap.unsqueeze(axis).to_broadcast(list(shape)), v.bufs)

    ident_f = sb("ident_f", [128, 128], F32); ident_b = sb("ident_b", [128, 128], BF16)
    tri_f = sb("tri_f", [128, 128], F32); tri_b = sb("tri_b", [128, 128], BF16)
    ones_f = sb("ones_f", [128, 128], F32)
    c.dma("sync", ident_f[:], c_ident); c.dma("sync", tri_f[:], c_tri)
    cp(ident_b[:], ident_f[:]); cp(tri_b[:], tri_f[:]); memset(ones_f[:], 1.0)

    gmix = sb("gmix", [128, 2, 8], F32); gffn = sb("gffn", [128, 2, 8], F32)
    bgate = sb("bgate", [128, 2, 3, 8], F32)
    wconv = sb("wconv", [128, 2, 4, 8], F32); bconv = sb("bconv", [128, 2, 8], F32)
    gml = sb("gml", [128, 2, 4], F32)
    lng = sb("lng", [128, 2, 512], F32); lnb = sb("lnb", [128, 2, 512], F32)
    bsbc = sb("bsbc", [128, 2, 512], F32)
    wsT = sb("wsT", [128, 2, 4, 128], BF16)
    bsm = sb("bsm", [128, 2, 16], F32)
    wsm = sb("wsm", [128, 2, 8, 16], BF16)
    nfin = sb("nfin", [128, D], F32)
    for l in range(2):
        c.dma("sync", gmix[:, l, :], norm_mix[l].rearrange("(c p) -> p c", p=128))
        c.dma("sync", gffn[:, l, :], norm_ffn[l].rearrange("(c p) -> p c", p=128))
        for b in range(3):
            c.dma("sync", bgate[:, l, b, :], b_gate[l, b].rearrange("(c p) -> p c", p=128))
        for j in range(4):
            c.dma("sync", wconv[:, l, j, :], ml_conv_w[l, j].rearrange("(c p) -> p c", p=128))
        c.dma("sync", bconv[:, l, :], ml_conv_b[l].rearrange("(c p) -> p c", p=128))
        c.dma("sync", gml[:, l, :], ml_norm_g[l].rearrange("h p -> p h"))
        c.dma("sync", lng[:, l, :], sg_ln_g[l:l + 1, :].partition_broadcast(128))
        c.dma("sync", lnb[:, l, :], sg_ln_b[l:l + 1, :].partition_broadcast(128))
        c.dma("sync", bsbc[:, l, :], sg_b.rearrange("l g t -> l (g t)")[l:l + 1, :].partition_broadcast(128))
        c.dma("sync", bsm[:, l, 0:8], b_fox_f[l:l + 1, :].partition_broadcast(128))
        c.dma("sync", bsm[:, l, 8:12], b_ml_i[l:l + 1, :].partition_broadcast(128))
        c.dma("sync", bsm[:, l, 12:16], b_ml_f[l:l + 1, :].partition_broadcast(128))
        wv = w_in[l].rearrange("(kc p) n -> p kc n", p=128)
        c.dma("gpsimd", wsm[:, l, :, 0:8], wv[:, :, C_FF:C_FF + 8])
        c.dma("gpsimd", wsm[:, l, :, 8:16], wv[:, :, C_MI:C_MI + 8])
    c.dma("sync", nfin[:], norm_final.partition_broadcast(128))

    bank = [c.psum("bank%d" % i, [128, 512], F32) for i in range(8)]

    def bkb(i):
        return bank[i].t[:, :].bitcast(BF16)

    stg = [sb("stg%d" % i, [128, D], F32) for i in range(2)]
    for l in range(2):
        wtv = V(stg[0].t[:, 0:512].rearrange("p (g s) -> p g s", g=4), [stg[0].buf])
        c.dma("sync", wtv, sg_w[l].rearrange("g t s -> t g s"))
        for g in range(4):
            mm(bank[g][:, 0:128], stg[0][:, g * 128:(g + 1) * 128], ident_f[:], True, True)
            tt(wsT[:, l, g, :], bank[g][:, 0:128], tri_f[:], ALU.mult)

    X = sb("X", [128, 4, D], F32)
    xnT = sb("xnT", [128, 8, 512], BF16)
    slabs = [sb("slab%d" % i, [128, SLAB], BF16) for i in range(3)]
    xs_bs = [sb("xs_b%d" % i, [128, D], BF16) for i in range(2)]
    st_col = sb("st_col", [128, 8], F32)
    uT = sb("uT", [128, 4, 512], BF16); vb = sb("vb", [128, 4, 512], BF16); ysgT = sb("ysgT", [128, 4, 512], BF16)
    sga = sb("sga", [128, 512], F32); sgt = sb("sgt", [128, 512], F32)
    qT = sb("qT", [128, 4, 512], BF16); kTb = sb("kTb", [128, 4, 512], BF16)
    vaug = sb("vaug", [128, 4, 8, 72], BF16)
    yfoxT = sb("yfoxT", [64, 8, 512], BF16)
    zz = sb("zz", [128, 4, 16], F32); lsg = sb("lsg", [128, 4, 16], F32)
    zt = [sb("zt%d" % i, [128, 16], F32) for i in range(3)]
    negF = [sb("negF%d" % l, [128, 33, 8], F32) for l in range(2)]
    Rrun = [sb("Rrun%d" % l, [128, 8], F32) for l in range(2)]
    biasqs = [sb("biasq%d" % i, [128, 33, 8], F32) for i in range(4)]
    arena = sb("arena", [128, 12800], BF16)
    ar = arena.t
    def aview(off, n, name):
        t_ = T(ar[:, off:off + n], name)
        return t_
    ktg = [aview(0, 2048, "ktg0"), aview(2048, 2048, "ktg1")]
    vg = [aview(4096, 2304, "vg0"), aview(6400, 2304, "vg1")]
    kst = aview(8704, 2048, "kst"); vst = aview(10752, 2048, "vst")
    hT = aview(0, 11264, "hT")
    mixTt = aview(0, 4096, "mixT")
    pT = [sb("pT%d" % i, [128, 4, 128], BF16) for i in range(4)]
    vws = [sb("vw%d" % i, [128, 8, 72], BF16) for i in range(2)]
    ptot = sb("ptot", [128, 33, 8], F32)
    rawT = [sb("rawT%d" % i, [128, 4, 131], F32) for i in range(2)]
    qmT = sb("qmT", [128, 4, 512], BF16); kmT = sb("kmT", [128, 4, 512], BF16)
    vml = sb("vml", [128, 4, 512], BF16); omt = sb("omt", [128, 4, 512], BF16)
    vpa = sb("vpa", [128, 4, 136], BF16); ktok = sb("ktok", [128, 4, 128], BF16)
    sTm = sb("sTm", [128, 4, 128], BF16); Cbf = sb("Cbf", [128, 4, 136], BF16)
    ymt = sb("ymt", [128, 512], BF16); ymlT = sb("ymlT", [128, 4, 512], BF16)
    Cst = [sb("Cst%d" % l, [128, 4, 132], F32) for l in range(2)]
    mrep = [sb("mrep%d" % l, [128, 4], F32) for l in range(2)]
    halo = [sb("halo%d" % l, [128, 8, 3], F32) for l in range(2)]
    sm = [sb("sm%d" % i, [128, 16], F32) for i in range(8)]
    dg = sb("dg", [4, 4], F32)
    gsb = [sb("gsb%d" % i, [128, 512], BF16) for i in range(6)]
    tmf = [sb("tmf%d" % i, [128, 512], F32) for i in range(4)]
    hh = tmf[0]; rsum = tmf[1]; bcs = tmf[2]
    saf = [sga, sgt]
    yout = stg
    mixv = mixTt.t[:, :].rearrange("p (k w) -> p k w", k=8)

    memset(vaug[:, :, :, 64:65], 1.0)
    for i in range(2):
        memset(V(vg[i].t[:, :].rearrange("p (k h e) -> p k h e", k=4, h=8)[:, :, :, 64:65], [vg[i].buf]), 1.0)

    class WS:
        def __init__(self):
            self.plan = []
            self.issued = 0
            self.taken = 0

        def add(self, parts):
            self.plan.append(parts)

        def _issue(self, i):
            slot = slabs[i % len(slabs)]
            for (off, shape, src) in self.plan[i]:
                n = 1
                for s_ in shape[1:]:
                    n *= s_
                dst = slot.t[0:shape[0], off:off + n]
                if len(shape) == 3:
                    dst = dst.rearrange("p (a b) -> p a b", a=shape[1])
                elif len(shape) == 4:
                    dst = dst.rearrange("p (a b d) -> p a b d", a=shape[1], b=shape[2])
                if "nowdma" not in ABL:
                    c.dma("gpsimd", V(dst, [slot.buf]), src)

        def take(self):
            i = self.taken
            self.taken += 1
            while self.issued < len(self.plan) and self.issued <= i + len(slabs) - 2:
                self._issue(self.issued)
                self.issued += 1
            assert self.issued > i
            return slabs[i % len(slabs)]

    ws = WS()

    def plan_layer(l):
        wv = w_in[l].rearrange("(kc p) n -> p kc n", p=128)
        for c0 in (C_SGU, C_SGV, C_FQ, C_FK, C_FV, C_MQK, C_MQK + 512, C_MV, C_MO):
            ws.add([(0, (128, 8, 512), wv[:, :, c0:c0 + 512])])
        for j in range(8):
            ws.add([(b * 1024, (128, 8, 128), wv[:, :, b * 1024 + j * 128: b * 1024 + (j + 1) * 128]) for b in range(3)])
            ws.add([(0, (128, 4, 128), w_br[l, 0].rearrange("(c p) n -> p c n", p=128)[:, :, j * 128:(j + 1) * 128]),
                    (512, (128, 4, 128), w_br[l, 2].rearrange("(c p) n -> p c n", p=128)[:, :, j * 128:(j + 1) * 128]),
                    (1024, (64, 8, 128), w_br[l, 1].rearrange("(h p) n -> p h n", p=64)[:, :, j * 128:(j + 1) * 128])])
        wo = w_out[l].rearrange("(kc p) n -> p kc n", p=128)
        for nh in range(2):
            ws.add([(0, (128, 8, 512), wo[:, :, nh * 512:(nh + 1) * 512])])
        wf = w_ffn_in[l].rearrange("(kc p) n -> p kc n", p=128)
        for jg in range(6):
            ncol = 512 if jg < 5 else 256
            ws.add([(0, (128, 8, ncol), wf[:, :, jg * 512: jg * 512 + ncol])])
            ws.add([(0, (128, 8, ncol), wf[:, :, DFF + jg * 512: DFF + jg * 512 + ncol])])
        wfo = w_ffn_out[l].rearrange("(j p) n -> p j n", p=128)
        for nh in range(2):
            for js in range(6):
                nj = 4 if js < 5 else 2
                ws.add([(0, (128, nj, 512), wfo[:, js * 4: js * 4 + nj, nh * 512:(nh + 1) * 512])])

    def rstd_from_ss(ss, n, Lp, tmp):
        ts(ss, ss, 1.0 / n, EPS, ALU.mult, ALU.add)
        act(ss, ss, AF.Ln)
        act(ss, ss, AF.Exp, scale=-0.5)

    def norm_to_featT(l, gcol, L, ntile):
        W = ntile * L
        for t in range(ntile):
            ss = st_col[0:L, 0:1]
            xs_b = xs_bs[t % 2]
            act(xs_b[0:L, :], X[0:L, t, :], AF.Square, accum=ss)
            rstd_from_ss(ss, D, L, None)
            ts(xs_b[0:L, :], X[0:L, t, :], ss, None, ALU.mult)
            pb = bkb(t % 2)
            for kc in range(8):
                tr(V(pb[:, kc * L:(kc + 1) * L], [bank[t % 2].buf]), xs_b[0:L, kc * 128:(kc + 1) * 128], ident_b[0:L, 0:L], signal=(kc == 7))
            tt(xnT[:, :, t * L:(t + 1) * L], V(pb[:, 0:8 * L].rearrange("p (c j) -> p c j", c=8), [bank[t % 2].buf]),
               bc(gcol, [128, 8, L], 2), ALU.mult)

    def proj_F(slab, cols, W, bk):
        sv = slab.t[:, :].rearrange("p (a b) -> p a b", a=8)
        for kc in range(8):
            mm(bank[bk][:, 0:W], V(sv[:, kc, cols[0]:cols[1]], [slab.buf]), xnT[:, kc, 0:W], kc == 0, kc == 7)

    def proj_T(slab, ncol, t, L, bk, n0=0):
        sv = slab.t[:, 0:8 * ncol].rearrange("p (a b) -> p a b", a=8)
        for kc in range(8):
            mm(bank[bk][0:L, 0:ncol], xnT[:, kc, t * L:(t + 1) * L], V(sv[:, kc, :], [slab.buf]), kc == 0, kc == 7)

    def layer_block(l, tiles):
        L = tiles[0]["L"]
        NT = len(tiles)
        W = NT * L
        norm_to_featT(l, gmix[:, l, :], L, NT)
        if STAGE < 1:
            return
        s_u = ws.take()
        for cc in range(4):
            bk = cc % 4
            proj_F(s_u, (cc * 128, (cc + 1) * 128), W, bk)
            act(uT[:, cc, 0:W], bank[bk][:, 0:W], AF.Gelu)
        s_v = ws.take()
        for t, tl in enumerate(tiles):
            bk = 4 + t % 2
            proj_T(s_v, 512, t, L, bk)
            sacc = st_col[0:L, 1:2]
            act(sga[0:L, :], bank[bk][0:L, :], AF.Gelu, accum=sacc)
            ts(sacc, sacc, -1.0 / 512, None, ALU.mult)
            ssq = st_col[0:L, 2:3]
            act(sgt[0:L, :], sga[0:L, :], AF.Square, bias=sacc, accum=ssq)
            rstd_from_ss(ssq, 512, L, None)
            ts(sga[0:L, :], sga[0:L, :], sacc, ssq, ALU.add, ALU.mult)
            tt(sga[0:L, :], sga[0:L, :], lng[0:L, l, :], ALU.mult)
            if tl["kind"] == "s":
                tt(sgt[0:L, :], sga[0:L, :], lnb[0:L, l, :], ALU.add)
                c.dma("sync", sgv_s[l, tl["b"]], sgt[0:L, :])
                cp(vb[0:L, t, :], sgt[0:L, :])
            else:
                tt(vb[0:L, t, :], sga[0:L, :], lnb[0:L, l, :], ALU.add)
            bk2 = 6 + t % 2
            for g in range(4):
                mm(bank[bk2][:, g * L:(g + 1) * L], vb[0:L, t, g * 128:(g + 1) * 128], wsT[0:L, l, g, 0:L], True, True, signal=(g == 3))
            mx = V(bank[bk2].t[:, 0:4 * L].rearrange("p (g j) -> p g j", g=4), [bank[bk2].buf])
            tmv = V(tmf[0].t[:, 0:4 * L].rearrange("p (g j) -> p g j", g=4), [tmf[0].buf])
            tt(tmv, mx, V(bsbc.t[:, l, :].rearrange("p (g j) -> p g j", g=4)[:, :, 0:L], [bsbc.buf]), ALU.add)
            tt(ysgT[:, :, t * L:(t + 1) * L], tmv, uT[:, :, t * L:(t + 1) * L], ALU.mult)
        if STAGE < 2:
            return
        s_q = ws.take()
        for cc in range(4):
            proj_F(s_q, (cc * 128, (cc + 1) * 128), W, cc)
            act(qT[:, cc, 0:W], bank[cc][:, 0:W], AF.Copy, scale=0.125)
        s_k = ws.take()
        for cc in range(4):
            proj_F(s_k, (cc * 128, (cc + 1) * 128), W, 4 + cc)
            cp(kTb[:, cc, 0:W], bank[4 + cc][:, 0:W])
        for t, tl in enumerate(tiles):
            proj_T(s_k, 512, t, L, t % 4)
            st = stg[t % 2]
            cp(st[0:L, 0:512], bank[t % 4][0:L, :], eng="scalar")
            dst = (fk_p[l, tl["b"], tl["pos"]:tl["pos"] + L, :] if tl["kind"] == "p" else fk_s[l, tl["b"]])
            c.dma("sync", dst, st[0:L, 0:512])
        s_vv = ws.take()
        for t, tl in enumerate(tiles):
            bk = 4 + t % 4
            proj_T(s_vv, 512, t, L, bk)
            st = stg[t % 2]
            cp(st[0:L, 512:1024], bank[bk][0:L, :], eng="scalar")
            dst = (fv_p[l, tl["b"], tl["pos"]:tl["pos"] + L, :] if tl["kind"] == "p" else fv_s[l, tl["b"]])
            c.dma("sync", dst, st[0:L, 512:1024])
            cp(vaug[0:L, t, :, 0:64], V(bank[bk].t[0:L, :].rearrange("p (h e) -> p h e", h=8), [bank[bk].buf]))
        for t, tl in enumerate(tiles):
            bk = t % 4
            for kc in range(8):
                mm(bank[bk][0:L, 0:16], xnT[:, kc, t * L:(t + 1) * L], wsm[:, l, kc, :], kc == 0, kc == 7)
            tt(zz[0:L, t, :], bank[bk][0:L, 0:16], bsm[0:L, l, :], ALU.add)
            stt(zt[0][0:L, :], zz[0:L, t, :], -1.0, zz[0:L, t, :], ALU.mult, ALU.max)
            act(zt[1][0:L, :], zt[0][0:L, :], AF.Exp, scale=-1.0)
            act(zt[1][0:L, :], zt[1][0:L, :], AF.Ln, bias=1.0)
            ts(zt[2][0:L, :], zz[0:L, t, :], 0.0, None, ALU.min)
            tt(lsg[0:L, t, :], zt[2][0:L, :], zt[1][0:L, :], ALU.subtract)
            dst = (flf_p[l, tl["b"], tl["pos"]:tl["pos"] + L, :] if tl["kind"] == "p" else flf_s[l, tl["b"]])
            c.dma("sync", dst, lsg[0:L, t, 0:8])
        if STAGE < 3:
            return
        if tiles[0]["kind"] == "p":
            for t, tl in enumerate(tiles):
                fox_prologue(l, t, tl, L, t, 4 + t)
        for t, tl in enumerate(tiles):
            fox_tile(l, t, tl, L)
        if STAGE < 4:
            return
        s_mq = ws.take()
        s_mk = ws.take()
        for ci in range(8):
            slab = s_mq if ci < 4 else s_mk
            cc = ci % 4
            bk = ci % 4
            proj_F(slab, (cc * 128, (cc + 1) * 128), W, bk)
            rw = rawT[ci % 2]
            cp(rw[:, 0:NT, 3:3 + L], V(bank[bk].t[:, 0:W].rearrange("p (t j) -> p t j", t=NT), [bank[bk].buf]), eng="scalar")
            for t, tl in enumerate(tiles):
                if tl["kind"] == "s":
                    c.dma("sync", rw[:, t, 0:3], smconv[l, tl["b"], :, ci * 128:(ci + 1) * 128].rearrange("j p -> p j"))
                elif t == 0:
                    if tl["first"]:
                        memset(rw[:, 0, 0:3], 0.0)
                    else:
                        cp(rw[:, 0, 0:3], halo[l][:, ci, :])
                else:
                    cp(rw[:, t, 0:3], rw[:, t - 1, L:L + 3])
            for t, tl in enumerate(tiles):
                if tl["kind"] == "s" or tl["last"]:
                    dst = (mconv_p if tl["kind"] == "p" else mconv_s)[l, tl["b"], :, ci * 128:(ci + 1) * 128].rearrange("j p -> p j")
                    c.dma("sync", dst, rw[:, t, L:L + 3])
            if tiles[-1]["kind"] == "p" and not tiles[-1]["last"]:
                cp(halo[l][:, ci, :], rw[:, NT - 1, L:L + 3])
            ca_t = stg[ci % 2]
            class _CA:
                def __getitem__(self_, idx):
                    return V(ca_t.t[:, 0:512].rearrange("p (t j) -> p t j", t=4)[idx], [ca_t.buf])
            ca = _CA()
            ts(ca[:, 0:NT, 0:L], rw[:, 0:NT, 3:3 + L], wconv[:, l, 3, ci:ci + 1], None, ALU.mult)
            for j in (2, 1, 0):
                stt(ca[:, 0:NT, 0:L], rw[:, 0:NT, j:j + L], wconv[:, l, j, ci:ci + 1], ca[:, 0:NT, 0:L], ALU.mult, ALU.add)
            if ci < 4:
                act(V(qmT.t[:, cc, 0:W].rearrange("p (t j) -> p t j", t=NT), [qmT.buf]), ca[:, 0:NT, 0:L], AF.Silu, bias=bconv[:, l, ci:ci + 1])
            else:
                act(ca[:, 0:NT, 0:L], ca[:, 0:NT, 0:L], AF.Silu, bias=bconv[:, l, ci:ci + 1])
                ts(V(kmT.t[:, cc, 0:W].rearrange("p (t j) -> p t j", t=NT), [kmT.buf]), ca[:, 0:NT, 0:L], 128.0 ** -0.5, None, ALU.mult)
        s_mv = ws.take()
        for t in range(NT):
            bk = 4 + t % 4
            proj_T(s_mv, 512, t, L, bk)
            cp(vml[0:L, t, :], bank[bk][0:L, :])
        s_mo = ws.take()
        for t in range(NT):
            bk = t % 4
            proj_T(s_mo, 512, t, L, bk)
            act(omt[0:L, t, :], bank[bk][0:L, :], AF.Sigmoid)
        if STAGE < 5:
            return
        for t, tl in enumerate(tiles):
            if "noml" not in ABL:
                ml_tile(l, t, tl, L)
        if DBG and l == 0:
            c.dma("sync", dbg_fox, yfoxT[:]); c.dma("sync", dbg_sg, ysgT[:]); c.dma("sync", dbg_ml, ymlT[:])
        if STAGE < 6:
            return
        for j in range(8):
            s_g = ws.take()
            s_b = ws.take()
            gv = s_g.t[:, 0:3072].rearrange("p (b kc n) -> p b kc n", b=3, kc=8)
            gs = []
            for b in range(3):
                n_ = 3 * j + b
                gbk = bank[n_ % 4]
                for kc in range(8):
                    mm(gbk[:, 0:W], V(gv[:, b, kc, :], [s_g.buf]), xnT[:, kc, 0:W], kc == 0, kc == 7)
                g_ = gsb[n_ % 6]
                act(g_[:, 0:W], gbk[:, 0:W], AF.Sigmoid, bias=bgate[:, l, b, j:j + 1])
                gs.append(g_)
            w0 = s_b.t[:, 0:512].rearrange("p (c n) -> p c n", c=4)
            w2 = s_b.t[:, 512:1024].rearrange("p (c n) -> p c n", c=4)
            w1 = s_b.t[0:64, 1024:2048].rearrange("p (h n) -> p h n", h=8)
            bb = [bank[4 + (3 * j + b) % 4] for b in range(3)]
            for cc in range(4):
                mm(bb[0][:, 0:W], V(w0[:, cc, :], [s_b.buf]), ysgT[:, cc, 0:W], cc == 0, cc == 3)
            for h in range(8):
                mm(bb[1][:, 0:W], V(w1[:, h, :], [s_b.buf]), yfoxT[0:64, h, 0:W], h == 0, h == 7)
            for cc in range(4):
                mm(bb[2][:, 0:W], V(w2[:, cc, :], [s_b.buf]), ymlT[:, cc, 0:W], cc == 0, cc == 3)
            tA = tmf[(2 * j) % 4]
            tB = tmf[(2 * j + 1) % 4]
            tt(tA[:, 0:W], bb[0][:, 0:W], gs[0][:, 0:W], ALU.mult)
            tt(tB[:, 0:W], bb[1][:, 0:W], gs[1][:, 0:W], ALU.mult)
            tt(tA[:, 0:W], tA[:, 0:W], tB[:, 0:W], ALU.add)
            tt(tB[:, 0:W], bb[2][:, 0:W], gs[2][:, 0:W], ALU.mult)
            tt(V(mixv[:, j, 0:W], [mixTt.buf]), tA[:, 0:W], tB[:, 0:W], ALU.add)
        for nh in range(2):
            s_o = ws.take()
            sv = s_o.t[:, :].rearrange("p (a b) -> p a b", a=8)
            for t in range(NT):
                bk = (nh * NT + t) % 8
                for kc in range(8):
                    mm(bank[bk][0:L, :], V(mixv[:, kc, t * L:(t + 1) * L], [mixTt.buf]), V(sv[:, kc, :], [s_o.buf]), kc == 0, kc == 7)
                tt(X[0:L, t, nh * 512:(nh + 1) * 512], X[0:L, t, nh * 512:(nh + 1) * 512], bank[bk][0:L, :], ALU.add)
        if STAGE < 7:
            return
        norm_to_featT(l, gffn[:, l, :], L, NT)
        c.barrier()
        hv = hT.t[:, :].rearrange("p (j w) -> p j w", j=22)
        for jg in range(6):
            s_a = ws.take()
            s_bb = ws.take()
            nj = 4 if jg < 5 else 2
            ncol = nj * 128
            for jj in range(nj):
                j = jg * 4 + jj
                bka, bkb_ = (2 * jj) % 8, (2 * jj + 1) % 8
                sva = s_a.t[:, 0:8 * ncol].rearrange("p (a b) -> p a b", a=8)
                svb = s_bb.t[:, 0:8 * ncol].rearrange("p (a b) -> p a b", a=8)
                for kc in range(8):
                    mm(bank[bka][:, 0:W], V(sva[:, kc, jj * 128:(jj + 1) * 128], [s_a.buf]), xnT[:, kc, 0:W], kc == 0, kc == 7)
                for kc in range(8):
                    mm(bank[bkb_][:, 0:W], V(svb[:, kc, jj * 128:(jj + 1) * 128], [s_bb.buf]), xnT[:, kc, 0:W], kc == 0, kc == 7)
                sa = saf[j % 2]
                act(sa[:, 0:W], bank[bka][:, 0:W], AF.Silu)
                tt(V(hv[:, j, 0:W], [hT.buf]), sa[:, 0:W], bank[bkb_][:, 0:W], ALU.mult)
        for nh in range(2):
            for js in range(6):
                s_f = ws.take()
                nj = 4 if js < 5 else 2
                sv = s_f.t[:, 0:nj * 512].rearrange("p (a b) -> p a b", a=nj)
                for t in range(NT):
                    bk = nh * 4 + t
                    for jj in range(nj):
                        j = js * 4 + jj
                        mm(bank[bk][0:L, :], V(hv[:, j, t * L:(t + 1) * L], [hT.buf]), V(sv[:, jj, :], [s_f.buf]), j == 0, j == 21, signal=(jj == nj - 1))
            for t in range(NT):
                bk = nh * 4 + t
                tt(X[0:L, t, nh * 512:(nh + 1) * 512], X[0:L, t, nh * 512:(nh + 1) * 512], bank[bk][0:L, :], ALU.add)
        c.barrier()

    def fox_prologue(l, t, tl, L, bk_c, bk_t):
        kb = tl["kb"]
        b = tl["b"]
        R = Rrun[l]
        nF = negF[l]
        biasq = biasqs[t]
        if tl["kind"] == "s":
            clt = stg[1]
            cl3 = clt.t[:, 0:NKP * 8].rearrange("p (k h) -> p k h", k=NKP)
            for q4 in range((NKP + 7) // 8):
                k1 = min(NKP, q4 * 8 + 8)
                c.dma("sync", V(cl3[:, q4 * 8:k1, :], [clt.buf]),
                      clf[l, b, q4 * 1024:k1 * 128, :].rearrange("(k p) h -> p k h", p=128))
            mm(bank[6][:, 0:NKP * 8], tri_f[:], clt[:, 0:NKP * 8], True, True)
            mm(bank[7][:, 0:NKP * 8], ones_f[:], clt[:, 0:NKP * 8], True, True)
            memset(ptot[:, 0, :], 0.0)
            tot3 = bank[7].t[:, 0:NKP * 8].rearrange("p (k h) -> p k h", k=NKP)
            for k in range(NKP):
                tt(ptot[:, k + 1, :], ptot[:, k, :], V(tot3[:, k, :], [bank[7].buf]), ALU.add)
            cin3 = V(bank[6].t[:, 0:NKP * 8].rearrange("p (k h) -> p k h", k=NKP), [bank[6].buf])
            stt(nF[:, 0:NKP, :], cin3, -1.0, ptot[:, 0:NKP, :], ALU.mult, ALU.subtract)
            cp(R[:, :], ptot[:, NKP, :])
        elif tl["first"]:
            memset(R[:, :], 0.0)
        mm(bank[bk_c][0:L, 256:272], tri_f[0:L, 0:L], lsg[0:L, t, :], True, True)
        mm(bank[bk_t][:, 256:272], ones_f[0:L, :], lsg[0:L, t, :], True, True)
        stt(nF[0:L, kb, :], bank[bk_c][0:L, 256:264], -1.0, R[0:L, :], ALU.mult, ALU.subtract)
        tt(R[:, :], R[:, :], bank[bk_t][:, 256:264], ALU.add)
        nkb = kb + 1
        tt(biasq[:, 0:nkb, :], nF[:, 0:nkb, :], bc(R[:, :], [128, nkb, 8], 1), ALU.add)
        act(biasq[:, 0:nkb, :], biasq[:, 0:nkb, :], AF.Exp)
        if tl["kind"] == "p" and not tl["last"]:
            c.dma("sync", V(kts[l, kb].rearrange("p (c n) -> p c n", c=4), [kts_buf[l][kb]]), kTb[:, :, t * L:(t + 1) * L])
            c.dma("sync", V(vsc[l, kb], [vsc_buf[l][kb]]), V(vaug.t[:, t, :, :].rearrange("p h e -> p (h e)"), [vaug.buf]))

    def fox_tile(l, t, tl, L):
        kb = tl["kb"]
        b = tl["b"]
        biasq = biasqs[t]
        if tl["kind"] == "s":
            fox_prologue(l, t, tl, L, 6, 7)
        if FS < 2:
            return
        oacc = [bank[0], bank[1]]

        items = []

        def load_group(g):
            k0 = g * 4
            nk = min(4, kb - k0)
            kt = ktg[g % 2]
            vgt = vg[g % 2]
            ktv = kt.t[:, :].rearrange("p (k c n) -> p k c n", k=4, c=4)
            vgv = vgt.t[:, :].rearrange("p (k h e) -> p k h e", k=4, h=8)
            if "nogl" in ABL:
                pass
            elif tl["kind"] == "p":
                c.dma("sync", V(ktv[:, 0:nk], [kt.buf]),
                      V(kts[l, k0:k0 + nk].rearrange("k p (c n) -> p k c n", c=4), [kts_buf[l][k0 + i] for i in range(nk)]))
                c.dma("sync", V(vgt.t[:, 0:nk * 576].rearrange("p (k f) -> p k f", k=nk), [vgt.buf]),
                      V(vsc[l, k0:k0 + nk].rearrange("k p f -> p k f"), [vsc_buf[l][k0 + i] for i in range(nk)]))
            else:
                ksv = kst.t[:, :].rearrange("p (k n) -> p k n", k=4)
                c.dma("gpsimd", V(ksv, [kst.buf]), ck[l, b, k0 * 128:(k0 + 4) * 128, :].rearrange("(k p) n -> p k n", p=128))
                c.dma("gpsimd", V(vst.t[:, :].rearrange("p (k n) -> p k n", k=4), [vst.buf]),
                      cv[l, b, k0 * 128:(k0 + 4) * 128, :].rearrange("(k p) n -> p k n", p=128))
                for k in range(4):
                    pb_i = 6 + (k % 2)
                    pb = bkb(pb_i)
                    for pr in range(4):
                        tr(V(pb[:, pr * 128:(pr + 1) * 128], [bank[pb_i].buf]), V(ksv[:, k, pr * 128:(pr + 1) * 128], [kst.buf]),
                           ident_b[:, :], signal=(pr == 3))
                    cp(V(ktv[:, k, :, :], [kt.buf]), V(pb[:, 0:512].rearrange("p (c n) -> p c n", c=4), [bank[pb_i].buf]), eng="scalar")
                cp(V(vgv[:, :, :, 0:64], [vgt.buf]), V(vst.t[:, :].rearrange("p (k h e) -> p k h e", k=4, h=8), [vst.buf]))
                memset(V(vgv[:, :, :, 64:65], [vgt.buf]), 1.0)
            for k in range(nk):
                items.append((ktv[:, k], [kt.buf], vgv[:, k], [vgt.buf], 128, k0 + k, False))

        ngrp = (kb + 3) // 4
        state = {"first": True}

        def qk_exp(it, slot):
            kview, kbuf, vview, vbuf, Lk, kbi, diag = it
            for h in range(8):
                p, half = h // 2, h % 2
                sb_ = bank[2 + 2 * slot + half]
                mm(sb_[0:Lk, p * L:(p + 1) * L], V(kview[64 * half:64 * half + 64, p, 0:Lk], kbuf),
                   qT[64 * half:64 * half + 64, p, t * L:(t + 1) * L], True, True)
            if QS < 1:
                return
            for half in range(2):
                sb_ = bank[2 + 2 * slot + half]
                if "noexp" not in ABL:
                    act(pT[2 * slot + half][0:Lk, :, 0:L],
                        V(sb_.t[0:Lk, 0:4 * L].rearrange("p (c j) -> p c j", c=4), [sb_.buf]), AF.Exp)
            vw = vws[slot]
            tt(vw[0:Lk, :, 0:65], V(vview[0:Lk, :, 0:65], vbuf), bc(biasq[0:Lk, kbi, :], [Lk, 8, 65], 2), ALU.mult)
            if QS < 2:
                return
            if diag:
                for hb in range(2):
                    pvv = pT[2 * slot + hb][0:Lk, :, 0:L]
                    tt(pvv, pvv, bc(tri_b[0:Lk, 0:L], [Lk, 4, L], 1), ALU.mult)

        def pv_mm(it, slot, last):
            kview, kbuf, vview, vbuf, Lk, kbi, diag = it
            if QS < 3:
                return
            for h in range(8):
                ob = oacc[h // 4]
                pv = pT[2 * slot + h % 2][0:Lk, h // 2, 0:L]
                mm(ob[0:65, (h % 4) * L:(h % 4 + 1) * L], vws[slot][0:Lk, h, 0:65], pv,
                   start=(state["first"] and h % 4 == 0), stop=last, signal=(last and h % 4 == 3))
            state["first"] = False

        pipe = {"prev": None, "idx": 0}

        def push(it):
            slot = pipe["idx"] % 2
            qk_exp(it, slot)
            if pipe["prev"] is not None:
                pv_mm(pipe["prev"][0], pipe["prev"][1], False)
            pipe["prev"] = (it, slot)
            pipe["idx"] += 1

        push((kTb.t[:, :, t * L:(t + 1) * L], [kTb.buf], vaug.t[:, t], [vaug.buf], L, kb, True))
        for g in range(ngrp if (FS >= 4 and "nopast" not in ABL) else 0):
            del items[:]
            load_group(g)
            for it in list(items):
                push(it)
        pv_mm(pipe["prev"][0], pipe["prev"][1], True)
        if FS < 3:
            c.op("vector", lambda e: e.memset(xs_bs[0].t[:, 0:4], 0.0), reads=[oacc[0][:], oacc[1][:]], writes=[xs_bs[0][:]])
            return
        for hb in range(2 if "notail" not in ABL else 0):
            ob = oacc[hb]
            c.op("vector", lambda e, ob=ob: e.reciprocal(out=rsum.t[64:65, 0:4 * L], in_=ob.t[64:65, 0:4 * L]), reads=[ob[:]], writes=[rsum[:]])
            mm(bank[6 + hb][0:64, 0:4 * L], ones_f[64:65, 0:64], rsum[64:65, 0:4 * L], True, True)
            cp(bcs[0:64, 0:4 * L], bank[6 + hb][0:64, 0:4 * L])
            tt(yfoxT[0:64, hb * 4:(hb + 1) * 4, t * L:(t + 1) * L],
               V(ob.t[0:64, 0:4 * L].rearrange("p (h j) -> p h j", h=4), [ob.buf]),
               V(bcs.t[0:64, 0:4 * L].rearrange("p (h j) -> p h j", h=4), [bcs.buf]), ALU.mult)

    def ml_tile(l, t, tl, L):
        b = tl["b"]
        Cs = Cst[l]
        mr = mrep[l]
        if tl["kind"] == "s":
            c.dma("sync", Cs[:, :, 0:128], smc[l, b].rearrange("h d e -> d h e"))
            c.dma("sync", Cs[:, :, 128], smn[l, b].rearrange("h d -> d h"))
            c.dma("sync", mr[:, :], smm[l, b:b + 1, :].partition_broadcast(128))
        elif tl["first"]:
            memset(Cs[:, :, :], 0.0)
            memset(mr[:, :], 0.0)
        mm(bank[6][0:L, 256:272], tri_f[0:L, 0:L], lsg[0:L, t, :], True, True)
        mm(bank[7][:, 256:272], ones_f[0:L, :], lsg[0:L, t, :], True, True)
        bcol = sm[0]
        cp(bcol[0:L, 0:4], bank[6][0:L, 268:272])
        bend = sm[1]
        cp(bend[:, 0:4], bank[7][:, 268:272])
        acol = sm[2]
        tt(acol[0:L, 0:4], zz[0:L, t, 8:12], bcol[0:L, 0:4], ALU.subtract)
        mm(bank[6][0:4, 0:L], acol[0:L, 0:4], ident_f[0:L, 0:L], True, True)
        amx = sm[3]
        c.op("vector", lambda e: e.reduce_max(out=amx.t[0:4, 0:1], in_=bank[6].t[0:4, 0:L], axis=AX.X), reads=[bank[6][:]], writes=[amx[:]])
        ts(dg[:, :], ident_f[0:4, 0:4], amx[0:4, 0:1], None, ALU.mult)
        mm(bank[7][:, 0:4], ones_f[0:4, :], dg[:, :], True, True)
        Rr = sm[4]
        tt(Rr[:, 0:4], bank[7][:, 0:4], mr[:, :], ALU.max)
        ecol = sm[5]
        tt(ecol[0:L, 0:4], acol[0:L, 0:4], Rr[0:L, 0:4], ALU.subtract)
        act(ecol[0:L, 0:4], ecol[0:L, 0:4], AF.Exp)
        sc = sm[3]
        tt(sc[:, 4:8], mr[:, :], Rr[:, 0:4], ALU.subtract)
        act(sc[:, 4:8], sc[:, 4:8], AF.Exp)
        thr = sm[2]
        tt(thr[0:L, 4:8], bcol[0:L, 0:4], Rr[0:L, 0:4], ALU.add)
        act(thr[0:L, 4:8], thr[0:L, 4:8], AF.Exp, scale=-1.0)
        tt(mr[:, :], bend[:, 0:4], Rr[:, 0:4], ALU.add)
        tt(vpa[0:L, :, 0:128], V(vml.t[0:L, t, :].rearrange("p (h e) -> p h e", h=4), [vml.buf]), bc(ecol[0:L, 0:4], [L, 4, 128], 2), ALU.mult)
        cp(vpa[0:L, :, 128], ecol[0:L, 0:4])
        pb = bkb(5)
        for h in range(4):
            tr(V(pb[0:L, h * 128:(h + 1) * 128], [bank[5].buf]), kmT[:, h, t * L:(t + 1) * L], ident_b[:, :], signal=(h == 3))
        cp(ktok[0:L, :, :], V(pb[0:L, 0:512].rearrange("p (h d) -> p h d", h=4), [bank[5].buf]), eng="scalar")
        for h in range(4):
            mm(bank[4][0:L, h * L:(h + 1) * L], kmT[:, h, t * L:(t + 1) * L], qmT[:, h, t * L:(t + 1) * L], True, True, signal=(h == 3))
        tt(sTm[0:L, :, 0:L], V(bank[4].t[0:L, 0:4 * L].rearrange("p (h j) -> p h j", h=4), [bank[4].buf]), bc(tri_f[0:L, 0:L], [L, 4, L], 1), ALU.mult)
        tt(Cbf[:, :, 0:129], Cs[:, :, 0:129], bc(sc[:, 4:8], [128, 4, 129], 2), ALU.mult)
        for h in range(4):
            nb = bank[h // 2]
            o0 = (h % 2) * 192
            mm(nb[0:L, o0:o0 + 129], sTm[0:L, h, 0:L], vpa[0:L, h, 0:129], True, False, signal=False)
            mm(nb[0:L, o0:o0 + 129], qmT[:, h, t * L:(t + 1) * L], Cbf[:, h, 0:129], False, True, signal=True)
            db = bank[2 + h // 2]
            mm(db[:, o0:o0 + 129], ktok[0:L, h, :], vpa[0:L, h, 0:129], True, True)
        for h in range(4):
            db = bank[2 + h // 2]
            o0 = (h % 2) * 192
            stt(Cs[:, h, 0:129], Cs[:, h, 0:129], sc[:, 4 + h:5 + h], db[:, o0:o0 + 129], ALU.mult, ALU.add)
        dn = sm[5]
        for h in range(4):
            nb = bank[h // 2]
            o0 = (h % 2) * 192
            cp(dn[0:L, 8 + h:9 + h], nb[0:L, o0 + 128:o0 + 129])
        stt(dn[0:L, 12:16], dn[0:L, 8:12], -1.0, dn[0:L, 8:12], ALU.mult, ALU.max)
        tt(dn[0:L, 12:16], dn[0:L, 12:16], thr[0:L, 4:8], ALU.max)
        c.op("vector", lambda e: e.reciprocal(out=dn.t[0:L, 12:16], in_=dn.t[0:L, 12:16]), reads=[dn[:]], writes=[dn[:]])
        ssq = sm[1]
        for h in range(4):
            nb = bank[h // 2]
            o0 = (h % 2) * 192
            stt(hh[0:L, h * 128:(h + 1) * 128], nb[0:L, o0:o0 + 128], dn[0:L, 12 + h:13 + h], omt[0:L, t, h * 128:(h + 1) * 128], ALU.mult, ALU.mult)
            act(ymt[0:L, h * 128:(h + 1) * 128], hh[0:L, h * 128:(h + 1) * 128], AF.Square, accum=ssq[0:L, 8 + h:9 + h])
        ts(ssq[0:L, 8:12], ssq[0:L, 8:12], 1.0 / 128, EPS, ALU.mult, ALU.add)
        act(ssq[0:L, 8:12], ssq[0:L, 8:12], AF.Ln)
        act(ssq[0:L, 8:12], ssq[0:L, 8:12], AF.Exp, scale=-0.5)
        tt(V(ymt.t[0:L, :].rearrange("p (h e) -> p h e", h=4), [ymt.buf]), V(hh.t[0:L, :].rearrange("p (h e) -> p h e", h=4), [hh.buf]),
           bc(ssq[0:L, 8:12], [L, 4, 128], 2), ALU.mult)
        pb = bkb(5)
        for h in range(4):
            tr(V(pb[:, h * L:(h + 1) * L], [bank[5].buf]), ymt[0:L, h * 128:(h + 1) * 128], ident_b[0:L, 0:L], signal=(h == 3))
        tt(ymlT[:, :, t * L:(t + 1) * L], V(pb[:, 0:4 * L].rearrange("p (h j) -> p h j", h=4), [bank[5].buf]), bc(gml[:, l, :], [128, 4, L], 2), ALU.mult)
        if tl["kind"] == "s" or tl["last"]:
            mc_o, mn_o, mm_o = (mc_p, mn_p, mm_p) if tl["kind"] == "p" else (mc_s, mn_s, mm_s)
            c.dma("sync", mc_o[l, b].rearrange("h d e -> d h e"), Cs[:, :, 0:128])
            c.dma("sync", mn_o[l, b].rearrange("h d -> d h"), Cs[:, :, 128])
            c.dma("sync", mm_o[l, b:b + 1, :], mr[0:1, :])

    blocks = []
    for s in range(NSEQ):
        for bi in range(NBLK):
            tiles = []
            for t in range(4):
                pos = bi * 512 + t * 128
                tiles.append(dict(kind="p", L=128, b=s, pos=pos, kb=pos // 128, first=(pos == 0), last=(pos == SP - 128)))
            blocks.append(tiles)
    if DO_SAMPLE:
        blocks.append([dict(kind="s", L=16, b=bb, pos=PAST, kb=NKP, first=True, last=True) for bb in range(4)])

    for bi_, tiles in enumerate(blocks):
        for l in range(NL):
            plan_layer(l)
    for tiles in blocks:
        L = tiles[0]["L"]
        for t, tl in enumerate(tiles):
            src = xp[tl["b"], tl["pos"]:tl["pos"] + L, :] if tl["kind"] == "p" else xs[tl["b"]]
            c.dma("sync", X[0:L, t, :], src)
        for l in range(NL):
            layer_block(l, tiles)
        for t, tl in enumerate(tiles):
            ss = st_col[0:L, 0:1]
            act(xs_bs[t % 2][0:L, :], X[0:L, t, :], AF.Square, accum=ss)
            rstd_from_ss(ss, D, L, None)
            yo = yout[t % 2]
            stt(yo[0:L, :], X[0:L, t, :], ss, nfin[0:L, :], ALU.mult, ALU.mult)
            dst = y_p[tl["b"], tl["pos"]:tl["pos"] + L, :] if tl["kind"] == "p" else y_s[tl["b"]]
            c.dma("sync", dst, yo[0:L, :])
    c.finish()
    cx.close()
    return nc, c


_CONST = None


def _consts():
    ident = np.eye(128, dtype=np.float32)
    tri = np.triu(np.ones((128, 128), dtype=np.float32))
    return ident, tri


def make_in_maps(inputs):
    ident, tri = _consts()
    f = lambda a: np.ascontiguousarray(np.asarray(a, dtype=np.float32))
    shared = dict(
        norm_mix=f(inputs["norm_mix"]), w_in=f(inputs["w_in"]), b_gate=f(inputs["b_gate"]),
        sg_ln_g=f(inputs["sg_ln_g"]), sg_ln_b=f(inputs["sg_ln_b"]), sg_w=f(inputs["sg_w"]), sg_b=f(inputs["sg_b"]),
        b_fox_f=f(inputs["b_fox_f"]), ml_conv_w=f(inputs["ml_conv_w"]), ml_conv_b=f(inputs["ml_conv_b"]),
        b_ml_i=f(inputs["b_ml_i"]), b_ml_f=f(inputs["b_ml_f"]), ml_norm_g=f(inputs["ml_norm_g"]),
        w_br=f(inputs["w_br"]), w_out=f(inputs["w_out"]), norm_ffn=f(inputs["norm_ffn"]),
        w_ffn_in=f(inputs["w_ffn_in"]), w_ffn_out=f(inputs["w_ffn_out"]),
        norm_final=f(inputs["norm_final"]).reshape(1, D), c_ident=ident, c_tri=tri)
    maps = []
    for i in range(8):
        m = dict(shared)
        m["xp"] = f(inputs["x_prompt"][2 * i:2 * i + 2])
        m["xs"] = f(inputs["x_sample"][4 * i:4 * i + 4])
        m["ck"] = f(np.asarray(inputs["cache_fox_k"])[:, 4 * i:4 * i + 4].reshape(2, 4, 4096, 512))
        m["cv"] = f(np.asarray(inputs["cache_fox_v"])[:, 4 * i:4 * i + 4].reshape(2, 4, 4096, 512))
        m["clf"] = f(np.asarray(inputs["cache_fox_logf"])[:, 4 * i:4 * i + 4])
        m["smc"] = f(np.asarray(inputs["state_ml_c"])[:, 4 * i:4 * i + 4])
        m["smn"] = f(np.asarray(inputs["state_ml_n"])[:, 4 * i:4 * i + 4])
        m["smm"] = f(np.asarray(inputs["state_ml_m"])[:, 4 * i:4 * i + 4])
        m["smconv"] = f(np.asarray(inputs["state_ml_conv"])[:, 4 * i:4 * i + 4])
        maps.append(m)
    return maps


def assemble(results):
    cat = lambda k, ax: np.concatenate([np.asarray(r[k]) for r in results], axis=ax)
    y_p = cat("y_p", 0); y_s = cat("y_s", 0)
    fk_p = cat("fk_p", 1).reshape(2, 16, 4096, 8, 64); fv_p = cat("fv_p", 1).reshape(2, 16, 4096, 8, 64)
    flf_p = cat("flf_p", 1); mc_p = cat("mc_p", 1); mn_p = cat("mn_p", 1); mm_p = cat("mm_p", 1); mconv_p = cat("mconv_p", 1)
    fk_s = cat("fk_s", 1).reshape(2, 32, 16, 8, 64); fv_s = cat("fv_s", 1).reshape(2, 32, 16, 8, 64)
    flf_s = cat("flf_s", 1); mc_s = cat("mc_s", 1); mn_s = cat("mn_s", 1); mm_s = cat("mm_s", 1)
    mconv_s = cat("mconv_s", 1); sgv_s = cat("sgv_s", 1)
    return (y_p, y_s, fk_p, fv_p, flf_p, mc_p, mn_p, mm_p, mconv_p,
            fk_s, fv_s, flf_s, mc_s, mn_s, mm_s, mconv_s, sgv_s)


_NC = {}


def kernel(**inputs):
    cfg = inputs.pop("_cfg", {})
    key = tuple(sorted(cfg.items()))
    if key not in _NC:
        _NC[key] = build(cfg)[0]
    nc = _NC[key]
    maps = make_in_maps(inputs)
    res = run_bass_kernel_spmd(nc, maps, core_ids=list(range(8)))
    return assemble(res.results)


def kernel_debug(cfg, maps, trace=False):
    nc = build(cfg)[0]
    res = run_bass_kernel_spmd(nc, maps, core_ids=list(range(len(maps))), trace=trace)
    if trace:
        print("EXEC_TIME_NS", res.exec_time_ns)
    return res.results
```

```python
import contextlib
import numpy as np
import concourse.bass as bass
import concourse.mybir as mybir
from concourse.bass_utils import run_bass_kernel_spmd

F32 = mybir.dt.float32
BF16 = mybir.dt.bfloat16
AF = mybir.ActivationFunctionType
ALU = mybir.AluOpType
AX = mybir.AxisListType

D = 1024
NIN = 7696
DFF = 2816
EPS = 1e-6
C_GATE, C_SGU, C_SGV, C_FQ, C_FK, C_FV, C_FF = 0, 3072, 3584, 4096, 4608, 5120, 5632
C_MQK, C_MV, C_MI, C_MF, C_MO = 5640, 6664, 7176, 7180, 7184
SLAB = 4096


class Buf:
    __slots__ = ("name", "w", "r", "excl")

    def __init__(self, name, excl=False):
        self.name = name
        self.w = None
        self.r = {}
        self.excl = excl


class V:
    __slots__ = ("ap", "bufs")

    def __init__(self, ap, bufs):
        self.ap = ap
        self.bufs = bufs


class T:
    def __init__(self, t, name):
        self.t = t
        self.buf = Buf(name)

    def __getitem__(self, idx):
        return V(self.t[idx], [self.buf])

    def v(self, ap):
        return V(ap, [self.buf])


class Queue:
    def __init__(self, name, sem):
        self.name = name
        self.sem = sem
        self.cnt = 0
        self.ops = []
        self.seen = {}
        self.pending = False


class Ctx:
    def __init__(self, nc, n_dma_sems=12):
        self.nc = nc
        self.es = contextlib.ExitStack()
        self.q = {}
        for nm in ("sync", "scalar", "vector", "gpsimd", "tensor"):
            self.q[nm] = Queue(nm, self.es.enter_context(nc.semaphore("s_" + nm)))
        self.dq = {}
        for nm in ("sync", "gpsimd", "scalar"):
            self.dq[nm] = [Queue("dma_%s%d" % (nm, i), self.es.enter_context(nc.semaphore("s_d%s%d" % (nm, i))))
                           for i in range(n_dma_sems)]
        self.dq_next = {"sync": 0, "gpsimd": 0, "scalar": 0}
        self.n_inst = 0

    def sbuf(self, name, shape, dtype):
        return T(self.es.enter_context(self.nc.sbuf_tensor(name, list(shape), dtype)), name)

    def psum(self, name, shape, dtype):
        t = T(self.es.enter_context(self.nc.psum_tensor(name, list(shape), dtype)), name)
        t.buf.excl = True
        return t

    def _need(self, q, waits, dep, raw):
        if dep is None:
            return
        pq, c = dep
        if pq is q and ((not raw) or q.name == "tensor"):
            return
        if q.seen.get(pq, 0) >= c:
            return
        if waits.get(pq, 0) < c:
            waits[pq] = c

    def _deps(self, q, rb, wb):
        waits = {}
        for b in rb:
            self._need(q, waits, b.w, True)
            if b.excl:
                for rq, c in b.r.items():
                    if rq is not q:
                        self._need(q, waits, (rq, c), False)
        for b in wb:
            self._need(q, waits, b.w, False)
            for rq, c in b.r.items():
                self._need(q, waits, (rq, c), False)
        return waits

    def _emit_waits(self, q, waits):
        for pq, c in waits.items():
            q.seen[pq] = c
            q.ops.append(lambda e, sem=pq.sem, c=c: e.wait_ge(sem, c))

    def op(self, eng, fn, reads=(), writes=(), signal=True):
        q = self.q[eng]
        rb = [b for v in reads for b in v.bufs]
        wb = [b for v in writes for b in v.bufs]
        self._emit_waits(q, self._deps(q, rb, wb))
        self.n_inst += 1
        if signal:
            q.cnt += 1
            c = q.cnt
            q.ops.append(lambda e, fn=fn, sem=q.sem: fn(e).then_inc(sem, 1))
            q.pending = False
        else:
            c = q.cnt + 1
            q.ops.append(lambda e, fn=fn: fn(e))
            q.pending = True
        for b in rb:
            if b.r.get(q, 0) < c:
                b.r[q] = c
        for b in wb:
            b.w = (q, c)
            b.r = {}

    def dma(self, eng, out, in_, **kw):
        q = self.q[eng]
        pool = self.dq[eng]
        dq = pool[self.dq_next[eng]]
        self.dq_next[eng] = (self.dq_next[eng] + 1) % len(pool)
        rb = list(in_.bufs) if isinstance(in_, V) else []
        wb = list(out.bufs) if isinstance(out, V) else []
        waits = self._deps(q, rb, wb)
        if dq.cnt > 0 and q.seen.get(dq, 0) < dq.cnt and waits.get(dq, 0) < dq.cnt:
            waits[dq] = dq.cnt
        self._emit_waits(q, waits)
        dq.cnt += 16
        c = dq.cnt
        oap = out.ap if isinstance(out, V) else out
        iap = in_.ap if isinstance(in_, V) else in_
        q.ops.append(lambda e, oap=oap, iap=iap, sem=dq.sem, kw=kw: e.dma_start(out=oap, in_=iap, **kw).then_inc(sem, 16))
        self.n_inst += 1
        for b in rb:
            if b.r.get(dq, 0) < c:
                b.r[dq] = c
        for b in wb:
            b.w = (dq, c)
            b.r = {}

    def barrier(self):
        allq = list(self.q.values())
        for q in allq:
            assert not q.pending, q.name
        alld = [d for pool in self.dq.values() for d in pool]
        for q in allq:
            waits = {}
            for pq in allq + alld:
                if pq is not q and pq.cnt > 0 and q.seen.get(pq, 0) < pq.cnt:
                    waits[pq] = pq.cnt
            self._emit_waits(q, waits)

    def finish(self):
        self.barrier()
        nc = self.nc
        with nc.Block() as block:
            @block.sync
            def _(e):
                for f in self.q["sync"].ops:
                    f(e)

            @block.scalar
            def _(e):
                for f in self.q["scalar"].ops:
                    f(e)

            @block.vector
            def _(e):
                for f in self.q["vector"].ops:
                    f(e)

            @block.gpsimd
            def _(e):
                for f in self.q["gpsimd"].ops:
                    f(e)

            @block.tensor
            def _(e):
                for f in self.q["tensor"].ops:
                    f(e)
        self.es.close()


def build(cfg):
    NSEQ = cfg.get("nseq", 2)
    NBLK = cfg.get("nblk", cfg.get("SP", 4096) // 512)
    DO_SAMPLE = cfg.get("sample", True)
    NL = cfg.get("nl", 2)
    STAGE = cfg.get("stage", 99)
    FS = cfg.get("fs", 99)
    QS = cfg.get("qs", 99)
    ABL = cfg.get("abl", "")
    SP = cfg.get("SP", 4096)
    PAST = cfg.get("PAST", 4096)
    NKP = PAST // 128

    nc = bass.Bass("TRN2", target_bir_lowering=False)
    cx = contextlib.ExitStack()
    cx.enter_context(nc.allow_non_contiguous_dma(reason="small strided parameter / state transfers"))
    cx.enter_context(nc.allow_low_precision(reason="bf16 matmul operands, fp32 accumulation"))

    def din(name, shape, dt=F32):
        return nc.dram_tensor(name, list(shape), dt, kind="ExternalInput").ap()

    def dout(name, shape, dt=F32):
        return nc.dram_tensor(name, list(shape), dt, kind="ExternalOutput").ap()

    xp = din("xp", [2, SP, D]); xs = din("xs", [4, 16, D])
    ck = din("ck", [2, 4, PAST, 512]); cv = din("cv", [2, 4, PAST, 512]); clf = din("clf", [2, 4, PAST, 8])
    smc = din("smc", [2, 4, 4, 128, 128]); smn = din("smn", [2, 4, 4, 128]); smm = din("smm", [2, 4, 4])
    smconv = din("smconv", [2, 4, 3, D])
    norm_mix = din("norm_mix", [2, D]); w_in = din("w_in", [2, D, NIN]); b_gate = din("b_gate", [2, 3, D])
    sg_ln_g = din("sg_ln_g", [2, 512]); sg_ln_b = din("sg_ln_b", [2, 512]); sg_w = din("sg_w", [2, 4, 128, 128])
    sg_b = din("sg_b", [2, 4, 128]); b_fox_f = din("b_fox_f", [2, 8]); ml_conv_w = din("ml_conv_w", [2, 4, D])
    ml_conv_b = din("ml_conv_b", [2, D]); b_ml_i = din("b_ml_i", [2, 4]); b_ml_f = din("b_ml_f", [2, 4])
    ml_norm_g = din("ml_norm_g", [2, 4, 128]); w_br = din("w_br", [2, 3, 512, D]); w_out = din("w_out", [2, D, D])
    norm_ffn = din("norm_ffn", [2, D]); w_ffn_in = din("w_ffn_in", [2, D, 2 * DFF]); w_ffn_out = din("w_ffn_out", [2, DFF, D])
    norm_final = din("norm_final", [1, D])
    c_ident = din("c_ident", [128, 128]); c_tri = din("c_tri", [128, 128])

    y_p = dout("y_p", [2, SP, D]); y_s = dout("y_s", [4, 16, D])
    fk_p = dout("fk_p", [2, 2, SP, 512]); fv_p = dout("fv_p", [2, 2, SP, 512]); flf_p = dout("flf_p", [2, 2, SP, 8])
    mc_p = dout("mc_p", [2, 2, 4, 128, 128]); mn_p = dout("mn_p", [2, 2, 4, 128]); mm_p = dout("mm_p", [2, 2, 4])
    mconv_p = dout("mconv_p", [2, 2, 3, D])
    fk_s = dout("fk_s", [2, 4, 16, 512]); fv_s = dout("fv_s", [2, 4, 16, 512]); flf_s = dout("flf_s", [2, 4, 16, 8])
    mc_s = dout("mc_s", [2, 4, 4, 128, 128]); mn_s = dout("mn_s", [2, 4, 4, 128]); mm_s = dout("mm_s", [2, 4, 4])
    mconv_s = dout("mconv_s", [2, 4, 3, D]); sgv_s = dout("sgv_s", [2, 4, 16, 512])

    DBG = cfg.get("dbg", False)
    if DBG:
        dbg_fox = dout("dbg_fox", [64, 8, 512], BF16); dbg_sg = dout("dbg_sg", [128, 4, 512], BF16); dbg_ml = dout("dbg_ml", [128, 4, 512], BF16)
    kts = nc.dram_tensor("kts", [2, 32, 128, 512], BF16, kind="Internal").ap()
    vsc = nc.dram_tensor("vsc", [2, 32, 128, 576], BF16, kind="Internal").ap()
    NPL = 51
    wsc = nc.dram_tensor("wsc", [2, NPL, 128, SLAB], BF16, kind="Internal").ap()
    wsc_buf = [[Buf("wsc%d_%d" % (l, k)) for k in range(NPL)] for l in range(2)]
    kts_buf = [[Buf("kts%d_%d" % (l, k)) for k in range(32)] for l in range(2)]
    vsc_buf = [[Buf("vsc%d_%d" % (l, k)) for k in range(32)] for l in range(2)]

    c = Ctx(nc)
    sb = c.sbuf

    def rd(*xs_):
        return [x for x in xs_ if isinstance(x, V)]

    def A(x):
        return x.ap if isinstance(x, V) else x

    def mm(out, lhsT, rhs, start, stop, signal=None):
        c.op("tensor", lambda e: e.matmul(out.ap, lhsT=lhsT.ap, rhs=rhs.ap, start=start, stop=stop, skip_group_check=True),
             reads=[lhsT, rhs], writes=[out], signal=(stop if signal is None else signal))

    def tr(out, in_, ident, signal=True):
        c.op("tensor", lambda e: e.transpose(out.ap, in_.ap, ident.ap), reads=[in_, ident], writes=[out], signal=signal)

    def act(out, in_, func, bias=None, scale=1.0, accum=None, eng="scalar"):
        kw = {}
        if bias is not None:
            kw["bias"] = A(bias)
        if accum is not None:
            kw["accum_out"] = accum.ap
        c.op("scalar", lambda e: e.activation(out=out.ap, in_=in_.ap, func=func, scale=A(scale), **kw),
             reads=rd(in_, bias, scale), writes=[out] + ([accum] if accum is not None else []))

    def tt(out, in0, in1, op, eng="vector"):
        c.op(eng, lambda e: e.tensor_tensor(out=out.ap, in0=in0.ap, in1=in1.ap, op=op), reads=[in0, in1], writes=[out])

    def ts(out, in0, s1, s2, op0, op1=None, eng="vector"):
        if op1 is None:
            c.op(eng, lambda e: e.tensor_single_scalar(out=out.ap, in_=in0.ap, scalar=A(s1), op=op0), reads=rd(in0, s1), writes=[out])
        else:
            c.op(eng, lambda e: e.tensor_scalar(out=out.ap, in0=in0.ap, scalar1=A(s1), scalar2=A(s2), op0=op0, op1=op1),
                 reads=rd(in0, s1, s2), writes=[out])

    def stt(out, in0, scalar, in1, op0, op1, eng="vector"):
        c.op(eng, lambda e: e.scalar_tensor_tensor(out=out.ap, in0=in0.ap, scalar=A(scalar), in1=in1.ap, op0=op0, op1=op1),
             reads=rd(in0, scalar, in1), writes=[out])

    def cp(out, in_, eng="vector"):
        if eng == "scalar":
            c.op("scalar", lambda e: e.copy(out=out.ap, in_=in_.ap), reads=[in_], writes=[out])
        else:
            c.op(eng, lambda e: e.tensor_copy(out=out.ap, in_=in_.ap), reads=[in_], writes=[out])

    def memset(out, val, eng="vector"):
        c.op(eng, lambda e: e.memset(out.ap, val), writes=[out])

    def bc(v, shape, axis):
        return V(v.ap.unsqueeze(axis).to_broadcast(list(shape)), v.bufs)

    ident_f = sb("ident_f", [128, 128], F32); ident_b = sb("ident_b", [128, 128], BF16)
    tri_f = sb("tri_f", [128, 128], F32); tri_b = sb("tri_b", [128, 128], BF16)
    ones_f = sb("ones_f", [128, 128], F32)
    c.dma("sync", ident_f[:], c_ident); c.dma("sync", tri_f[:], c_tri)
    cp(ident_b[:], ident_f[:]); cp(tri_b[:], tri_f[:]); memset(ones_f[:], 1.0)

    gmix = sb("gmix", [128, 2, 8], F32); gffn = sb("gffn", [128, 2, 8], F32)
    bgate = sb("bgate", [128, 2, 3, 8], F32)
    wconv = sb("wconv", [128, 2, 4, 8], F32); bconv = sb("bconv", [128, 2, 8], F32)
    gml = sb("gml", [128, 2, 4], F32)
    lng = sb("lng", [128, 2, 512], F32); lnb = sb("lnb", [128, 2, 512], F32)
    bsbc = sb("bsbc", [128, 2, 512], F32)
    wsT = sb("wsT", [128, 2, 4, 128], BF16)
    bsm = sb("bsm", [128, 2, 16], F32)
    wsm = sb("wsm", [128, 2, 8, 16], BF16)
    nfin = sb("nfin", [128, D], F32)
    for l in range(2):
        c.dma("sync", gmix[:, l, :], norm_mix[l].rearrange("(c p) -> p c", p=128))
        c.dma("sync", gffn[:, l, :], norm_ffn[l].rearrange("(c p) -> p c", p=128))
        for b in range(3):
            c.dma("sync", bgate[:, l, b, :], b_gate[l, b].rearrange("(c p) -> p c", p=128))
        for j in range(4):
            c.dma("sync", wconv[:, l, j, :], ml_conv_w[l, j].rearrange("(c p) -> p c", p=128))
        c.dma("sync", bconv[:, l, :], ml_conv_b[l].rearrange("(c p) -> p c", p=128))
        c.dma("sync", gml[:, l, :], ml_norm_g[l].rearrange("h p -> p h"))
        c.dma("sync", lng[:, l, :], sg_ln_g[l:l + 1, :].partition_broadcast(128))
        c.dma("sync", lnb[:, l, :], sg_ln_b[l:l + 1, :].partition_broadcast(128))
        c.dma("sync", bsbc[:, l, :], sg_b.rearrange("l g t -> l (g t)")[l:l + 1, :].partition_broadcast(128))
        c.dma("sync", bsm[:, l, 0:8], b_fox_f[l:l + 1, :].partition_broadcast(128))
        c.dma("sync", bsm[:, l, 8:12], b_ml_i[l:l + 1, :].partition_broadcast(128))
        c.dma("sync", bsm[:, l, 12:16], b_ml_f[l:l + 1, :].partition_broadcast(128))
        wv = w_in[l].rearrange("(kc p) n -> p kc n", p=128)
        c.dma("gpsimd", wsm[:, l, :, 0:8], wv[:, :, C_FF:C_FF + 8])
        c.dma("gpsimd", wsm[:, l, :, 8:16], wv[:, :, C_MI:C_MI + 8])
    c.dma("sync", nfin[:], norm_final.partition_broadcast(128))

    bank = [c.psum("bank%d" % i, [128, 512], F32) for i in range(8)]

    def bkb(i):
        return bank[i].t[:, :].bitcast(BF16)

    stg = [sb("stg%d" % i, [128, D], F32) for i in range(2)]
    for l in range(2):
        wtv = V(stg[0].t[:, 0:512].rearrange("p (g s) -> p g s", g=4), [stg[0].buf])
        c.dma("sync", wtv, sg_w[l].rearrange("g t s -> t g s"))
        for g in range(4):
            mm(bank[g][:, 0:128], stg[0][:, g * 128:(g + 1) * 128], ident_f[:], True, True)
            tt(wsT[:, l, g, :], bank[g][:, 0:128], tri_f[:], ALU.mult)

    X = sb("X", [128, 4, D], F32)
    xnT = sb("xnT", [128, 8, 512], BF16)
    slabs = [sb("slab%d" % i, [128, SLAB], BF16) for i in range(3)]
    xs_bs = [sb("xs_b%d" % i, [128, D], BF16) for i in range(2)]
    st_col = sb("st_col", [128, 8], F32)
    uT = sb("uT", [128, 4, 512], BF16); vb = sb("vb", [128, 4, 512], BF16); ysgT = sb("ysgT", [128, 4, 512], BF16)
    sga = sb("sga", [128, 512], F32); sgt = sb("sgt", [128, 512], F32)
    qT = sb("qT", [128, 4, 512], BF16); kTb = sb("kTb", [128, 4, 512], BF16)
    vaug = sb("vaug", [128, 4, 8, 72], BF16)
    yfoxT = sb("yfoxT", [64, 8, 512], BF16)
    zz = sb("zz", [128, 4, 16], F32); lsg = sb("lsg", [128, 4, 16], F32)
    zt = [sb("zt%d" % i, [128, 16], F32) for i in range(3)]
    negF = [sb("negF%d" % l, [128, 33, 8], F32) for l in range(2)]
    Rrun = [sb("Rrun%d" % l, [128, 8], F32) for l in range(2)]
    biasqs = [sb("biasq%d" % i, [128, 33, 8], F32) for i in range(4)]
    arena = sb("arena", [128, 12800], BF16)
    ar = arena.t
    def aview(off, n, name):
        t_ = T(ar[:, off:off + n], name)
        return t_
    ktg = [aview(0, 2048, "ktg0"), aview(2048, 2048, "ktg1")]
    vg = [aview(4096, 2304, "vg0"), aview(6400, 2304, "vg1")]
    kst = aview(8704, 2048, "kst"); vst = aview(10752, 2048, "vst")
    hT = aview(0, 11264, "hT")
    mixTt = aview(0, 4096, "mixT")
    pT = [sb("pT%d" % i, [128, 4, 128], BF16) for i in range(4)]
    vws = [sb("vw%d" % i, [128, 8, 72], BF16) for i in range(2)]
    ptot = sb("ptot", [128, 33, 8], F32)
    rawT = [sb("rawT%d" % i, [128, 4, 131], F32) for i in range(2)]
    qmT = sb("qmT", [128, 4, 512], BF16); kmT = sb("kmT", [128, 4, 512], BF16)
    vml = sb("vml", [128, 4, 512], BF16); omt = sb("omt", [128, 4, 512], BF16)
    vpa = sb("vpa", [128, 4, 136], BF16); ktok = sb("ktok", [128, 4, 128], BF16)
    sTm = sb("sTm", [128, 4, 128], BF16); Cbf = sb("Cbf", [128, 4, 136], BF16)
    ymt = sb("ymt", [128, 512], BF16); ymlT = sb("ymlT", [128, 4, 512], BF16)
    Cst = [sb("Cst%d" % l, [128, 4, 132], F32) for l in range(2)]
    mrep = [sb("mrep%d" % l, [128, 4], F32) for l in range(2)]
    halo = [sb("halo%d" % l, [128, 8, 3], F32) for l in range(2)]
    sm = [sb("sm%d" % i, [128, 16], F32) for i in range(8)]
    dg = sb("dg", [4, 4], F32)
    gsb = [sb("gsb%d" % i, [128, 512], BF16) for i in range(6)]
    tmf = [sb("tmf%d" % i, [128, 512], F32) for i in range(4)]
    hh = tmf[0]; rsum = tmf[1]; bcs = tmf[2]
    saf = [sga, sgt]
    yout = stg
    mixv = mixTt.t[:, :].rearrange("p (k w) -> p k w", k=8)

    memset(vaug[:, :, :, 64:65], 1.0)
    for i in range(2):
        memset(V(vg[i].t[:, :].rearrange("p (k h e) -> p k h e", k=4, h=8)[:, :, :, 64:65], [vg[i].buf]), 1.0)

    class WS:
        def __init__(self):
            self.plan = []
            self.issued = 0
            self.taken = 0

        def add(self, parts):
            self.plan.append(parts)

        def _issue(self, i):
            slot = slabs[i % len(slabs)]
            bi_ = i // (NPL * NL)
            l_ = (i // NPL) % NL
            li_ = i % NPL
            if bi_ > 0 and "nowsc" not in ABL and "nowdma" not in ABL:
                c.dma("gpsimd", slot[:], V(wsc[l_, li_], [wsc_buf[l_][li_]]))
                return
            for (off, shape, src) in self.plan[i]:
                n = 1
                for s_ in shape[1:]:
                    n *= s_
                dst = slot.t[0:shape[0], off:off + n]
                if len(shape) == 3:
                    dst = dst.rearrange("p (a b) -> p a b", a=shape[1])
                elif len(shape) == 4:
                    dst = dst.rearrange("p (a b d) -> p a b d", a=shape[1], b=shape[2])
                if "nowdma" not in ABL:
                    c.dma("gpsimd", V(dst, [slot.buf]), src)
            if "nowsc" not in ABL and "nowdma" not in ABL and len(self.plan) > NPL * NL:
                c.dma("gpsimd", V(wsc[l_, li_], [wsc_buf[l_][li_]]), slot[:])

        def take(self):
            i = self.taken
            self.taken += 1
            while self.issued < len(self.plan) and self.issued <= i + len(slabs) - 2:
                self._issue(self.issued)
                self.issued += 1
            assert self.issued > i
            return slabs[i % len(slabs)]

    ws = WS()

    def plan_layer(l):
        wv = w_in[l].rearrange("(kc p) n -> p kc n", p=128)
        for c0 in (C_SGU, C_SGV, C_FQ, C_FK, C_FV, C_MQK, C_MQK + 512, C_MV, C_MO):
            ws.add([(0, (128, 8, 512), wv[:, :, c0:c0 + 512])])
        for j in range(8):
            ws.add([(b * 1024, (128, 8, 128), wv[:, :, b * 1024 + j * 128: b * 1024 + (j + 1) * 128]) for b in range(3)])
            ws.add([(0, (128, 4, 128), w_br[l, 0].rearrange("(c p) n -> p c n", p=128)[:, :, j * 128:(j + 1) * 128]),
                    (512, (128, 4, 128), w_br[l, 2].rearrange("(c p) n -> p c n", p=128)[:, :, j * 128:(j + 1) * 128]),
                    (1024, (64, 8, 128), w_br[l, 1].rearrange("(h p) n -> p h n", p=64)[:, :, j * 128:(j + 1) * 128])])
        wo = w_out[l].rearrange("(kc p) n -> p kc n", p=128)
        for nh in range(2):
            ws.add([(0, (128, 8, 512), wo[:, :, nh * 512:(nh + 1) * 512])])
        wf = w_ffn_in[l].rearrange("(kc p) n -> p kc n", p=128)
        for jg in range(6):
            ncol = 512 if jg < 5 else 256
            ws.add([(0, (128, 8, ncol), wf[:, :, jg * 512: jg * 512 + ncol])])
            ws.add([(0, (128, 8, ncol), wf[:, :, DFF + jg * 512: DFF + jg * 512 + ncol])])
        wfo = w_ffn_out[l].rearrange("(j p) n -> p j n", p=128)
        for nh in range(2):
            for js in range(6):
                nj = 4 if js < 5 else 2
                ws.add([(0, (128, nj, 512), wfo[:, js * 4: js * 4 + nj, nh * 512:(nh + 1) * 512])])

    def rstd_from_ss(ss, n, Lp, tmp):
        ts(ss, ss, 1.0 / n, EPS, ALU.mult, ALU.add)
        act(ss, ss, AF.Ln)
        act(ss, ss, AF.Exp, scale=-0.5)

    def norm_to_featT(l, gcol, L, ntile):
        W = ntile * L
        for t in range(ntile):
            ss = st_col[0:L, 0:1]
            xs_b = xs_bs[t % 2]
            act(xs_b[0:L, :], X[0:L, t, :], AF.Square, accum=ss)
            rstd_from_ss(ss, D, L, None)
            ts(xs_b[0:L, :], X[0:L, t, :], ss, None, ALU.mult)
            pb = bkb(t % 2)
            for kc in range(8):
                tr(V(pb[:, kc * L:(kc + 1) * L], [bank[t % 2].buf]), xs_b[0:L, kc * 128:(kc + 1) * 128], ident_b[0:L, 0:L], signal=(kc == 7))
            tt(xnT[:, :, t * L:(t + 1) * L], V(pb[:, 0:8 * L].rearrange("p (c j) -> p c j", c=8), [bank[t % 2].buf]),
               bc(gcol, [128, 8, L], 2), ALU.mult)

    def proj_F(slab, cols, W, bk):
        sv = slab.t[:, :].rearrange("p (a b) -> p a b", a=8)
        for kc in range(8):
            mm(bank[bk][:, 0:W], V(sv[:, kc, cols[0]:cols[1]], [slab.buf]), xnT[:, kc, 0:W], kc == 0, kc == 7)

    def proj_T(slab, ncol, t, L, bk, n0=0):
        sv = slab.t[:, 0:8 * ncol].rearrange("p (a b) -> p a b", a=8)
        for kc in range(8):
            mm(bank[bk][0:L, 0:ncol], xnT[:, kc, t * L:(t + 1) * L], V(sv[:, kc, :], [slab.buf]), kc == 0, kc == 7)

    def layer_block(l, tiles):
        L = tiles[0]["L"]
        NT = len(tiles)
        W = NT * L
        norm_to_featT(l, gmix[:, l, :], L, NT)
        if STAGE < 1:
            return
        s_u = ws.take()
        for cc in range(4):
            bk = cc % 4
            proj_F(s_u, (cc * 128, (cc + 1) * 128), W, bk)
            act(uT[:, cc, 0:W], bank[bk][:, 0:W], AF.Gelu)
        s_v = ws.take()
        for t, tl in enumerate(tiles):
            bk = 4 + t % 2
            proj_T(s_v, 512, t, L, bk)
            sacc = st_col[0:L, 1:2]
            act(sga[0:L, :], bank[bk][0:L, :], AF.Gelu, accum=sacc)
            ts(sacc, sacc, -1.0 / 512, None, ALU.mult)
            ssq = st_col[0:L, 2:3]
            act(sgt[0:L, :], sga[0:L, :], AF.Square, bias=sacc, accum=ssq)
            rstd_from_ss(ssq, 512, L, None)
            ts(sga[0:L, :], sga[0:L, :], sacc, ssq, ALU.add, ALU.mult)
            tt(sga[0:L, :], sga[0:L, :], lng[0:L, l, :], ALU.mult)
            if tl["kind"] == "s":
                tt(sgt[0:L, :], sga[0:L, :], lnb[0:L, l, :], ALU.add)
                c.dma("sync", sgv_s[l, tl["b"]], sgt[0:L, :])
                cp(vb[0:L, t, :], sgt[0:L, :])
            else:
                tt(vb[0:L, t, :], sga[0:L, :], lnb[0:L, l, :], ALU.add)
            bk2 = 6 + t % 2
            for g in range(4):
                mm(bank[bk2][:, g * L:(g + 1) * L], vb[0:L, t, g * 128:(g + 1) * 128], wsT[0:L, l, g, 0:L], True, True, signal=(g == 3))
            mx = V(bank[bk2].t[:, 0:4 * L].rearrange("p (g j) -> p g j", g=4), [bank[bk2].buf])
            tmv = V(tmf[0].t[:, 0:4 * L].rearrange("p (g j) -> p g j", g=4), [tmf[0].buf])
            tt(tmv, mx, V(bsbc.t[:, l, :].rearrange("p (g j) -> p g j", g=4)[:, :, 0:L], [bsbc.buf]), ALU.add)
            tt(ysgT[:, :, t * L:(t + 1) * L], tmv, uT[:, :, t * L:(t + 1) * L], ALU.mult)
        if STAGE < 2:
            return
        s_q = ws.take()
        for cc in range(4):
            proj_F(s_q, (cc * 128, (cc + 1) * 128), W, cc)
            act(qT[:, cc, 0:W], bank[cc][:, 0:W], AF.Copy, scale=0.125)
        s_k = ws.take()
        for cc in range(4):
            proj_F(s_k, (cc * 128, (cc + 1) * 128), W, 4 + cc)
            cp(kTb[:, cc, 0:W], bank[4 + cc][:, 0:W])
        for t, tl in enumerate(tiles):
            proj_T(s_k, 512, t, L, t % 4)
            st = stg[t % 2]
            cp(st[0:L, 0:512], bank[t % 4][0:L, :], eng="scalar")
            dst = (fk_p[l, tl["b"], tl["pos"]:tl["pos"] + L, :] if tl["kind"] == "p" else fk_s[l, tl["b"]])
            c.dma("sync", dst, st[0:L, 0:512])
        s_vv = ws.take()
        for t, tl in enumerate(tiles):
            bk = 4 + t % 4
            proj_T(s_vv, 512, t, L, bk)
            st = stg[t % 2]
            cp(st[0:L, 512:1024], bank[bk][0:L, :], eng="scalar")
            dst = (fv_p[l, tl["b"], tl["pos"]:tl["pos"] + L, :] if tl["kind"] == "p" else fv_s[l, tl["b"]])
            c.dma("sync", dst, st[0:L, 512:1024])
            cp(vaug[0:L, t, :, 0:64], V(bank[bk].t[0:L, :].rearrange("p (h e) -> p h e", h=8), [bank[bk].buf]))
        for t, tl in enumerate(tiles):
            bk = t % 4
            for kc in range(8):
                mm(bank[bk][0:L, 0:16], xnT[:, kc, t * L:(t + 1) * L], wsm[:, l, kc, :], kc == 0, kc == 7)
            tt(zz[0:L, t, :], bank[bk][0:L, 0:16], bsm[0:L, l, :], ALU.add)
            stt(zt[0][0:L, :], zz[0:L, t, :], -1.0, zz[0:L, t, :], ALU.mult, ALU.max)
            act(zt[1][0:L, :], zt[0][0:L, :], AF.Exp, scale=-1.0)
            act(zt[1][0:L, :], zt[1][0:L, :], AF.Ln, bias=1.0)
            ts(zt[2][0:L, :], zz[0:L, t, :], 0.0, None, ALU.min)
            tt(lsg[0:L, t, :], zt[2][0:L, :], zt[1][0:L, :], ALU.subtract)
            dst = (flf_p[l, tl["b"], tl["pos"]:tl["pos"] + L, :] if tl["kind"] == "p" else flf_s[l, tl["b"]])
            c.dma("sync", dst, lsg[0:L, t, 0:8])
        if STAGE < 3:
            return
        if tiles[0]["kind"] == "p":
            for t, tl in enumerate(tiles):
                fox_prologue(l, t, tl, L, t, 4 + t)
        for t, tl in enumerate(tiles):
            fox_tile(l, t, tl, L)
        if STAGE < 4:
            return
        s_mq = ws.take()
        s_mk = ws.take()
        for ci in range(8):
            slab = s_mq if ci < 4 else s_mk
            cc = ci % 4
            bk = ci % 4
            proj_F(slab, (cc * 128, (cc + 1) * 128), W, bk)
            rw = rawT[ci % 2]
            cp(rw[:, 0:NT, 3:3 + L], V(bank[bk].t[:, 0:W].rearrange("p (t j) -> p t j", t=NT), [bank[bk].buf]), eng="scalar")
            for t, tl in enumerate(tiles):
                if tl["kind"] == "s":
                    c.dma("sync", rw[:, t, 0:3], smconv[l, tl["b"], :, ci * 128:(ci + 1) * 128].rearrange("j p -> p j"))
                elif t == 0:
                    if tl["first"]:
                        memset(rw[:, 0, 0:3], 0.0)
                    else:
                        cp(rw[:, 0, 0:3], halo[l][:, ci, :])
                else:
                    cp(rw[:, t, 0:3], rw[:, t - 1, L:L + 3])
            for t, tl in enumerate(tiles):
                if tl["kind"] == "s" or tl["last"]:
                    dst = (mconv_p if tl["kind"] == "p" else mconv_s)[l, tl["b"], :, ci * 128:(ci + 1) * 128].rearrange("j p -> p j")
                    c.dma("sync", dst, rw[:, t, L:L + 3])
            if tiles[-1]["kind"] == "p" and not tiles[-1]["last"]:
                cp(halo[l][:, ci, :], rw[:, NT - 1, L:L + 3])
            ca_t = stg[ci % 2]
            class _CA:
                def __getitem__(self_, idx):
                    return V(ca_t.t[:, 0:512].rearrange("p (t j) -> p t j", t=4)[idx], [ca_t.buf])
            ca = _CA()
            ts(ca[:, 0:NT, 0:L], rw[:, 0:NT, 3:3 + L], wconv[:, l, 3, ci:ci + 1], None, ALU.mult)
            for j in (2, 1, 0):
                stt(ca[:, 0:NT, 0:L], rw[:, 0:NT, j:j + L], wconv[:, l, j, ci:ci + 1], ca[:, 0:NT, 0:L], ALU.mult, ALU.add)
            if ci < 4:
                act(V(qmT.t[:, cc, 0:W].rearrange("p (t j) -> p t j", t=NT), [qmT.buf]), ca[:, 0:NT, 0:L], AF.Silu, bias=bconv[:, l, ci:ci + 1])
            else:
                act(ca[:, 0:NT, 0:L], ca[:, 0:NT, 0:L], AF.Silu, bias=bconv[:, l, ci:ci + 1])
                ts(V(kmT.t[:, cc, 0:W].rearrange("p (t j) -> p t j", t=NT), [kmT.buf]), ca[:, 0:NT, 0:L], 128.0 ** -0.5, None, ALU.mult)
        s_mv = ws.take()
        for t in range(NT):
            bk = 4 + t % 4
            proj_T(s_mv, 512, t, L, bk)
            cp(vml[0:L, t, :], bank[bk][0:L, :])
        s_mo = ws.take()
        for t in range(NT):
            bk = t % 4
            proj_T(s_mo, 512, t, L, bk)
            act(omt[0:L, t, :], bank[bk][0:L, :], AF.Sigmoid)
        if STAGE < 5:
            return
        for t, tl in enumerate(tiles):
            if "noml" not in ABL:
                ml_tile(l, t, tl, L)
        if DBG and l == 0:
            c.dma("sync", dbg_fox, yfoxT[:]); c.dma("sync", dbg_sg, ysgT[:]); c.dma("sync", dbg_ml, ymlT[:])
        if STAGE < 6:
            return
        for j in range(8):
            s_g = ws.take()
            s_b = ws.take()
            gv = s_g.t[:, 0:3072].rearrange("p (b kc n) -> p b kc n", b=3, kc=8)
            gs = []
            for b in range(3):
                n_ = 3 * j + b
                gbk = bank[n_ % 4]
                for kc in range(8):
                    mm(gbk[:, 0:W], V(gv[:, b, kc, :], [s_g.buf]), xnT[:, kc, 0:W], kc == 0, kc == 7)
                g_ = gsb[n_ % 6]
                act(g_[:, 0:W], gbk[:, 0:W], AF.Sigmoid, bias=bgate[:, l, b, j:j + 1])
                gs.append(g_)
            w0 = s_b.t[:, 0:512].rearrange("p (c n) -> p c n", c=4)
            w2 = s_b.t[:, 512:1024].rearrange("p (c n) -> p c n", c=4)
            w1 = s_b.t[0:64, 1024:2048].rearrange("p (h n) -> p h n", h=8)
            bb = [bank[4 + (3 * j + b) % 4] for b in range(3)]
            for cc in range(4):
                mm(bb[0][:, 0:W], V(w0[:, cc, :], [s_b.buf]), ysgT[:, cc, 0:W], cc == 0, cc == 3)
            for h in range(8):
                mm(bb[1][:, 0:W], V(w1[:, h, :], [s_b.buf]), yfoxT[0:64, h, 0:W], h == 0, h == 7)
            for cc in range(4):
                mm(bb[2][:, 0:W], V(w2[:, cc, :], [s_b.buf]), ymlT[:, cc, 0:W], cc == 0, cc == 3)
            tA = tmf[(2 * j) % 4]
            tB = tmf[(2 * j + 1) % 4]
            tt(tA[:, 0:W], bb[0][:, 0:W], gs[0][:, 0:W], ALU.mult)
            tt(tB[:, 0:W], bb[1][:, 0:W], gs[1][:, 0:W], ALU.mult)
            tt(tA[:, 0:W], tA[:, 0:W], tB[:, 0:W], ALU.add)
            tt(tB[:, 0:W], bb[2][:, 0:W], gs[2][:, 0:W], ALU.mult)
            tt(V(mixv[:, j, 0:W], [mixTt.buf]), tA[:, 0:W], tB[:, 0:W], ALU.add)
        for nh in range(2):
            s_o = ws.take()
            sv = s_o.t[:, :].rearrange("p (a b) -> p a b", a=8)
            for t in range(NT):
                bk = (nh * NT + t) % 8
                for kc in range(8):
                    mm(bank[bk][0:L, :], V(mixv[:, kc, t * L:(t + 1) * L], [mixTt.buf]), V(sv[:, kc, :], [s_o.buf]), kc == 0, kc == 7)
                tt(X[0:L, t, nh * 512:(nh + 1) * 512], X[0:L, t, nh * 512:(nh + 1) * 512], bank[bk][0:L, :], ALU.add)
        if STAGE < 7:
            return
        norm_to_featT(l, gffn[:, l, :], L, NT)
        c.barrier()
        hv = hT.t[:, :].rearrange("p (j w) -> p j w", j=22)
        for jg in range(6):
            s_a = ws.take()
            s_bb = ws.take()
            nj = 4 if jg < 5 else 2
            ncol = nj * 128
            for jj in range(nj):
                j = jg * 4 + jj
                bka, bkb_ = (2 * jj) % 8, (2 * jj + 1) % 8
                sva = s_a.t[:, 0:8 * ncol].rearrange("p (a b) -> p a b", a=8)
                svb = s_bb.t[:, 0:8 * ncol].rearrange("p (a b) -> p a b", a=8)
                for kc in range(8):
                    mm(bank[bka][:, 0:W], V(sva[:, kc, jj * 128:(jj + 1) * 128], [s_a.buf]), xnT[:, kc, 0:W], kc == 0, kc == 7)
                for kc in range(8):
                    mm(bank[bkb_][:, 0:W], V(svb[:, kc, jj * 128:(jj + 1) * 128], [s_bb.buf]), xnT[:, kc, 0:W], kc == 0, kc == 7)
                sa = saf[j % 2]
                act(sa[:, 0:W], bank[bka][:, 0:W], AF.Silu)
                tt(V(hv[:, j, 0:W], [hT.buf]), sa[:, 0:W], bank[bkb_][:, 0:W], ALU.mult)
        for nh in range(2):
            for js in range(6):
                s_f = ws.take()
                nj = 4 if js < 5 else 2
                sv = s_f.t[:, 0:nj * 512].rearrange("p (a b) -> p a b", a=nj)
                for t in range(NT):
                    bk = nh * 4 + t
                    for jj in range(nj):
                        j = js * 4 + jj
                        mm(bank[bk][0:L, :], V(hv[:, j, t * L:(t + 1) * L], [hT.buf]), V(sv[:, jj, :], [s_f.buf]), j == 0, j == 21, signal=(jj == nj - 1))
            for t in range(NT):
                bk = nh * 4 + t
                tt(X[0:L, t, nh * 512:(nh + 1) * 512], X[0:L, t, nh * 512:(nh + 1) * 512], bank[bk][0:L, :], ALU.add)
        c.barrier()

    def fox_prologue(l, t, tl, L, bk_c, bk_t):
        kb = tl["kb"]
        b = tl["b"]
        R = Rrun[l]
        nF = negF[l]
        biasq = biasqs[t]
        if tl["kind"] == "s":
            clt = stg[1]
            cl3 = clt.t[:, 0:NKP * 8].rearrange("p (k h) -> p k h", k=NKP)
            for q4 in range((NKP + 7) // 8):
                k1 = min(NKP, q4 * 8 + 8)
                c.dma("sync", V(cl3[:, q4 * 8:k1, :], [clt.buf]),
                      clf[l, b, q4 * 1024:k1 * 128, :].rearrange("(k p) h -> p k h", p=128))
            mm(bank[6][:, 0:NKP * 8], tri_f[:], clt[:, 0:NKP * 8], True, True)
            mm(bank[7][:, 0:NKP * 8], ones_f[:], clt[:, 0:NKP * 8], True, True)
            memset(ptot[:, 0, :], 0.0)
            tot3 = bank[7].t[:, 0:NKP * 8].rearrange("p (k h) -> p k h", k=NKP)
            for k in range(NKP):
                tt(ptot[:, k + 1, :], ptot[:, k, :], V(tot3[:, k, :], [bank[7].buf]), ALU.add)
            cin3 = V(bank[6].t[:, 0:NKP * 8].rearrange("p (k h) -> p k h", k=NKP), [bank[6].buf])
            stt(nF[:, 0:NKP, :], cin3, -1.0, ptot[:, 0:NKP, :], ALU.mult, ALU.subtract)
            cp(R[:, :], ptot[:, NKP, :])
        elif tl["first"]:
            memset(R[:, :], 0.0)
        mm(bank[bk_c][0:L, 256:272], tri_f[0:L, 0:L], lsg[0:L, t, :], True, True)
        mm(bank[bk_t][:, 256:272], ones_f[0:L, :], lsg[0:L, t, :], True, True)
        stt(nF[0:L, kb, :], bank[bk_c][0:L, 256:264], -1.0, R[0:L, :], ALU.mult, ALU.subtract)
        tt(R[:, :], R[:, :], bank[bk_t][:, 256:264], ALU.add)
        nkb = kb + 1
        tt(biasq[:, 0:nkb, :], nF[:, 0:nkb, :], bc(R[:, :], [128, nkb, 8], 1), ALU.add)
        act(biasq[:, 0:nkb, :], biasq[:, 0:nkb, :], AF.Exp)
        if tl["kind"] == "p" and not tl["last"]:
            c.dma("sync", V(kts[l, kb].rearrange("p (c n) -> p c n", c=4), [kts_buf[l][kb]]), kTb[:, :, t * L:(t + 1) * L])
            c.dma("sync", V(vsc[l, kb], [vsc_buf[l][kb]]), V(vaug.t[:, t, :, :].rearrange("p h e -> p (h e)"), [vaug.buf]))

    def fox_tile(l, t, tl, L):
        kb = tl["kb"]
        b = tl["b"]
        biasq = biasqs[t]
        if tl["kind"] == "s":
            fox_prologue(l, t, tl, L, 6, 7)
        if FS < 2:
            return
        oacc = [bank[0], bank[1]]

        items = []

        def load_group(g):
            k0 = g * 4
            nk = min(4, kb - k0)
            kt = ktg[g % 2]
            vgt = vg[g % 2]
            ktv = kt.t[:, :].rearrange("p (k c n) -> p k c n", k=4, c=4)
            vgv = vgt.t[:, :].rearrange("p (k h e) -> p k h e", k=4, h=8)
            if "nogl" in ABL:
                pass
            elif tl["kind"] == "p":
                c.dma("sync", V(ktv[:, 0:nk], [kt.buf]),
                      V(kts[l, k0:k0 + nk].rearrange("k p (c n) -> p k c n", c=4), [kts_buf[l][k0 + i] for i in range(nk)]))
                c.dma("sync", V(vgt.t[:, 0:nk * 576].rearrange("p (k f) -> p k f", k=nk), [vgt.buf]),
                      V(vsc[l, k0:k0 + nk].rearrange("k p f -> p k f"), [vsc_buf[l][k0 + i] for i in range(nk)]))
            else:
                ksv = kst.t[:, :].rearrange("p (k n) -> p k n", k=4)
                c.dma("gpsimd", V(ksv, [kst.buf]), ck[l, b, k0 * 128:(k0 + 4) * 128, :].rearrange("(k p) n -> p k n", p=128))
                c.dma("gpsimd", V(vst.t[:, :].rearrange("p (k n) -> p k n", k=4), [vst.buf]),
                      cv[l, b, k0 * 128:(k0 + 4) * 128, :].rearrange("(k p) n -> p k n", p=128))
                for k in range(4):
                    pb_i = 6 + (k % 2)
                    pb = bkb(pb_i)
                    for pr in range(4):
                        tr(V(pb[:, pr * 128:(pr + 1) * 128], [bank[pb_i].buf]), V(ksv[:, k, pr * 128:(pr + 1) * 128], [kst.buf]),
                           ident_b[:, :], signal=(pr == 3))
                    cp(V(ktv[:, k, :, :], [kt.buf]), V(pb[:, 0:512].rearrange("p (c n) -> p c n", c=4), [bank[pb_i].buf]), eng="scalar")
                cp(V(vgv[:, :, :, 0:64], [vgt.buf]), V(vst.t[:, :].rearrange("p (k h e) -> p k h e", k=4, h=8), [vst.buf]))
                memset(V(vgv[:, :, :, 64:65], [vgt.buf]), 1.0)
            for k in range(nk):
                items.append((ktv[:, k], [kt.buf], vgv[:, k], [vgt.buf], 128, k0 + k, False))

        ngrp = (kb + 3) // 4
        state = {"first": True}

        def qk_exp(it, slot):
            kview, kbuf, vview, vbuf, Lk, kbi, diag = it
            for h in range(8):
                p, half = h // 2, h % 2
                sb_ = bank[2 + 2 * slot + half]
                mm(sb_[0:Lk, p * L:(p + 1) * L], V(kview[64 * half:64 * half + 64, p, 0:Lk], kbuf),
                   qT[64 * half:64 * half + 64, p, t * L:(t + 1) * L], True, True)
            if QS < 1:
                return
            for half in range(2):
                sb_ = bank[2 + 2 * slot + half]
                if "noexp" not in ABL:
                    act(pT[2 * slot + half][0:Lk, :, 0:L],
                        V(sb_.t[0:Lk, 0:4 * L].rearrange("p (c j) -> p c j", c=4), [sb_.buf]), AF.Exp)
            vw = vws[slot]
            tt(vw[0:Lk, :, 0:65], V(vview[0:Lk, :, 0:65], vbuf), bc(biasq[0:Lk, kbi, :], [Lk, 8, 65], 2), ALU.mult)
            if QS < 2:
                return
            if diag:
                for hb in range(2):
                    pvv = pT[2 * slot + hb][0:Lk, :, 0:L]
                    tt(pvv, pvv, bc(tri_b[0:Lk, 0:L], [Lk, 4, L], 1), ALU.mult)

        def pv_mm(it, slot, last):
            kview, kbuf, vview, vbuf, Lk, kbi, diag = it
            if QS < 3:
                return
            for h in range(8):
                ob = oacc[h // 4]
                pv = pT[2 * slot + h % 2][0:Lk, h // 2, 0:L]
                mm(ob[0:65, (h % 4) * L:(h % 4 + 1) * L], vws[slot][0:Lk, h, 0:65], pv,
                   start=(state["first"] and h % 4 == 0), stop=last, signal=(last and h % 4 == 3))
            state["first"] = False

        pipe = {"prev": None, "idx": 0}

        def push(it):
            slot = pipe["idx"] % 2
            qk_exp(it, slot)
            if pipe["prev"] is not None:
                pv_mm(pipe["prev"][0], pipe["prev"][1], False)
            pipe["prev"] = (it, slot)
            pipe["idx"] += 1

        push((kTb.t[:, :, t * L:(t + 1) * L], [kTb.buf], vaug.t[:, t], [vaug.buf], L, kb, True))
        for g in range(ngrp if (FS >= 4 and "nopast" not in ABL) else 0):
            del items[:]
            load_group(g)
            for it in list(items):
                push(it)
        pv_mm(pipe["prev"][0], pipe["prev"][1], True)
        if FS < 3:
            c.op("vector", lambda e: e.memset(xs_bs[0].t[:, 0:4], 0.0), reads=[oacc[0][:], oacc[1][:]], writes=[xs_bs[0][:]])
            return
        for hb in range(2 if "notail" not in ABL else 0):
            ob = oacc[hb]
            c.op("vector", lambda e, ob=ob: e.reciprocal(out=rsum.t[64:65, 0:4 * L], in_=ob.t[64:65, 0:4 * L]), reads=[ob[:]], writes=[rsum[:]])
            mm(bank[6 + hb][0:64, 0:4 * L], ones_f[64:65, 0:64], rsum[64:65, 0:4 * L], True, True)
            cp(bcs[0:64, 0:4 * L], bank[6 + hb][0:64, 0:4 * L])
            tt(yfoxT[0:64, hb * 4:(hb + 1) * 4, t * L:(t + 1) * L],
               V(ob.t[0:64, 0:4 * L].rearrange("p (h j) -> p h j", h=4), [ob.buf]),
               V(bcs.t[0:64, 0:4 * L].rearrange("p (h j) -> p h j", h=4), [bcs.buf]), ALU.mult)

    def ml_tile(l, t, tl, L):
        b = tl["b"]
        Cs = Cst[l]
        mr = mrep[l]
        if tl["kind"] == "s":
            c.dma("sync", Cs[:, :, 0:128], smc[l, b].rearrange("h d e -> d h e"))
            c.dma("sync", Cs[:, :, 128], smn[l, b].rearrange("h d -> d h"))
            c.dma("sync", mr[:, :], smm[l, b:b + 1, :].partition_broadcast(128))
        elif tl["first"]:
            memset(Cs[:, :, :], 0.0)
            memset(mr[:, :], 0.0)
        mm(bank[6][0:L, 256:272], tri_f[0:L, 0:L], lsg[0:L, t, :], True, True)
        mm(bank[7][:, 256:272], ones_f[0:L, :], lsg[0:L, t, :], True, True)
        bcol = sm[0]
        cp(bcol[0:L, 0:4], bank[6][0:L, 268:272])
        bend = sm[1]
        cp(bend[:, 0:4], bank[7][:, 268:272])
        acol = sm[2]
        tt(acol[0:L, 0:4], zz[0:L, t, 8:12], bcol[0:L, 0:4], ALU.subtract)
        mm(bank[6][0:4, 0:L], acol[0:L, 0:4], ident_f[0:L, 0:L], True, True)
        amx = sm[3]
        c.op("vector", lambda e: e.reduce_max(out=amx.t[0:4, 0:1], in_=bank[6].t[0:4, 0:L], axis=AX.X), reads=[bank[6][:]], writes=[amx[:]])
        ts(dg[:, :], ident_f[0:4, 0:4], amx[0:4, 0:1], None, ALU.mult)
        mm(bank[7][:, 0:4], ones_f[0:4, :], dg[:, :], True, True)
        Rr = sm[4]
        tt(Rr[:, 0:4], bank[7][:, 0:4], mr[:, :], ALU.max)
        ecol = sm[5]
        tt(ecol[0:L, 0:4], acol[0:L, 0:4], Rr[0:L, 0:4], ALU.subtract)
        act(ecol[0:L, 0:4], ecol[0:L, 0:4], AF.Exp)
        sc = sm[3]
        tt(sc[:, 4:8], mr[:, :], Rr[:, 0:4], ALU.subtract)
        act(sc[:, 4:8], sc[:, 4:8], AF.Exp)
        thr = sm[2]
        tt(thr[0:L, 4:8], bcol[0:L, 0:4], Rr[0:L, 0:4], ALU.add)
        act(thr[0:L, 4:8], thr[0:L, 4:8], AF.Exp, scale=-1.0)
        tt(mr[:, :], bend[:, 0:4], Rr[:, 0:4], ALU.add)
        tt(vpa[0:L, :, 0:128], V(vml.t[0:L, t, :].rearrange("p (h e) -> p h e", h=4), [vml.buf]), bc(ecol[0:L, 0:4], [L, 4, 128], 2), ALU.mult)
        cp(vpa[0:L, :, 128], ecol[0:L, 0:4])
        pb = bkb(5)
        for h in range(4):
            tr(V(pb[0:L, h * 128:(h + 1) * 128], [bank[5].buf]), kmT[:, h, t * L:(t + 1) * L], ident_b[:, :], signal=(h == 3))
        cp(ktok[0:L, :, :], V(pb[0:L, 0:512].rearrange("p (h d) -> p h d", h=4), [bank[5].buf]), eng="scalar")
        for h in range(4):
            mm(bank[4][0:L, h * L:(h + 1) * L], kmT[:, h, t * L:(t + 1) * L], qmT[:, h, t * L:(t + 1) * L], True, True, signal=(h == 3))
        tt(sTm[0:L, :, 0:L], V(bank[4].t[0:L, 0:4 * L].rearrange("p (h j) -> p h j", h=4), [bank[4].buf]), bc(tri_f[0:L, 0:L], [L, 4, L], 1), ALU.mult)
        tt(Cbf[:, :, 0:129], Cs[:, :, 0:129], bc(sc[:, 4:8], [128, 4, 129], 2), ALU.mult)
        for h in range(4):
            nb = bank[h // 2]
            o0 = (h % 2) * 192
            mm(nb[0:L, o0:o0 + 129], sTm[0:L, h, 0:L], vpa[0:L, h, 0:129], True, False, signal=False)
            mm(nb[0:L, o0:o0 + 129], qmT[:, h, t * L:(t + 1) * L], Cbf[:, h, 0:129], False, True, signal=True)
            db = bank[2 + h // 2]
            mm(db[:, o0:o0 + 129], ktok[0:L, h, :], vpa[0:L, h, 0:129], True, True)
        for h in range(4):
            db = bank[2 + h // 2]
            o0 = (h % 2) * 192
            stt(Cs[:, h, 0:129], Cs[:, h, 0:129], sc[:, 4 + h:5 + h], db[:, o0:o0 + 129], ALU.mult, ALU.add)
        dn = sm[5]
        for h in range(4):
            nb = bank[h // 2]
            o0 = (h % 2) * 192
            cp(dn[0:L, 8 + h:9 + h], nb[0:L, o0 + 128:o0 + 129])
        stt(dn[0:L, 12:16], dn[0:L, 8:12], -1.0, dn[0:L, 8:12], ALU.mult, ALU.max)
        tt(dn[0:L, 12:16], dn[0:L, 12:16], thr[0:L, 4:8], ALU.max)
        c.op("vector", lambda e: e.reciprocal(out=dn.t[0:L, 12:16], in_=dn.t[0:L, 12:16]), reads=[dn[:]], writes=[dn[:]])
        ssq = sm[1]
        for h in range(4):
            nb = bank[h // 2]
            o0 = (h % 2) * 192
            stt(hh[0:L, h * 128:(h + 1) * 128], nb[0:L, o0:o0 + 128], dn[0:L, 12 + h:13 + h], omt[0:L, t, h * 128:(h + 1) * 128], ALU.mult, ALU.mult)
            act(ymt[0:L, h * 128:(h + 1) * 128], hh[0:L, h * 128:(h + 1) * 128], AF.Square, accum=ssq[0:L, 8 + h:9 + h])
        ts(ssq[0:L, 8:12], ssq[0:L, 8:12], 1.0 / 128, EPS, ALU.mult, ALU.add)
        act(ssq[0:L, 8:12], ssq[0:L, 8:12], AF.Ln)
        act(ssq[0:L, 8:12], ssq[0:L, 8:12], AF.Exp, scale=-0.5)
        tt(V(ymt.t[0:L, :].rearrange("p (h e) -> p h e", h=4), [ymt.buf]), V(hh.t[0:L, :].rearrange("p (h e) -> p h e", h=4), [hh.buf]),
           bc(ssq[0:L, 8:12], [L, 4, 128], 2), ALU.mult)
        pb = bkb(5)
        for h in range(4):
            tr(V(pb[:, h * L:(h + 1) * L], [bank[5].buf]), ymt[0:L, h * 128:(h + 1) * 128], ident_b[0:L, 0:L], signal=(h == 3))
        tt(ymlT[:, :, t * L:(t + 1) * L], V(pb[:, 0:4 * L].rearrange("p (h j) -> p h j", h=4), [bank[5].buf]), bc(gml[:, l, :], [128, 4, L], 2), ALU.mult)
        if tl["kind"] == "s" or tl["last"]:
            mc_o, mn_o, mm_o = (mc_p, mn_p, mm_p) if tl["kind"] == "p" else (mc_s, mn_s, mm_s)
            c.dma("sync", mc_o[l, b].rearrange("h d e -> d h e"), Cs[:, :, 0:128])
            c.dma("sync", mn_o[l, b].rearrange("h d -> d h"), Cs[:, :, 128])
            c.dma("sync", mm_o[l, b:b + 1, :], mr[0:1, :])

    blocks = []
    for s in range(NSEQ):
        for bi in range(NBLK):
            tiles = []
            for t in range(4):
                pos = bi * 512 + t * 128
                tiles.append(dict(kind="p", L=128, b=s, pos=pos, kb=pos // 128, first=(pos == 0), last=(pos == SP - 128)))
            blocks.append(tiles)
    if DO_SAMPLE:
        blocks.append([dict(kind="s", L=16, b=bb, pos=PAST, kb=NKP, first=True, last=True) for bb in range(4)])

    for bi_, tiles in enumerate(blocks):
        for l in range(NL):
            plan_layer(l)
    for tiles in blocks:
        L = tiles[0]["L"]
        for t, tl in enumerate(tiles):
            src = xp[tl["b"], tl["pos"]:tl["pos"] + L, :] if tl["kind"] == "p" else xs[tl["b"]]
            c.dma("sync", X[0:L, t, :], src)
        for l in range(NL):
            layer_block(l, tiles)
        for t, tl in enumerate(tiles):
            ss = st_col[0:L, 0:1]
            act(xs_bs[t % 2][0:L, :], X[0:L, t, :], AF.Square, accum=ss)
            rstd_from_ss(ss, D, L, None)
            yo = yout[t % 2]
            stt(yo[0:L, :], X[0:L, t, :], ss, nfin[0:L, :], ALU.mult, ALU.mult)
            dst = y_p[tl["b"], tl["pos"]:tl["pos"] + L, :] if tl["kind"] == "p" else y_s[tl["b"]]
            c.dma("sync", dst, yo[0:L, :])
    c.finish()
    cx.close()
    return nc, c


_CONST = None


def _consts():
    ident = np.eye(128, dtype=np.float32)
    tri = np.triu(np.ones((128, 128), dtype=np.float32))
    return ident, tri


def make_in_maps(inputs):
    ident, tri = _consts()
    f = lambda a: np.ascontiguousarray(np.asarray(a, dtype=np.float32))
    shared = dict(
        norm_mix=f(inputs["norm_mix"]), w_in=f(inputs["w_in"]), b_gate=f(inputs["b_gate"]),
        sg_ln_g=f(inputs["sg_ln_g"]), sg_ln_b=f(inputs["sg_ln_b"]), sg_w=f(inputs["sg_w"]), sg_b=f(inputs["sg_b"]),
        b_fox_f=f(inputs["b_fox_f"]), ml_conv_w=f(inputs["ml_conv_w"]), ml_conv_b=f(inputs["ml_conv_b"]),
        b_ml_i=f(inputs["b_ml_i"]), b_ml_f=f(inputs["b_ml_f"]), ml_norm_g=f(inputs["ml_norm_g"]),
        w_br=f(inputs["w_br"]), w_out=f(inputs["w_out"]), norm_ffn=f(inputs["norm_ffn"]),
        w_ffn_in=f(inputs["w_ffn_in"]), w_ffn_out=f(inputs["w_ffn_out"]),
        norm_final=f(inputs["norm_final"]).reshape(1, D), c_ident=ident, c_tri=tri)
    maps = []
    for i in range(8):
        m = dict(shared)
        m["xp"] = f(inputs["x_prompt"][2 * i:2 * i + 2])
        m["xs"] = f(inputs["x_sample"][4 * i:4 * i + 4])
        m["ck"] = f(np.asarray(inputs["cache_fox_k"])[:, 4 * i:4 * i + 4].reshape(2, 4, 4096, 512))
        m["cv"] = f(np.asarray(inputs["cache_fox_v"])[:, 4 * i:4 * i + 4].reshape(2, 4, 4096, 512))
        m["clf"] = f(np.asarray(inputs["cache_fox_logf"])[:, 4 * i:4 * i + 4])
        m["smc"] = f(np.asarray(inputs["state_ml_c"])[:, 4 * i:4 * i + 4])
        m["smn"] = f(np.asarray(inputs["state_ml_n"])[:, 4 * i:4 * i + 4])
        m["smm"] = f(np.asarray(inputs["state_ml_m"])[:, 4 * i:4 * i + 4])
        m["smconv"] = f(np.asarray(inputs["state_ml_conv"])[:, 4 * i:4 * i + 4])
        maps.append(m)
    return maps


def assemble(results):
    cat = lambda k, ax: np.concatenate([np.asarray(r[k]) for r in results], axis=ax)
    y_p = cat("y_p", 0); y_s = cat("y_s", 0)
    fk_p = cat("fk_p", 1).reshape(2, 16, 4096, 8, 64); fv_p = cat("fv_p", 1).reshape(2, 16, 4096, 8, 64)
    flf_p = cat("flf_p", 1); mc_p = cat("mc_p", 1); mn_p = cat("mn_p", 1); mm_p = cat("mm_p", 1); mconv_p = cat("mconv_p", 1)
    fk_s = cat("fk_s", 1).reshape(2, 32, 16, 8, 64); fv_s = cat("fv_s", 1).reshape(2, 32, 16, 8, 64)
    flf_s = cat("flf_s", 1); mc_s = cat("mc_s", 1); mn_s = cat("mn_s", 1); mm_s = cat("mm_s", 1)
    mconv_s = cat("mconv_s", 1); sgv_s = cat("sgv_s", 1)
    return (y_p, y_s, fk_p, fv_p, flf_p, mc_p, mn_p, mm_p, mconv_p,
            fk_s, fv_s, flf_s, mc_s, mn_s, mm_s, mconv_s, sgv_s)


_NC = {}


def kernel(**inputs):
    cfg = inputs.pop("_cfg", {})
    key = tuple(sorted(cfg.items()))
    if key not in _NC:
        _NC[key] = build(cfg)[0]
    nc = _NC[key]
    maps = make_in_maps(inputs)
    res = run_bass_kernel_spmd(nc, maps, core_ids=list(range(8)))
    return assemble(res.results)


def kernel_debug(cfg, maps, trace=False):
    nc = build(cfg)[0]
    res = run_bass_kernel_spmd(nc, maps, core_ids=list(range(len(maps))), trace=trace)
    if trace:
        print("EXEC_TIME_NS", res.exec_time_ns)
    return res.results
```

```python
import contextlib
import numpy as np
import concourse.bass as bass
import concourse.mybir as mybir
from concourse.bass_utils import run_bass_kernel_spmd

F32 = mybir.dt.float32
BF16 = mybir.dt.bfloat16
AF = mybir.ActivationFunctionType
ALU = mybir.AluOpType
AX = mybir.AxisListType

D = 1024
NIN = 7696
DFF = 2816
EPS = 1e-6
C_GATE, C_SGU, C_SGV, C_FQ, C_FK, C_FV, C_FF = 0, 3072, 3584, 4096, 4608, 5120, 5632
C_MQK, C_MV, C_MI, C_MF, C_MO = 5640, 6664, 7176, 7180, 7184
SLAB = 4096


class Buf:
    __slots__ = ("name", "w", "r", "excl")

    def __init__(self, name, excl=False):
        self.name = name
        self.w = None
        self.r = {}
        self.excl = excl


class V:
    __slots__ = ("ap", "bufs")

    def __init__(self, ap, bufs):
        self.ap = ap
        self.bufs = bufs


class T:
    def __init__(self, t, name):
        self.t = t
        self.buf = Buf(name)

    def __getitem__(self, idx):
        return V(self.t[idx], [self.buf])

    def v(self, ap):
        return V(ap, [self.buf])


class Queue:
    def __init__(self, name, sem):
        self.name = name
        self.sem = sem
        self.cnt = 0
        self.ops = []
        self.seen = {}
        self.pending = False


class Ctx:
    def __init__(self, nc, n_dma_sems=12):
        self.nc = nc
        self.es = contextlib.ExitStack()
        self.q = {}
        for nm in ("sync", "scalar", "vector", "gpsimd", "tensor"):
            self.q[nm] = Queue(nm, self.es.enter_context(nc.semaphore("s_" + nm)))
        self.dq = {}
        for nm in ("sync", "gpsimd", "scalar"):
            self.dq[nm] = [Queue("dma_%s%d" % (nm, i), self.es.enter_context(nc.semaphore("s_d%s%d" % (nm, i))))
                           for i in range(n_dma_sems)]
        self.dq_next = {"sync": 0, "gpsimd": 0, "scalar": 0}
        self.n_inst = 0

    def sbuf(self, name, shape, dtype):
        return T(self.es.enter_context(self.nc.sbuf_tensor(name, list(shape), dtype)), name)

    def psum(self, name, shape, dtype):
        t = T(self.es.enter_context(self.nc.psum_tensor(name, list(shape), dtype)), name)
        t.buf.excl = True
        return t

    def _need(self, q, waits, dep, raw):
        if dep is None:
            return
        pq, c = dep
        if pq is q and ((not raw) or q.name == "tensor"):
            return
        if q.seen.get(pq, 0) >= c:
            return
        if waits.get(pq, 0) < c:
            waits[pq] = c

    def _deps(self, q, rb, wb):
        waits = {}
        for b in rb:
            self._need(q, waits, b.w, True)
            if b.excl:
                for rq, c in b.r.items():
                    if rq is not q:
                        self._need(q, waits, (rq, c), False)
        for b in wb:
            self._need(q, waits, b.w, False)
            for rq, c in b.r.items():
                self._need(q, waits, (rq, c), False)
        return waits

    def _emit_waits(self, q, waits):
        for pq, c in waits.items():
            q.seen[pq] = c
            q.ops.append(lambda e, sem=pq.sem, c=c: e.wait_ge(sem, c))

    def op(self, eng, fn, reads=(), writes=(), signal=True):
        q = self.q[eng]
        rb = [b for v in reads for b in v.bufs]
        wb = [b for v in writes for b in v.bufs]
        self._emit_waits(q, self._deps(q, rb, wb))
        self.n_inst += 1
        if signal:
            q.cnt += 1
            c = q.cnt
            q.ops.append(lambda e, fn=fn, sem=q.sem: fn(e).then_inc(sem, 1))
            q.pending = False
        else:
            c = q.cnt + 1
            q.ops.append(lambda e, fn=fn: fn(e))
            q.pending = True
        for b in rb:
            if b.r.get(q, 0) < c:
                b.r[q] = c
        for b in wb:
            b.w = (q, c)
            b.r = {}

    def dma(self, eng, out, in_, **kw):
        q = self.q[eng]
        pool = self.dq[eng]
        dq = pool[self.dq_next[eng]]
        self.dq_next[eng] = (self.dq_next[eng] + 1) % len(pool)
        rb = list(in_.bufs) if isinstance(in_, V) else []
        wb = list(out.bufs) if isinstance(out, V) else []
        waits = self._deps(q, rb, wb)
        if dq.cnt > 0 and q.seen.get(dq, 0) < dq.cnt and waits.get(dq, 0) < dq.cnt:
            waits[dq] = dq.cnt
        self._emit_waits(q, waits)
        dq.cnt += 16
        c = dq.cnt
        oap = out.ap if isinstance(out, V) else out
        iap = in_.ap if isinstance(in_, V) else in_
        q.ops.append(lambda e, oap=oap, iap=iap, sem=dq.sem, kw=kw: e.dma_start(out=oap, in_=iap, **kw).then_inc(sem, 16))
        self.n_inst += 1
        for b in rb:
            if b.r.get(dq, 0) < c:
                b.r[dq] = c
        for b in wb:
            b.w = (dq, c)
            b.r = {}

    def barrier(self):
        allq = list(self.q.values())
        for q in allq:
            assert not q.pending, q.name
        alld = [d for pool in self.dq.values() for d in pool]
        for q in allq:
            waits = {}
            for pq in allq + alld:
                if pq is not q and pq.cnt > 0 and q.seen.get(pq, 0) < pq.cnt:
                    waits[pq] = pq.cnt
            self._emit_waits(q, waits)

    def finish(self):
        self.barrier()
        nc = self.nc
        with nc.Block() as block:
            @block.sync
            def _(e):
                for f in self.q["sync"].ops:
                    f(e)

            @block.scalar
            def _(e):
                for f in self.q["scalar"].ops:
                    f(e)

            @block.vector
            def _(e):
                for f in self.q["vector"].ops:
                    f(e)

            @block.gpsimd
            def _(e):
                for f in self.q["gpsimd"].ops:
                    f(e)

            @block.tensor
            def _(e):
                for f in self.q["tensor"].ops:
                    f(e)
        self.es.close()


def build(cfg):
    NSEQ = cfg.get("nseq", 2)
    NBLK = cfg.get("nblk", cfg.get("SP", 4096) // 512)
    DO_SAMPLE = cfg.get("sample", True)
    NL = cfg.get("nl", 2)
    STAGE = cfg.get("stage", 99)
    FS = cfg.get("fs", 99)
    QS = cfg.get("qs", 99)
    ABL = cfg.get("abl", "")
    SP = cfg.get("SP", 4096)
    PAST = cfg.get("PAST", 4096)
    NKP = PAST // 128

    nc = bass.Bass("TRN2", target_bir_lowering=False)
    cx = contextlib.ExitStack()
    cx.enter_context(nc.allow_non_contiguous_dma(reason="small strided parameter / state transfers"))
    cx.enter_context(nc.allow_low_precision(reason="bf16 matmul operands, fp32 accumulation"))

    def din(name, shape, dt=F32):
        return nc.dram_tensor(name, list(shape), dt, kind="ExternalInput").ap()

    def dout(name, shape, dt=F32):
        return nc.dram_tensor(name, list(shape), dt, kind="ExternalOutput").ap()

    xp = din("xp", [2, SP, D]); xs = din("xs", [4, 16, D])
    ck = din("ck", [2, 4, PAST, 512]); cv = din("cv", [2, 4, PAST, 512]); clf = din("clf", [2, 4, PAST, 8])
    smc = din("smc", [2, 4, 4, 128, 128]); smn = din("smn", [2, 4, 4, 128]); smm = din("smm", [2, 4, 4])
    smconv = din("smconv", [2, 4, 3, D])
    norm_mix = din("norm_mix", [2, D]); w_in = din("w_in", [2, D, NIN]); b_gate = din("b_gate", [2, 3, D])
    sg_ln_g = din("sg_ln_g", [2, 512]); sg_ln_b = din("sg_ln_b", [2, 512]); sg_w = din("sg_w", [2, 4, 128, 128])
    sg_b = din("sg_b", [2, 4, 128]); b_fox_f = din("b_fox_f", [2, 8]); ml_conv_w = din("ml_conv_w", [2, 4, D])
    ml_conv_b = din("ml_conv_b", [2, D]); b_ml_i = din("b_ml_i", [2, 4]); b_ml_f = din("b_ml_f", [2, 4])
    ml_norm_g = din("ml_norm_g", [2, 4, 128]); w_br = din("w_br", [2, 3, 512, D]); w_out = din("w_out", [2, D, D])
    norm_ffn = din("norm_ffn", [2, D]); w_ffn_in = din("w_ffn_in", [2, D, 2 * DFF]); w_ffn_out = din("w_ffn_out", [2, DFF, D])
    norm_final = din("norm_final", [1, D])
    c_ident = din("c_ident", [128, 128]); c_tri = din("c_tri", [128, 128])

    y_p = dout("y_p", [2, SP, D]); y_s = dout("y_s", [4, 16, D])
    fk_p = dout("fk_p", [2, 2, SP, 512]); fv_p = dout("fv_p", [2, 2, SP, 512]); flf_p = dout("flf_p", [2, 2, SP, 8])
    mc_p = dout("mc_p", [2, 2, 4, 128, 128]); mn_p = dout("mn_p", [2, 2, 4, 128]); mm_p = dout("mm_p", [2, 2, 4])
    mconv_p = dout("mconv_p", [2, 2, 3, D])
    fk_s = dout("fk_s", [2, 4, 16, 512]); fv_s = dout("fv_s", [2, 4, 16, 512]); flf_s = dout("flf_s", [2, 4, 16, 8])
    mc_s = dout("mc_s", [2, 4, 4, 128, 128]); mn_s = dout("mn_s", [2, 4, 4, 128]); mm_s = dout("mm_s", [2, 4, 4])
    mconv_s = dout("mconv_s", [2, 4, 3, D]); sgv_s = dout("sgv_s", [2, 4, 16, 512])

    DBG = cfg.get("dbg", False)
    if DBG:
        dbg_fox = dout("dbg_fox", [64, 8, 512], BF16); dbg_sg = dout("dbg_sg", [128, 4, 512], BF16); dbg_ml = dout("dbg_ml", [128, 4, 512], BF16)
    kts = nc.dram_tensor("kts", [2, 32, 128, 512], BF16, kind="Internal").ap()
    vsc = nc.dram_tensor("vsc", [2, 32, 128, 576], BF16, kind="Internal").ap()
    NPL = 51
    wsc = nc.dram_tensor("wsc", [2, NPL, 128, SLAB], BF16, kind="Internal").ap()
    wsc_buf = [[Buf("wsc%d_%d" % (l, k)) for k in range(NPL)] for l in range(2)]
    kts_buf = [[Buf("kts%d_%d" % (l, k)) for k in range(32)] for l in range(2)]
    vsc_buf = [[Buf("vsc%d_%d" % (l, k)) for k in range(32)] for l in range(2)]

    c = Ctx(nc)
    sb = c.sbuf

    def rd(*xs_):
        return [x for x in xs_ if isinstance(x, V)]

    def A(x):
        return x.ap if isinstance(x, V) else x

    def mm(out, lhsT, rhs, start, stop, signal=None):
        c.op("tensor", lambda e: e.matmul(out.ap, lhsT=lhsT.ap, rhs=rhs.ap, start=start, stop=stop, skip_group_check=True),
             reads=[lhsT, rhs], writes=[out], signal=(stop if signal is None else signal))

    def tr(out, in_, ident, signal=True):
        c.op("tensor", lambda e: e.transpose(out.ap, in_.ap, ident.ap), reads=[in_, ident], writes=[out], signal=signal)

    def act(out, in_, func, bias=None, scale=1.0, accum=None, eng="scalar"):
        kw = {}
        if bias is not None:
            kw["bias"] = A(bias)
        if accum is not None:
            kw["accum_out"] = accum.ap
        c.op("scalar", lambda e: e.activation(out=out.ap, in_=in_.ap, func=func, scale=A(scale), **kw),
             reads=rd(in_, bias, scale), writes=[out] + ([accum] if accum is not None else []))

    def tt(out, in0, in1, op, eng="vector"):
        c.op(eng, lambda e: e.tensor_tensor(out=out.ap, in0=in0.ap, in1=in1.ap, op=op), reads=[in0, in1], writes=[out])

    def ts(out, in0, s1, s2, op0, op1=None, eng="vector"):
        if op1 is None:
            c.op(eng, lambda e: e.tensor_single_scalar(out=out.ap, in_=in0.ap, scalar=A(s1), op=op0), reads=rd(in0, s1), writes=[out])
        else:
            c.op(eng, lambda e: e.tensor_scalar(out=out.ap, in0=in0.ap, scalar1=A(s1), scalar2=A(s2), op0=op0, op1=op1),
                 reads=rd(in0, s1, s2), writes=[out])

    def stt(out, in0, scalar, in1, op0, op1, eng="vector"):
        c.op(eng, lambda e: e.scalar_tensor_tensor(out=out.ap, in0=in0.ap, scalar=A(scalar), in1=in1.ap, op0=op0, op1=op1),
             reads=rd(in0, scalar, in1), writes=[out])

    def cp(out, in_, eng="vector"):
        if eng == "scalar":
            c.op("scalar", lambda e: e.copy(out=out.ap, in_=in_.ap), reads=[in_], writes=[out])
        else:
            c.op(eng, lambda e: e.tensor_copy(out=out.ap, in_=in_.ap), reads=[in_], writes=[out])

    def memset(out, val, eng="vector"):
        c.op(eng, lambda e: e.memset(out.ap, val), writes=[out])

    def bc(v, shape, axis):
        return V(v.ap.unsqueeze(axis).to_broadcast(list(shape)), v.bufs)

    ident_f = sb("ident_f", [128, 128], F32); ident_b = sb("ident_b", [128, 128], BF16)
    tri_f = sb("tri_f", [128, 128], F32); tri_b = sb("tri_b", [128, 128], BF16)
    ones_f = sb("ones_f", [128, 128], F32)
    c.dma("sync", ident_f[:], c_ident); c.dma("sync", tri_f[:], c_tri)
    cp(ident_b[:], ident_f[:]); cp(tri_b[:], tri_f[:]); memset(ones_f[:], 1.0)

    gmix = sb("gmix", [128, 2, 8], F32); gffn = sb("gffn", [128, 2, 8], F32)
    bgate = sb("bgate", [128, 2, 3, 8], F32)
    wconv = sb("wconv", [128, 2, 4, 8], F32); bconv = sb("bconv", [128, 2, 8], F32)
    gml = sb("gml", [128, 2, 4], F32)
    lng = sb("lng", [128, 2, 512], F32); lnb = sb("lnb", [128, 2, 512], F32)
    bsbc = sb("bsbc", [128, 2, 512], F32)
    wsT = sb("wsT", [128, 2, 4, 128], BF16)
    bsm = sb("bsm", [128, 2, 16], F32)
    wsm = sb("wsm", [128, 2, 8, 16], BF16)
    nfin = sb("nfin", [128, D], F32)
    for l in range(2):
        c.dma("sync", gmix[:, l, :], norm_mix[l].rearrange("(c p) -> p c", p=128))
        c.dma("sync", gffn[:, l, :], norm_ffn[l].rearrange("(c p) -> p c", p=128))
        for b in range(3):
            c.dma("sync", bgate[:, l, b, :], b_gate[l, b].rearrange("(c p) -> p c", p=128))
        for j in range(4):
            c.dma("sync", wconv[:, l, j, :], ml_conv_w[l, j].rearrange("(c p) -> p c", p=128))
        c.dma("sync", bconv[:, l, :], ml_conv_b[l].rearrange("(c p) -> p c", p=128))
        c.dma("sync", gml[:, l, :], ml_norm_g[l].rearrange("h p -> p h"))
        c.dma("sync", lng[:, l, :], sg_ln_g[l:l + 1, :].partition_broadcast(128))
        c.dma("sync", lnb[:, l, :], sg_ln_b[l:l + 1, :].partition_broadcast(128))
        c.dma("sync", bsbc[:, l, :], sg_b.rearrange("l g t -> l (g t)")[l:l + 1, :].partition_broadcast(128))
        c.dma("sync", bsm[:, l, 0:8], b_fox_f[l:l + 1, :].partition_broadcast(128))
        c.dma("sync", bsm[:, l, 8:12], b_ml_i[l:l + 1, :].partition_broadcast(128))
        c.dma("sync", bsm[:, l, 12:16], b_ml_f[l:l + 1, :].partition_broadcast(128))
        wv = w_in[l].rearrange("(kc p) n -> p kc n", p=128)
        c.dma("gpsimd", wsm[:, l, :, 0:8], wv[:, :, C_FF:C_FF + 8])
        c.dma("gpsimd", wsm[:, l, :, 8:16], wv[:, :, C_MI:C_MI + 8])
    c.dma("sync", nfin[:], norm_final.partition_broadcast(128))

    bank = [c.psum("bank%d" % i, [128, 512], F32) for i in range(8)]

    def bkb(i):
        return bank[i].t[:, :].bitcast(BF16)

    stg = [sb("stg%d" % i, [128, D], F32) for i in range(2)]
    for l in range(2):
        wtv = V(stg[0].t[:, 0:512].rearrange("p (g s) -> p g s", g=4), [stg[0].buf])
        c.dma("sync", wtv, sg_w[l].rearrange("g t s -> t g s"))
        for g in range(4):
            mm(bank[g][:, 0:128], stg[0][:, g * 128:(g + 1) * 128], ident_f[:], True, True)
            tt(wsT[:, l, g, :], bank[g][:, 0:128], tri_f[:], ALU.mult)

    X = sb("X", [128, 4, D], F32)
    xnT = sb("xnT", [128, 8, 512], BF16)
    slabs = [sb("slab%d" % i, [128, SLAB], BF16) for i in range(3)]
    xs_bs = [sb("xs_b%d" % i, [128, D], BF16) for i in range(2)]
    st_col = sb("st_col", [128, 8], F32)
    uT = sb("uT", [128, 4, 512], BF16); vb = sb("vb", [128, 4, 512], BF16); ysgT = sb("ysgT", [128, 4, 512], BF16)
    sga = sb("sga", [128, 512], F32); sgt = sb("sgt", [128, 512], F32)
    qT = sb("qT", [128, 4, 512], BF16); kTb = sb("kTb", [128, 4, 512], BF16)
    vaug = sb("vaug", [128, 4, 8, 72], BF16)
    yfoxT = sb("yfoxT", [64, 8, 512], BF16)
    zz = sb("zz", [128, 4, 16], F32); lsg = sb("lsg", [128, 4, 16], F32)
    zt = [sb("zt%d" % i, [128, 16], F32) for i in range(3)]
    negF = [sb("negF%d" % l, [128, 33, 8], F32) for l in range(2)]
    Rrun = [sb("Rrun%d" % l, [128, 8], F32) for l in range(2)]
    biasqs = [sb("biasq%d" % i, [128, 33, 8], F32) for i in range(4)]
    arena = sb("arena", [128, 12800], BF16)
    ar = arena.t
    def aview(off, n, name):
        t_ = T(ar[:, off:off + n], name)
        return t_
    ktg = [aview(0, 2048, "ktg0"), aview(2048, 2048, "ktg1")]
    vg = [aview(4096, 2304, "vg0"), aview(6400, 2304, "vg1")]
    kst = aview(8704, 2048, "kst"); vst = aview(10752, 2048, "vst")
    hT = aview(0, 11264, "hT")
    mixTt = aview(0, 4096, "mixT")
    pT = [sb("pT%d" % i, [128, 4, 128], BF16) for i in range(4)]
    vws = [sb("vw%d" % i, [128, 8, 72], BF16) for i in range(2)]
    ptot = sb("ptot", [128, 33, 8], F32)
    rawT = [sb("rawT%d" % i, [128, 4, 131], F32) for i in range(2)]
    qmT = sb("qmT", [128, 4, 512], BF16); kmT = sb("kmT", [128, 4, 512], BF16)
    vml = sb("vml", [128, 4, 512], BF16); omt = sb("omt", [128, 4, 512], BF16)
    vpa = sb("vpa", [128, 4, 136], BF16); ktok = sb("ktok", [128, 4, 128], BF16)
    sTm = sb("sTm", [128, 4, 128], BF16); Cbf = sb("Cbf", [128, 4, 136], BF16)
    ymt = sb("ymt", [128, 512], BF16); ymlT = sb("ymlT", [128, 4, 512], BF16)
    Cst = [sb("Cst%d" % l, [128, 4, 132], F32) for l in range(2)]
    mrep = [sb("mrep%d" % l, [128, 4], F32) for l in range(2)]
    halo = [sb("halo%d" % l, [128, 8, 3], F32) for l in range(2)]
    sm = [sb("sm%d" % i, [128, 16], F32) for i in range(8)]
    dg = sb("dg", [4, 4], F32)
    gsb = [sb("gsb%d" % i, [128, 512], BF16) for i in range(6)]
    tmf = [sb("tmf%d" % i, [128, 512], F32) for i in range(4)]
    hh = tmf[0]; rsum = tmf[1]; bcs = tmf[2]
    saf = [sga, sgt]
    yout = stg
    mixv = mixTt.t[:, :].rearrange("p (k w) -> p k w", k=8)

    memset(vaug[:, :, :, 64:65], 1.0)
    for i in range(2):
        memset(V(vg[i].t[:, :].rearrange("p (k h e) -> p k h e", k=4, h=8)[:, :, :, 64:65], [vg[i].buf]), 1.0)

    class WS:
        def __init__(self):
            self.plan = []
            self.issued = 0
            self.taken = 0

        def add(self, parts):
            self.plan.append(parts)

        def _issue(self, i):
            slot = slabs[i % len(slabs)]
            bi_ = i // (NPL * NL)
            l_ = (i // NPL) % NL
            li_ = i % NPL
            if bi_ > 0 and "nowsc" not in ABL and "nowdma" not in ABL:
                c.dma("gpsimd", slot[:], V(wsc[l_, li_], [wsc_buf[l_][li_]]))
                return
            for (off, shape, src) in self.plan[i]:
                n = 1
                for s_ in shape[1:]:
                    n *= s_
                dst = slot.t[0:shape[0], off:off + n]
                if len(shape) == 3:
                    dst = dst.rearrange("p (a b) -> p a b", a=shape[1])
                elif len(shape) == 4:
                    dst = dst.rearrange("p (a b d) -> p a b d", a=shape[1], b=shape[2])
                if "nowdma" not in ABL:
                    c.dma("gpsimd", V(dst, [slot.buf]), src)
            if "nowsc" not in ABL and "nowdma" not in ABL and len(self.plan) > NPL * NL:
                c.dma("gpsimd", V(wsc[l_, li_], [wsc_buf[l_][li_]]), slot[:])

        def take(self):
            i = self.taken
            self.taken += 1
            while self.issued < len(self.plan) and self.issued <= i + len(slabs) - 2:
                self._issue(self.issued)
                self.issued += 1
            assert self.issued > i
            return slabs[i % len(slabs)]

    ws = WS()

    def plan_layer(l):
        wv = w_in[l].rearrange("(kc p) n -> p kc n", p=128)
        for c0 in (C_SGU, C_SGV, C_FQ, C_FK, C_FV, C_MQK, C_MQK + 512, C_MV, C_MO):
            ws.add([(0, (128, 8, 512), wv[:, :, c0:c0 + 512])])
        for j in range(8):
            ws.add([(b * 1024, (128, 8, 128), wv[:, :, b * 1024 + j * 128: b * 1024 + (j + 1) * 128]) for b in range(3)])
            ws.add([(0, (128, 4, 128), w_br[l, 0].rearrange("(c p) n -> p c n", p=128)[:, :, j * 128:(j + 1) * 128]),
                    (512, (128, 4, 128), w_br[l, 2].rearrange("(c p) n -> p c n", p=128)[:, :, j * 128:(j + 1) * 128]),
                    (1024, (64, 8, 128), w_br[l, 1].rearrange("(h p) n -> p h n", p=64)[:, :, j * 128:(j + 1) * 128])])
        wo = w_out[l].rearrange("(kc p) n -> p kc n", p=128)
        for nh in range(2):
            ws.add([(0, (128, 8, 512), wo[:, :, nh * 512:(nh + 1) * 512])])
        wf = w_ffn_in[l].rearrange("(kc p) n -> p kc n", p=128)
        for jg in range(6):
            ncol = 512 if jg < 5 else 256
            ws.add([(0, (128, 8, ncol), wf[:, :, jg * 512: jg * 512 + ncol])])
            ws.add([(0, (128, 8, ncol), wf[:, :, DFF + jg * 512: DFF + jg * 512 + ncol])])
        wfo = w_ffn_out[l].rearrange("(j p) n -> p j n", p=128)
        for nh in range(2):
            for js in range(6):
                nj = 4 if js < 5 else 2
                ws.add([(0, (128, nj, 512), wfo[:, js * 4: js * 4 + nj, nh * 512:(nh + 1) * 512])])

    def rstd_from_ss(ss, n, Lp, tmp):
        ts(ss, ss, 1.0 / n, EPS, ALU.mult, ALU.add)
        act(ss, ss, AF.Ln)
        act(ss, ss, AF.Exp, scale=-0.5)

    def norm_to_featT(l, gcol, L, ntile):
        W = ntile * L
        for t in range(ntile):
            ss = st_col[0:L, 0:1]
            xs_b = xs_bs[t % 2]
            act(xs_b[0:L, :], X[0:L, t, :], AF.Square, accum=ss)
            rstd_from_ss(ss, D, L, None)
            ts(xs_b[0:L, :], X[0:L, t, :], ss, None, ALU.mult)
            pb = bkb(t % 2)
            for kc in range(8):
                tr(V(pb[:, kc * L:(kc + 1) * L], [bank[t % 2].buf]), xs_b[0:L, kc * 128:(kc + 1) * 128], ident_b[0:L, 0:L], signal=(kc == 7))
            tt(xnT[:, :, t * L:(t + 1) * L], V(pb[:, 0:8 * L].rearrange("p (c j) -> p c j", c=8), [bank[t % 2].buf]),
               bc(gcol, [128, 8, L], 2), ALU.mult)

    def proj_F(slab, cols, W, bk):
        sv = slab.t[:, :].rearrange("p (a b) -> p a b", a=8)
        for kc in range(8):
            mm(bank[bk][:, 0:W], V(sv[:, kc, cols[0]:cols[1]], [slab.buf]), xnT[:, kc, 0:W], kc == 0, kc == 7)

    def proj_T(slab, ncol, t, L, bk, n0=0):
        sv = slab.t[:, 0:8 * ncol].rearrange("p (a b) -> p a b", a=8)
        for kc in range(8):
            mm(bank[bk][0:L, 0:ncol], xnT[:, kc, t * L:(t + 1) * L], V(sv[:, kc, :], [slab.buf]), kc == 0, kc == 7)

    def layer_block(l, tiles):
        L = tiles[0]["L"]
        NT = len(tiles)
        W = NT * L
        norm_to_featT(l, gmix[:, l, :], L, NT)
        if STAGE < 1:
            return
        s_u = ws.take()
        for cc in range(4):
            bk = cc % 4
            proj_F(s_u, (cc * 128, (cc + 1) * 128), W, bk)
            act(uT[:, cc, 0:W], bank[bk][:, 0:W], AF.Gelu)
        s_v = ws.take()
        for t, tl in enumerate(tiles):
            bk = 4 + t % 2
            proj_T(s_v, 512, t, L, bk)
            sacc = st_col[0:L, 1:2]
            act(sga[0:L, :], bank[bk][0:L, :], AF.Gelu, accum=sacc)
            ts(sacc, sacc, -1.0 / 512, None, ALU.mult)
            ssq = st_col[0:L, 2:3]
            act(sgt[0:L, :], sga[0:L, :], AF.Square, bias=sacc, accum=ssq)
            rstd_from_ss(ssq, 512, L, None)
            ts(sga[0:L, :], sga[0:L, :], sacc, ssq, ALU.add, ALU.mult)
            tt(sga[0:L, :], sga[0:L, :], lng[0:L, l, :], ALU.mult)
            if tl["kind"] == "s":
                tt(sgt[0:L, :], sga[0:L, :], lnb[0:L, l, :], ALU.add)
                c.dma("sync", sgv_s[l, tl["b"]], sgt[0:L, :])
                cp(vb[0:L, t, :], sgt[0:L, :])
            else:
                tt(vb[0:L, t, :], sga[0:L, :], lnb[0:L, l, :], ALU.add)
        if STAGE < 2:
            return
        s_q = ws.take()
        for cc in range(4):
            proj_F(s_q, (cc * 128, (cc + 1) * 128), W, cc)
            act(qT[:, cc, 0:W], bank[cc][:, 0:W], AF.Copy, scale=0.125)
        s_k = ws.take()
        for cc in range(4):
            proj_F(s_k, (cc * 128, (cc + 1) * 128), W, 4 + cc)
            cp(kTb[:, cc, 0:W], bank[4 + cc][:, 0:W])
        for t, tl in enumerate(tiles):
            proj_T(s_k, 512, t, L, t % 4)
            st = stg[t % 2]
            cp(st[0:L, 0:512], bank[t % 4][0:L, :], eng="scalar")
            dst = (fk_p[l, tl["b"], tl["pos"]:tl["pos"] + L, :] if tl["kind"] == "p" else fk_s[l, tl["b"]])
            c.dma("sync", dst, st[0:L, 0:512])
        s_vv = ws.take()
        for t, tl in enumerate(tiles):
            bk = 4 + t % 4
            proj_T(s_vv, 512, t, L, bk)
            st = stg[t % 2]
            cp(st[0:L, 512:1024], bank[bk][0:L, :], eng="scalar")
            dst = (fv_p[l, tl["b"], tl["pos"]:tl["pos"] + L, :] if tl["kind"] == "p" else fv_s[l, tl["b"]])
            c.dma("sync", dst, st[0:L, 512:1024])
            cp(vaug[0:L, t, :, 0:64], V(bank[bk].t[0:L, :].rearrange("p (h e) -> p h e", h=8), [bank[bk].buf]))
        for t, tl in enumerate(tiles):
            bk = t % 4
            for kc in range(8):
                mm(bank[bk][0:L, 0:16], xnT[:, kc, t * L:(t + 1) * L], wsm[:, l, kc, :], kc == 0, kc == 7)
            tt(zz[0:L, t, :], bank[bk][0:L, 0:16], bsm[0:L, l, :], ALU.add)
            stt(zt[0][0:L, :], zz[0:L, t, :], -1.0, zz[0:L, t, :], ALU.mult, ALU.max)
            act(zt[1][0:L, :], zt[0][0:L, :], AF.Exp, scale=-1.0)
            act(zt[1][0:L, :], zt[1][0:L, :], AF.Ln, bias=1.0)
            ts(zt[2][0:L, :], zz[0:L, t, :], 0.0, None, ALU.min)
            tt(lsg[0:L, t, :], zt[2][0:L, :], zt[1][0:L, :], ALU.subtract)
            dst = (flf_p[l, tl["b"], tl["pos"]:tl["pos"] + L, :] if tl["kind"] == "p" else flf_s[l, tl["b"]])
            c.dma("sync", dst, lsg[0:L, t, 0:8])
        for t, tl in enumerate(tiles):
            bk2 = 6 + t % 2
            for g in range(4):
                mm(bank[bk2][:, g * L:(g + 1) * L], vb[0:L, t, g * 128:(g + 1) * 128], wsT[0:L, l, g, 0:L], True, True, signal=(g == 3))
            mx = V(bank[bk2].t[:, 0:4 * L].rearrange("p (g j) -> p g j", g=4), [bank[bk2].buf])
            tmv = V(tmf[t % 2].t[:, 0:4 * L].rearrange("p (g j) -> p g j", g=4), [tmf[t % 2].buf])
            tt(tmv, mx, V(bsbc.t[:, l, :].rearrange("p (g j) -> p g j", g=4)[:, :, 0:L], [bsbc.buf]), ALU.add)
            tt(ysgT[:, :, t * L:(t + 1) * L], tmv, uT[:, :, t * L:(t + 1) * L], ALU.mult)
        if STAGE < 3:
            return
        if tiles[0]["kind"] == "p":
            for t, tl in enumerate(tiles):
                fox_prologue(l, t, tl, L, t, 4 + t)
        for t, tl in enumerate(tiles):
            fox_tile(l, t, tl, L)
        if STAGE < 4:
            return
        s_mq = ws.take()
        s_mk = ws.take()
        for ci in range(8):
            slab = s_mq if ci < 4 else s_mk
            cc = ci % 4
            bk = ci % 4
            proj_F(slab, (cc * 128, (cc + 1) * 128), W, bk)
            rw = rawT[ci % 2]
            cp(rw[:, 0:NT, 3:3 + L], V(bank[bk].t[:, 0:W].rearrange("p (t j) -> p t j", t=NT), [bank[bk].buf]), eng="scalar")
            for t, tl in enumerate(tiles):
                if tl["kind"] == "s":
                    c.dma("sync", rw[:, t, 0:3], smconv[l, tl["b"], :, ci * 128:(ci + 1) * 128].rearrange("j p -> p j"))
                elif t == 0:
                    if tl["first"]:
                        memset(rw[:, 0, 0:3], 0.0)
                    else:
                        cp(rw[:, 0, 0:3], halo[l][:, ci, :])
                else:
                    cp(rw[:, t, 0:3], rw[:, t - 1, L:L + 3])
            for t, tl in enumerate(tiles):
                if tl["kind"] == "s" or tl["last"]:
                    dst = (mconv_p if tl["kind"] == "p" else mconv_s)[l, tl["b"], :, ci * 128:(ci + 1) * 128].rearrange("j p -> p j")
                    c.dma("sync", dst, rw[:, t, L:L + 3])
            if tiles[-1]["kind"] == "p" and not tiles[-1]["last"]:
                cp(halo[l][:, ci, :], rw[:, NT - 1, L:L + 3])
            ca_t = stg[ci % 2]
            class _CA:
                def __getitem__(self_, idx):
                    return V(ca_t.t[:, 0:512].rearrange("p (t j) -> p t j", t=4)[idx], [ca_t.buf])
            ca = _CA()
            ts(ca[:, 0:NT, 0:L], rw[:, 0:NT, 3:3 + L], wconv[:, l, 3, ci:ci + 1], None, ALU.mult)
            for j in (2, 1, 0):
                stt(ca[:, 0:NT, 0:L], rw[:, 0:NT, j:j + L], wconv[:, l, j, ci:ci + 1], ca[:, 0:NT, 0:L], ALU.mult, ALU.add)
            if ci < 4:
                act(V(qmT.t[:, cc, 0:W].rearrange("p (t j) -> p t j", t=NT), [qmT.buf]), ca[:, 0:NT, 0:L], AF.Silu, bias=bconv[:, l, ci:ci + 1])
            else:
                act(ca[:, 0:NT, 0:L], ca[:, 0:NT, 0:L], AF.Silu, bias=bconv[:, l, ci:ci + 1])
                ts(V(kmT.t[:, cc, 0:W].rearrange("p (t j) -> p t j", t=NT), [kmT.buf]), ca[:, 0:NT, 0:L], 128.0 ** -0.5, None, ALU.mult)
        s_mv = ws.take()
        for t in range(NT):
            bk = 4 + t % 4
            proj_T(s_mv, 512, t, L, bk)
            cp(vml[0:L, t, :], bank[bk][0:L, :])
        s_mo = ws.take()
        for t in range(NT):
            bk = t % 4
            proj_T(s_mo, 512, t, L, bk)
            act(omt[0:L, t, :], bank[bk][0:L, :], AF.Sigmoid)
        if STAGE < 5:
            return
        for t, tl in enumerate(tiles):
            if "noml" not in ABL:
                ml_tile(l, t, tl, L)
        if DBG and l == 0:
            c.dma("sync", dbg_fox, yfoxT[:]); c.dma("sync", dbg_sg, ysgT[:]); c.dma("sync", dbg_ml, ymlT[:])
        if STAGE < 6:
            return
        for j in range(8):
            s_g = ws.take()
            s_b = ws.take()
            gv = s_g.t[:, 0:3072].rearrange("p (b kc n) -> p b kc n", b=3, kc=8)
            gs = []
            for b in range(3):
                n_ = 3 * j + b
                gbk = bank[n_ % 4]
                for kc in range(8):
                    mm(gbk[:, 0:W], V(gv[:, b, kc, :], [s_g.buf]), xnT[:, kc, 0:W], kc == 0, kc == 7)
                g_ = gsb[n_ % 6]
                act(g_[:, 0:W], gbk[:, 0:W], AF.Sigmoid, bias=bgate[:, l, b, j:j + 1])
                gs.append(g_)
            w0 = s_b.t[:, 0:512].rearrange("p (c n) -> p c n", c=4)
            w2 = s_b.t[:, 512:1024].rearrange("p (c n) -> p c n", c=4)
            w1 = s_b.t[0:64, 1024:2048].rearrange("p (h n) -> p h n", h=8)
            bb = [bank[4 + (3 * j + b) % 4] for b in range(3)]
            for cc in range(4):
                mm(bb[0][:, 0:W], V(w0[:, cc, :], [s_b.buf]), ysgT[:, cc, 0:W], cc == 0, cc == 3)
            for h in range(8):
                mm(bb[1][:, 0:W], V(w1[:, h, :], [s_b.buf]), yfoxT[0:64, h, 0:W], h == 0, h == 7)
            for cc in range(4):
                mm(bb[2][:, 0:W], V(w2[:, cc, :], [s_b.buf]), ymlT[:, cc, 0:W], cc == 0, cc == 3)
            tA = tmf[(2 * j) % 4]
            tB = tmf[(2 * j + 1) % 4]
            tt(tA[:, 0:W], bb[0][:, 0:W], gs[0][:, 0:W], ALU.mult)
            tt(tB[:, 0:W], bb[1][:, 0:W], gs[1][:, 0:W], ALU.mult)
            tt(tA[:, 0:W], tA[:, 0:W], tB[:, 0:W], ALU.add)
            tt(tB[:, 0:W], bb[2][:, 0:W], gs[2][:, 0:W], ALU.mult)
            tt(V(mixv[:, j, 0:W], [mixTt.buf]), tA[:, 0:W], tB[:, 0:W], ALU.add)
        for nh in range(2):
            s_o = ws.take()
            sv = s_o.t[:, :].rearrange("p (a b) -> p a b", a=8)
            for t in range(NT):
                bk = (nh * NT + t) % 8
                for kc in range(8):
                    mm(bank[bk][0:L, :], V(mixv[:, kc, t * L:(t + 1) * L], [mixTt.buf]), V(sv[:, kc, :], [s_o.buf]), kc == 0, kc == 7)
                tt(X[0:L, t, nh * 512:(nh + 1) * 512], X[0:L, t, nh * 512:(nh + 1) * 512], bank[bk][0:L, :], ALU.add)
        if STAGE < 7:
            return
        norm_to_featT(l, gffn[:, l, :], L, NT)
        c.barrier()
        hv = hT.t[:, :].rearrange("p (j w) -> p j w", j=22)
        for jg in range(6):
            s_a = ws.take()
            s_bb = ws.take()
            nj = 4 if jg < 5 else 2
            ncol = nj * 128
            for jj in range(nj):
                j = jg * 4 + jj
                bka, bkb_ = (2 * jj) % 8, (2 * jj + 1) % 8
                sva = s_a.t[:, 0:8 * ncol].rearrange("p (a b) -> p a b", a=8)
                svb = s_bb.t[:, 0:8 * ncol].rearrange("p (a b) -> p a b", a=8)
                for kc in range(8):
                    mm(bank[bka][:, 0:W], V(sva[:, kc, jj * 128:(jj + 1) * 128], [s_a.buf]), xnT[:, kc, 0:W], kc == 0, kc == 7)
                for kc in range(8):
                    mm(bank[bkb_][:, 0:W], V(svb[:, kc, jj * 128:(jj + 1) * 128], [s_bb.buf]), xnT[:, kc, 0:W], kc == 0, kc == 7)
                sa = saf[j % 2]
                act(sa[:, 0:W], bank[bka][:, 0:W], AF.Silu)
                tt(V(hv[:, j, 0:W], [hT.buf]), sa[:, 0:W], bank[bkb_][:, 0:W], ALU.mult)
        for nh in range(2):
            for js in range(6):
                s_f = ws.take()
                nj = 4 if js < 5 else 2
                sv = s_f.t[:, 0:nj * 512].rearrange("p (a b) -> p a b", a=nj)
                for t in range(NT):
                    bk = nh * 4 + t
                    for jj in range(nj):
                        j = js * 4 + jj
                        mm(bank[bk][0:L, :], V(hv[:, j, t * L:(t + 1) * L], [hT.buf]), V(sv[:, jj, :], [s_f.buf]), j == 0, j == 21, signal=(jj == nj - 1))
            for t in range(NT):
                bk = nh * 4 + t
                tt(X[0:L, t, nh * 512:(nh + 1) * 512], X[0:L, t, nh * 512:(nh + 1) * 512], bank[bk][0:L, :], ALU.add)
        c.barrier()

    def fox_prologue(l, t, tl, L, bk_c, bk_t):
        kb = tl["kb"]
        b = tl["b"]
        R = Rrun[l]
        nF = negF[l]
        biasq = biasqs[t]
        if tl["kind"] == "s":
            clt = stg[1]
            cl3 = clt.t[:, 0:NKP * 8].rearrange("p (k h) -> p k h", k=NKP)
            for q4 in range((NKP + 7) // 8):
                k1 = min(NKP, q4 * 8 + 8)
                c.dma("sync", V(cl3[:, q4 * 8:k1, :], [clt.buf]),
                      clf[l, b, q4 * 1024:k1 * 128, :].rearrange("(k p) h -> p k h", p=128))
            mm(bank[6][:, 0:NKP * 8], tri_f[:], clt[:, 0:NKP * 8], True, True)
            mm(bank[7][:, 0:NKP * 8], ones_f[:], clt[:, 0:NKP * 8], True, True)
            memset(ptot[:, 0, :], 0.0)
            tot3 = bank[7].t[:, 0:NKP * 8].rearrange("p (k h) -> p k h", k=NKP)
            for k in range(NKP):
                tt(ptot[:, k + 1, :], ptot[:, k, :], V(tot3[:, k, :], [bank[7].buf]), ALU.add)
            cin3 = V(bank[6].t[:, 0:NKP * 8].rearrange("p (k h) -> p k h", k=NKP), [bank[6].buf])
            stt(nF[:, 0:NKP, :], cin3, -1.0, ptot[:, 0:NKP, :], ALU.mult, ALU.subtract)
            cp(R[:, :], ptot[:, NKP, :])
        elif tl["first"]:
            memset(R[:, :], 0.0)
        mm(bank[bk_c][0:L, 256:272], tri_f[0:L, 0:L], lsg[0:L, t, :], True, True)
        mm(bank[bk_t][:, 256:272], ones_f[0:L, :], lsg[0:L, t, :], True, True)
        stt(nF[0:L, kb, :], bank[bk_c][0:L, 256:264], -1.0, R[0:L, :], ALU.mult, ALU.subtract)
        tt(R[:, :], R[:, :], bank[bk_t][:, 256:264], ALU.add)
        nkb = kb + 1
        tt(biasq[:, 0:nkb, :], nF[:, 0:nkb, :], bc(R[:, :], [128, nkb, 8], 1), ALU.add)
        act(biasq[:, 0:nkb, :], biasq[:, 0:nkb, :], AF.Exp)
        if tl["kind"] == "p" and not tl["last"]:
            c.dma("sync", V(kts[l, kb].rearrange("p (c n) -> p c n", c=4), [kts_buf[l][kb]]), kTb[:, :, t * L:(t + 1) * L])
            c.dma("sync", V(vsc[l, kb], [vsc_buf[l][kb]]), V(vaug.t[:, t, :, :].rearrange("p h e -> p (h e)"), [vaug.buf]))

    def fox_tile(l, t, tl, L):
        kb = tl["kb"]
        b = tl["b"]
        biasq = biasqs[t]
        if tl["kind"] == "s":
            fox_prologue(l, t, tl, L, 6, 7)
        if FS < 2:
            return
        oacc = [bank[0], bank[1]]

        items = []

        def load_group(g):
            k0 = g * 4
            nk = min(4, kb - k0)
            kt = ktg[g % 2]
            vgt = vg[g % 2]
            ktv = kt.t[:, :].rearrange("p (k c n) -> p k c n", k=4, c=4)
            vgv = vgt.t[:, :].rearrange("p (k h e) -> p k h e", k=4, h=8)
            if "nogl" in ABL:
                pass
            elif tl["kind"] == "p":
                c.dma("sync", V(ktv[:, 0:nk], [kt.buf]),
                      V(kts[l, k0:k0 + nk].rearrange("k p (c n) -> p k c n", c=4), [kts_buf[l][k0 + i] for i in range(nk)]))
                c.dma("sync", V(vgt.t[:, 0:nk * 576].rearrange("p (k f) -> p k f", k=nk), [vgt.buf]),
                      V(vsc[l, k0:k0 + nk].rearrange("k p f -> p k f"), [vsc_buf[l][k0 + i] for i in range(nk)]))
            else:
                ksv = kst.t[:, :].rearrange("p (k n) -> p k n", k=4)
                c.dma("gpsimd", V(ksv, [kst.buf]), ck[l, b, k0 * 128:(k0 + 4) * 128, :].rearrange("(k p) n -> p k n", p=128))
                c.dma("gpsimd", V(vst.t[:, :].rearrange("p (k n) -> p k n", k=4), [vst.buf]),
                      cv[l, b, k0 * 128:(k0 + 4) * 128, :].rearrange("(k p) n -> p k n", p=128))
                for k in range(4):
                    pb_i = 6 + (k % 2)
                    pb = bkb(pb_i)
                    for pr in range(4):
                        tr(V(pb[:, pr * 128:(pr + 1) * 128], [bank[pb_i].buf]), V(ksv[:, k, pr * 128:(pr + 1) * 128], [kst.buf]),
                           ident_b[:, :], signal=(pr == 3))
                    cp(V(ktv[:, k, :, :], [kt.buf]), V(pb[:, 0:512].rearrange("p (c n) -> p c n", c=4), [bank[pb_i].buf]), eng="scalar")
                cp(V(vgv[:, :, :, 0:64], [vgt.buf]), V(vst.t[:, :].rearrange("p (k h e) -> p k h e", k=4, h=8), [vst.buf]))
                memset(V(vgv[:, :, :, 64:65], [vgt.buf]), 1.0)
            for k in range(nk):
                items.append((ktv[:, k], [kt.buf], vgv[:, k], [vgt.buf], 128, k0 + k, False))

        ngrp = (kb + 3) // 4
        state = {"first": True}

        def qk_exp(it, slot):
            kview, kbuf, vview, vbuf, Lk, kbi, diag = it
            for h in range(8):
                p, half = h // 2, h % 2
                sb_ = bank[2 + 2 * slot + half]
                mm(sb_[0:Lk, p * L:(p + 1) * L], V(kview[64 * half:64 * half + 64, p, 0:Lk], kbuf),
                   qT[64 * half:64 * half + 64, p, t * L:(t + 1) * L], True, True)
            if QS < 1:
                return
            for half in range(2):
                sb_ = bank[2 + 2 * slot + half]
                if "noexp" not in ABL:
                    act(pT[2 * slot + half][0:Lk, :, 0:L],
                        V(sb_.t[0:Lk, 0:4 * L].rearrange("p (c j) -> p c j", c=4), [sb_.buf]), AF.Exp)
            vw = vws[slot]
            tt(vw[0:Lk, :, 0:65], V(vview[0:Lk, :, 0:65], vbuf), bc(biasq[0:Lk, kbi, :], [Lk, 8, 65], 2), ALU.mult,
               eng=("gpsimd" if "poolvw" in ABL else "vector"))
            if QS < 2:
                return
            if diag:
                for hb in range(2):
                    pvv = pT[2 * slot + hb][0:Lk, :, 0:L]
                    tt(pvv, pvv, bc(tri_b[0:Lk, 0:L], [Lk, 4, L], 1), ALU.mult)

        def pv_mm(it, slot, last):
            kview, kbuf, vview, vbuf, Lk, kbi, diag = it
            if QS < 3:
                return
            for h in range(8):
                ob = oacc[h // 4]
                pv = pT[2 * slot + h % 2][0:Lk, h // 2, 0:L]
                mm(ob[0:65, (h % 4) * L:(h % 4 + 1) * L], vws[slot][0:Lk, h, 0:65], pv,
                   start=(state["first"] and h % 4 == 0), stop=last, signal=(last and h % 4 == 3))
            state["first"] = False

        pipe = {"prev": None, "idx": 0}

        def push(it):
            slot = pipe["idx"] % 2
            qk_exp(it, slot)
            if pipe["prev"] is not None:
                pv_mm(pipe["prev"][0], pipe["prev"][1], False)
            pipe["prev"] = (it, slot)
            pipe["idx"] += 1

        push((kTb.t[:, :, t * L:(t + 1) * L], [kTb.buf], vaug.t[:, t], [vaug.buf], L, kb, True))
        for g in range(ngrp if (FS >= 4 and "nopast" not in ABL) else 0):
            del items[:]
            load_group(g)
            for it in list(items):
                push(it)
        pv_mm(pipe["prev"][0], pipe["prev"][1], True)
        if FS < 3:
            c.op("vector", lambda e: e.memset(xs_bs[0].t[:, 0:4], 0.0), reads=[oacc[0][:], oacc[1][:]], writes=[xs_bs[0][:]])
            return
        for hb in range(2 if "notail" not in ABL else 0):
            ob = oacc[hb]
            c.op("vector", lambda e, ob=ob: e.reciprocal(out=rsum.t[64:65, 0:4 * L], in_=ob.t[64:65, 0:4 * L]), reads=[ob[:]], writes=[rsum[:]])
            mm(bank[6 + hb][0:64, 0:4 * L], ones_f[64:65, 0:64], rsum[64:65, 0:4 * L], True, True)
            cp(bcs[0:64, 0:4 * L], bank[6 + hb][0:64, 0:4 * L])
            tt(yfoxT[0:64, hb * 4:(hb + 1) * 4, t * L:(t + 1) * L],
               V(ob.t[0:64, 0:4 * L].rearrange("p (h j) -> p h j", h=4), [ob.buf]),
               V(bcs.t[0:64, 0:4 * L].rearrange("p (h j) -> p h j", h=4), [bcs.buf]), ALU.mult)

    def ml_tile(l, t, tl, L):
        b = tl["b"]
        Cs = Cst[l]
        mr = mrep[l]
        if tl["kind"] == "s":
            c.dma("sync", Cs[:, :, 0:128], smc[l, b].rearrange("h d e -> d h e"))
            c.dma("sync", Cs[:, :, 128], smn[l, b].rearrange("h d -> d h"))
            c.dma("sync", mr[:, :], smm[l, b:b + 1, :].partition_broadcast(128))
        elif tl["first"]:
            memset(Cs[:, :, :], 0.0)
            memset(mr[:, :], 0.0)
        mm(bank[6][0:L, 256:272], tri_f[0:L, 0:L], lsg[0:L, t, :], True, True)
        mm(bank[7][:, 256:272], ones_f[0:L, :], lsg[0:L, t, :], True, True)
        bcol = sm[0]
        cp(bcol[0:L, 0:4], bank[6][0:L, 268:272])
        bend = sm[1]
        cp(bend[:, 0:4], bank[7][:, 268:272])
        acol = sm[2]
        tt(acol[0:L, 0:4], zz[0:L, t, 8:12], bcol[0:L, 0:4], ALU.subtract)
        mm(bank[6][0:4, 0:L], acol[0:L, 0:4], ident_f[0:L, 0:L], True, True)
        amx = sm[3]
        c.op("vector", lambda e: e.reduce_max(out=amx.t[0:4, 0:1], in_=bank[6].t[0:4, 0:L], axis=AX.X), reads=[bank[6][:]], writes=[amx[:]])
        ts(dg[:, :], ident_f[0:4, 0:4], amx[0:4, 0:1], None, ALU.mult)
        mm(bank[7][:, 0:4], ones_f[0:4, :], dg[:, :], True, True)
        Rr = sm[4]
        tt(Rr[:, 0:4], bank[7][:, 0:4], mr[:, :], ALU.max)
        ecol = sm[5]
        tt(ecol[0:L, 0:4], acol[0:L, 0:4], Rr[0:L, 0:4], ALU.subtract)
        act(ecol[0:L, 0:4], ecol[0:L, 0:4], AF.Exp)
        sc = sm[3]
        tt(sc[:, 4:8], mr[:, :], Rr[:, 0:4], ALU.subtract)
        act(sc[:, 4:8], sc[:, 4:8], AF.Exp)
        thr = sm[2]
        tt(thr[0:L, 4:8], bcol[0:L, 0:4], Rr[0:L, 0:4], ALU.add)
        act(thr[0:L, 4:8], thr[0:L, 4:8], AF.Exp, scale=-1.0)
        tt(mr[:, :], bend[:, 0:4], Rr[:, 0:4], ALU.add)
        tt(vpa[0:L, :, 0:128], V(vml.t[0:L, t, :].rearrange("p (h e) -> p h e", h=4), [vml.buf]), bc(ecol[0:L, 0:4], [L, 4, 128], 2), ALU.mult)
        cp(vpa[0:L, :, 128], ecol[0:L, 0:4])
        pb = bkb(5)
        for h in range(4):
            tr(V(pb[0:L, h * 128:(h + 1) * 128], [bank[5].buf]), kmT[:, h, t * L:(t + 1) * L], ident_b[:, :], signal=(h == 3))
        cp(ktok[0:L, :, :], V(pb[0:L, 0:512].rearrange("p (h d) -> p h d", h=4), [bank[5].buf]), eng="scalar")
        for h in range(4):
            mm(bank[4][0:L, h * L:(h + 1) * L], kmT[:, h, t * L:(t + 1) * L], qmT[:, h, t * L:(t + 1) * L], True, True, signal=(h == 3))
        tt(sTm[0:L, :, 0:L], V(bank[4].t[0:L, 0:4 * L].rearrange("p (h j) -> p h j", h=4), [bank[4].buf]), bc(tri_f[0:L, 0:L], [L, 4, L], 1), ALU.mult)
        tt(Cbf[:, :, 0:129], Cs[:, :, 0:129], bc(sc[:, 4:8], [128, 4, 129], 2), ALU.mult)
        for h in range(4):
            nb = bank[h // 2]
            o0 = (h % 2) * 192
            mm(nb[0:L, o0:o0 + 129], sTm[0:L, h, 0:L], vpa[0:L, h, 0:129], True, False, signal=False)
            mm(nb[0:L, o0:o0 + 129], qmT[:, h, t * L:(t + 1) * L], Cbf[:, h, 0:129], False, True, signal=True)
            db = bank[2 + h // 2]
            mm(db[:, o0:o0 + 129], ktok[0:L, h, :], vpa[0:L, h, 0:129], True, True)
        for h in range(4):
            db = bank[2 + h // 2]
            o0 = (h % 2) * 192
            stt(Cs[:, h, 0:129], Cs[:, h, 0:129], sc[:, 4 + h:5 + h], db[:, o0:o0 + 129], ALU.mult, ALU.add)
        dn = sm[5]
        for h in range(4):
            nb = bank[h // 2]
            o0 = (h % 2) * 192
            cp(dn[0:L, 8 + h:9 + h], nb[0:L, o0 + 128:o0 + 129])
        stt(dn[0:L, 12:16], dn[0:L, 8:12], -1.0, dn[0:L, 8:12], ALU.mult, ALU.max)
        tt(dn[0:L, 12:16], dn[0:L, 12:16], thr[0:L, 4:8], ALU.max)
        c.op("vector", lambda e: e.reciprocal(out=dn.t[0:L, 12:16], in_=dn.t[0:L, 12:16]), reads=[dn[:]], writes=[dn[:]])
        ssq = sm[1]
        for h in range(4):
            nb = bank[h // 2]
            o0 = (h % 2) * 192
            stt(hh[0:L, h * 128:(h + 1) * 128], nb[0:L, o0:o0 + 128], dn[0:L, 12 + h:13 + h], omt[0:L, t, h * 128:(h + 1) * 128], ALU.mult, ALU.mult)
            act(ymt[0:L, h * 128:(h + 1) * 128], hh[0:L, h * 128:(h + 1) * 128], AF.Square, accum=ssq[0:L, 8 + h:9 + h])
        ts(ssq[0:L, 8:12], ssq[0:L, 8:12], 1.0 / 128, EPS, ALU.mult, ALU.add)
        act(ssq[0:L, 8:12], ssq[0:L, 8:12], AF.Ln)
        act(ssq[0:L, 8:12], ssq[0:L, 8:12], AF.Exp, scale=-0.5)
        tt(V(ymt.t[0:L, :].rearrange("p (h e) -> p h e", h=4), [ymt.buf]), V(hh.t[0:L, :].rearrange("p (h e) -> p h e", h=4), [hh.buf]),
           bc(ssq[0:L, 8:12], [L, 4, 128], 2), ALU.mult)
        pb = bkb(5)
        for h in range(4):
            tr(V(pb[:, h * L:(h + 1) * L], [bank[5].buf]), ymt[0:L, h * 128:(h + 1) * 128], ident_b[0:L, 0:L], signal=(h == 3))
        tt(ymlT[:, :, t * L:(t + 1) * L], V(pb[:, 0:4 * L].rearrange("p (h j) -> p h j", h=4), [bank[5].buf]), bc(gml[:, l, :], [128, 4, L], 2), ALU.mult)
        if tl["kind"] == "s" or tl["last"]:
            mc_o, mn_o, mm_o = (mc_p, mn_p, mm_p) if tl["kind"] == "p" else (mc_s, mn_s, mm_s)
            c.dma("sync", mc_o[l, b].rearrange("h d e -> d h e"), Cs[:, :, 0:128])
            c.dma("sync", mn_o[l, b].rearrange("h d -> d h"), Cs[:, :, 128])
            c.dma("sync", mm_o[l, b:b + 1, :], mr[0:1, :])

    blocks = []
    for s in range(NSEQ):
        for bi in range(NBLK):
            tiles = []
            for t in range(4):
                pos = bi * 512 + t * 128
                tiles.append(dict(kind="p", L=128, b=s, pos=pos, kb=pos // 128, first=(pos == 0), last=(pos == SP - 128)))
            blocks.append(tiles)
    if DO_SAMPLE:
        blocks.append([dict(kind="s", L=16, b=bb, pos=PAST, kb=NKP, first=True, last=True) for bb in range(4)])

    for bi_, tiles in enumerate(blocks):
        for l in range(NL):
            plan_layer(l)
    for tiles in blocks:
        L = tiles[0]["L"]
        for t, tl in enumerate(tiles):
            src = xp[tl["b"], tl["pos"]:tl["pos"] + L, :] if tl["kind"] == "p" else xs[tl["b"]]
            c.dma("sync", X[0:L, t, :], src)
        for l in range(NL):
            layer_block(l, tiles)
        for t, tl in enumerate(tiles):
            ss = st_col[0:L, 0:1]
            act(xs_bs[t % 2][0:L, :], X[0:L, t, :], AF.Square, accum=ss)
            rstd_from_ss(ss, D, L, None)
            yo = yout[t % 2]
            stt(yo[0:L, :], X[0:L, t, :], ss, nfin[0:L, :], ALU.mult, ALU.mult)
            dst = y_p[tl["b"], tl["pos"]:tl["pos"] + L, :] if tl["kind"] == "p" else y_s[tl["b"]]
            c.dma("sync", dst, yo[0:L, :])
    c.finish()
    cx.close()
    return nc, c


_CONST = None


def _consts():
    ident = np.eye(128, dtype=np.float32)
    tri = np.triu(np.ones((128, 128), dtype=np.float32))
    return ident, tri


def make_in_maps(inputs):
    ident, tri = _consts()
    f = lambda a: np.ascontiguousarray(np.asarray(a, dtype=np.float32))
    shared = dict(
        norm_mix=f(inputs["norm_mix"]), w_in=f(inputs["w_in"]), b_gate=f(inputs["b_gate"]),
        sg_ln_g=f(inputs["sg_ln_g"]), sg_ln_b=f(inputs["sg_ln_b"]), sg_w=f(inputs["sg_w"]), sg_b=f(inputs["sg_b"]),
        b_fox_f=f(inputs["b_fox_f"]), ml_conv_w=f(inputs["ml_conv_w"]), ml_conv_b=f(inputs["ml_conv_b"]),
        b_ml_i=f(inputs["b_ml_i"]), b_ml_f=f(inputs["b_ml_f"]), ml_norm_g=f(inputs["ml_norm_g"]),
        w_br=f(inputs["w_br"]), w_out=f(inputs["w_out"]), norm_ffn=f(inputs["norm_ffn"]),
        w_ffn_in=f(inputs["w_ffn_in"]), w_ffn_out=f(inputs["w_ffn_out"]),
        norm_final=f(inputs["norm_final"]).reshape(1, D), c_ident=ident, c_tri=tri)
    maps = []
    for i in range(8):
        m = dict(shared)
        m["xp"] = f(inputs["x_prompt"][2 * i:2 * i + 2])
        m["xs"] = f(inputs["x_sample"][4 * i:4 * i + 4])
        m["ck"] = f(np.asarray(inputs["cache_fox_k"])[:, 4 * i:4 * i + 4].reshape(2, 4, 4096, 512))
        m["cv"] = f(np.asarray(inputs["cache_fox_v"])[:, 4 * i:4 * i + 4].reshape(2, 4, 4096, 512))
        m["clf"] = f(np.asarray(inputs["cache_fox_logf"])[:, 4 * i:4 * i + 4])
        m["smc"] = f(np.asarray(inputs["state_ml_c"])[:, 4 * i:4 * i + 4])
        m["smn"] = f(np.asarray(inputs["state_ml_n"])[:, 4 * i:4 * i + 4])
        m["smm"] = f(np.asarray(inputs["state_ml_m"])[:, 4 * i:4 * i + 4])
        m["smconv"] = f(np.asarray(inputs["state_ml_conv"])[:, 4 * i:4 * i + 4])
        maps.append(m)
    return maps


def assemble(results):
    cat = lambda k, ax: np.concatenate([np.asarray(r[k]) for r in results], axis=ax)
    y_p = cat("y_p", 0); y_s = cat("y_s", 0)
    fk_p = cat("fk_p", 1).reshape(2, 16, 4096, 8, 64); fv_p = cat("fv_p", 1).reshape(2, 16, 4096, 8, 64)
    flf_p = cat("flf_p", 1); mc_p = cat("mc_p", 1); mn_p = cat("mn_p", 1); mm_p = cat("mm_p", 1); mconv_p = cat("mconv_p", 1)
    fk_s = cat("fk_s", 1).reshape(2, 32, 16, 8, 64); fv_s = cat("fv_s", 1).reshape(2, 32, 16, 8, 64)
    flf_s = cat("flf_s", 1); mc_s = cat("mc_s", 1); mn_s = cat("mn_s", 1); mm_s = cat("mm_s", 1)
    mconv_s = cat("mconv_s", 1); sgv_s = cat("sgv_s", 1)
    return (y_p, y_s, fk_p, fv_p, flf_p, mc_p, mn_p, mm_p, mconv_p,
            fk_s, fv_s, flf_s, mc_s, mn_s, mm_s, mconv_s, sgv_s)


_NC = {}


def kernel(**inputs):
    cfg = inputs.pop("_cfg", {})
    key = tuple(sorted(cfg.items()))
    if key not in _NC:
        _NC[key] = build(cfg)[0]
    nc = _NC[key]
    maps = make_in_maps(inputs)
    res = run_bass_kernel_spmd(nc, maps, core_ids=list(range(8)))
    return assemble(res.results)


def kernel_debug(cfg, maps, trace=False):
    nc = build(cfg)[0]
    res = run_bass_kernel_spmd(nc, maps, core_ids=list(range(len(maps))), trace=trace)
    if trace:
        print("EXEC_TIME_NS", res.exec_time_ns)
    return res.results
```
